# Optimizing a Trainium2 kernel written in Bass

```python
import math
import jax, jax.numpy as jnp
from jax import lax
import numpy as np

D_MODEL = 2048
BATCH = 4
SEQ = 4096
DEPTH = 1

D_MIX = D_MODEL
D_ATTN = D_MIX // 2
D_RWKV = D_MIX - D_ATTN
ATTN_HEAD_DIM = 64
ATTN_HEADS = D_ATTN // (2 * ATTN_HEAD_DIM)
ATTN_V_DIM = 2 * ATTN_HEAD_DIM
RWKV_HEAD = 64
RWKV_HEADS = D_RWKV // RWKV_HEAD
DECAY_LORA = 64
ICLR_LORA = 64
GATE_LORA = 160
D_IN = 3 * D_ATTN + 3 * D_RWKV
D_FF = 5632
CONV_W = 3
NUM_BUCKETS = 32
MAX_EXACT = NUM_BUCKETS // 2
MAX_DISTANCE = 128
Q_BLOCK = 128
NORM_EPS = 1e-6
RWKV_GN_EPS = 64e-5
NEG_INF = -1e30

kernel_name = "hybrid_diffattn_rwkv7_convffn_block"


def rmsnorm(x, g, eps=NORM_EPS):
    xf = x.astype(jnp.float32)
    y = xf * lax.rsqrt(jnp.mean(xf * xf, axis=-1, keepdims=True) + eps)
    return (y * g.astype(jnp.float32)).astype(x.dtype)


def token_shift(x):
    return jnp.pad(x, ((0, 0), (1, 0), (0, 0)))[:, :-1]


def t5_bucket(dist):
    n = jnp.maximum(dist, 0)
    nf = jnp.maximum(n, 1).astype(jnp.float32)
    large = MAX_EXACT + (jnp.log(nf / MAX_EXACT) / math.log(MAX_DISTANCE / MAX_EXACT)
                         * (NUM_BUCKETS - MAX_EXACT)).astype(jnp.int32)
    large = jnp.minimum(large, NUM_BUCKETS - 1)
    return jnp.where(n < MAX_EXACT, n, large)


def diff_attention(q, k, v, q_norm_g, k_norm_g, rel_bias, lam, subln_g, lambda_init):
    B, S = q.shape[0], q.shape[1]
    n_blk = S // Q_BLOCK
    q = rmsnorm(q, q_norm_g) * (ATTN_HEAD_DIM ** -0.5)
    k = rmsnorm(k, k_norm_g)
    kt = jnp.transpose(k, (0, 2, 3, 1, 4))
    vt = jnp.transpose(v, (0, 2, 1, 3))
    qb = jnp.transpose(q, (0, 2, 3, 1, 4)).reshape(B, ATTN_HEADS, 2, n_blk, Q_BLOCK, ATTN_HEAD_DIM)
    qb = jnp.transpose(qb, (3, 0, 1, 2, 4, 5))
    starts = jnp.arange(n_blk, dtype=jnp.int32) * Q_BLOCK
    key_pos = jnp.arange(S, dtype=jnp.int32)
    lam32 = lam.astype(jnp.float32)

    def block(args):
        q_blk, start = args
        s = jnp.einsum('bhiqd,bhikd->bhiqk', q_blk, kt).astype(jnp.float32)
        dist = (start + jnp.arange(Q_BLOCK, dtype=jnp.int32))[:, None] - key_pos[None, :]
        bias = jnp.transpose(rel_bias[t5_bucket(dist)], (2, 0, 1)).astype(jnp.float32)
        s = jnp.where(dist >= 0, s + bias[None, :, None], NEG_INF)
        p = jax.nn.softmax(s, axis=-1)
        attn = p[:, :, 0] - lam32 * p[:, :, 1]
        return jnp.einsum('bhqk,bhkv->bhqv', attn.astype(vt.dtype), vt)

    out = lax.map(block, (qb, starts))
    out = jnp.transpose(out, (1, 0, 3, 2, 4)).reshape(B, S, ATTN_HEADS, ATTN_V_DIM)
    out = rmsnorm(out, subln_g) * (1.0 - lambda_init)
    return out.reshape(B, S, D_ATTN)


def rwkv7_time_mix(h, r_p, k_p, v_p, mu_rkv, mu_wag, w0, w1, w2, a0, a1, a2, g1, g2,
                   k_k, k_a, r_k, ln_x_g, ln_x_b):
    B, S = h.shape[0], h.shape[1]
    f32 = jnp.float32
    r = r_p + (token_shift(r_p) - r_p) * mu_rkv[0]
    k = k_p + (token_shift(k_p) - k_p) * mu_rkv[1]
    v = v_p + (token_shift(v_p) - v_p) * mu_rkv[2]
    dh = token_shift(h) - h
    xw = h + dh * mu_wag[0]
    xa = h + dh * mu_wag[1]
    xg = h + dh * mu_wag[2]
    w = -jax.nn.softplus(-(w0 + jnp.tanh(xw @ w1) @ w2).astype(f32)) - 0.5
    decay = jnp.exp(-jnp.exp(w))
    a = jax.nn.sigmoid((a0 + (xa @ a1) @ a2).astype(f32))
    g = jax.nn.sigmoid(xg @ g1) @ g2
    k32 = k.astype(f32)
    kk = (k32 * k_k.astype(f32)).reshape(B, S, RWKV_HEADS, RWKV_HEAD)
    kk = kk / jnp.maximum(jnp.linalg.norm(kk, axis=-1, keepdims=True), 1e-12)
    k32 = k32 * (1.0 + (a - 1.0) * k_a.astype(f32))
    hs = lambda t: t.astype(f32).reshape(B, S, RWKV_HEADS, RWKV_HEAD)
    r32, w32, k32, v32, a32 = hs(r), hs(decay), hs(k32), hs(v), hs(a)
    tm = lambda t: jnp.moveaxis(t, 1, 0)

    def step(state, inp):
        r_t, w_t, k_t, v_t, kk_t, a_t = inp
        sa = jnp.einsum('bhvk,bhk->bhv', state, -kk_t)
        state = (state * w_t[:, :, None, :] + sa[..., None] * (kk_t * a_t)[:, :, None, :]
                 + v_t[..., None] * k_t[:, :, None, :])
        y_t = jnp.einsum('bhvk,bhk->bhv', state, r_t)
        return state, y_t

    s0 = jnp.zeros((B, RWKV_HEADS, RWKV_HEAD, RWKV_HEAD), f32)
    _, y = lax.scan(step, s0, (tm(r32), tm(w32), tm(k32), tm(v32), tm(kk), tm(a32)))
    y = jnp.moveaxis(y, 0, 1)
    mu = jnp.mean(y, axis=-1, keepdims=True)
    var = jnp.mean(jnp.square(y - mu), axis=-1, keepdims=True)
    yn = ((y - mu) * lax.rsqrt(var + RWKV_GN_EPS) * ln_x_g.astype(f32).reshape(RWKV_HEADS, RWKV_HEAD)
          + ln_x_b.astype(f32).reshape(RWKV_HEADS, RWKV_HEAD))
    bonus = jnp.sum(r32 * k32 * r_k.astype(f32), axis=-1, keepdims=True) * v32
    out = (yn + bonus).reshape(B, S, D_RWKV)
    return (out * g.astype(f32)).astype(h.dtype)


def conv_glu_ffn(h, w_up, conv_w, conv_b, w_down):
    S = h.shape[1]
    up = h @ w_up
    upp = jnp.pad(up, ((0, 0), (CONV_W - 1, 0), (0, 0)))
    y = conv_b + conv_w[0] * upp[:, 0:S] + conv_w[1] * upp[:, 1:S + 1] + conv_w[2] * upp[:, 2:S + 2]
    gate, val = jnp.split(y, 2, axis=-1)
    return (jax.nn.silu(gate) * val) @ w_down


def setup_inputs(seed: int = 0) -> dict:
    key = jax.random.key(seed)
    ks = iter(jax.random.split(key, 40))
    f32 = jnp.float32
    nrm = lambda shape, scale: jax.random.normal(next(ks), shape, f32) * scale
    L = DEPTH
    return {
        "x": nrm((BATCH, SEQ, D_MODEL), 1.0),
        "c": nrm((BATCH, D_MODEL), 1.0),
        "rel_bias": nrm((NUM_BUCKETS, ATTN_HEADS), 0.5),
        "w_ada": nrm((L, D_MODEL, 6 * D_MODEL), 0.5 * D_MODEL ** -0.5),
        "b_ada": nrm((L, 6 * D_MODEL), 0.02),
        "norm_mix_g": 1.0 + nrm((L, D_MODEL), 0.02),
        "w_in": nrm((L, D_MODEL, D_IN), D_MODEL ** -0.5),
        "q_norm_g": 1.0 + nrm((L, ATTN_HEAD_DIM), 0.02),
        "k_norm_g": 1.0 + nrm((L, ATTN_HEAD_DIM), 0.02),
        "lambda_q1": nrm((L, ATTN_HEAD_DIM), 0.1),
        "lambda_k1": nrm((L, ATTN_HEAD_DIM), 0.1),
        "lambda_q2": nrm((L, ATTN_HEAD_DIM), 0.1),
        "lambda_k2": nrm((L, ATTN_HEAD_DIM), 0.1),
        "attn_subln_g": 1.0 + nrm((L, ATTN_V_DIM), 0.02),
        "mu_rkv": jax.random.uniform(next(ks), (L, 3, D_RWKV), f32),
        "mu_wag": jax.random.uniform(next(ks), (L, 3, D_MODEL), f32),
        "w0": jax.random.uniform(next(ks), (L, D_RWKV), f32, -4.0, 1.0),
        "w1": nrm((L, D_MODEL, DECAY_LORA), D_MODEL ** -0.5),
        "w2": nrm((L, DECAY_LORA, D_RWKV), 0.3 * DECAY_LORA ** -0.5),
        "a0": nrm((L, D_RWKV), 0.1),
        "a1": nrm((L, D_MODEL, ICLR_LORA), D_MODEL ** -0.5),
        "a2": nrm((L, ICLR_LORA, D_RWKV), 0.5 * ICLR_LORA ** -0.5),
        "g1": nrm((L, D_MODEL, GATE_LORA), D_MODEL ** -0.5),
        "g2": nrm((L, GATE_LORA, D_RWKV), GATE_LORA ** -0.5),
        "k_k": 0.85 + nrm((L, D_RWKV), 0.02),
        "k_a": 1.0 + nrm((L, D_RWKV), 0.02),
        "r_k": nrm((L, RWKV_HEADS, RWKV_HEAD), 0.1),
        "ln_x_g": 1.0 + nrm((L, D_RWKV), 0.02),
        "ln_x_b": nrm((L, D_RWKV), 0.02),
        "w_out": nrm((L, D_MIX, D_MODEL), D_MIX ** -0.5),
        "norm_ffn_g": 1.0 + nrm((L, D_MODEL), 0.02),
        "w_up": nrm((L, D_MODEL, 2 * D_FF), D_MODEL ** -0.5),
        "conv_w": nrm((L, CONV_W, 2 * D_FF), CONV_W ** -0.5),
        "conv_b": nrm((L, 2 * D_FF), 0.02),
        "w_down": nrm((L, D_FF, D_MODEL), D_FF ** -0.5),
    }


def reference(x, c, rel_bias, w_ada, b_ada, norm_mix_g, w_in, q_norm_g, k_norm_g,
              lambda_q1, lambda_k1, lambda_q2, lambda_k2, attn_subln_g, mu_rkv, mu_wag,
              w0, w1, w2, a0, a1, a2, g1, g2, k_k, k_a, r_k, ln_x_g, ln_x_b, w_out,
              norm_ffn_g, w_up, conv_w, conv_b, w_down):
    B, S = x.shape[0], x.shape[1]
    c_act = jax.nn.silu(c)
    for l in range(DEPTH):
        lambda_init = 0.8 - 0.6 * math.exp(-0.3 * l)
        mod = c_act @ w_ada[l] + b_ada[l]
        shift_m, scale_m, gate_m, shift_f, scale_f, gate_f = jnp.split(mod, 6, axis=-1)

        h = rmsnorm(x, norm_mix_g[l]) * (1.0 + scale_m[:, None]) + shift_m[:, None]
        p = h @ w_in[l]
        q, k, v, r_p, kr_p, vr_p = jnp.split(p, 6, axis=-1)
        q = q.reshape(B, S, ATTN_HEADS, 2, ATTN_HEAD_DIM)
        k = k.reshape(B, S, ATTN_HEADS, 2, ATTN_HEAD_DIM)
        v = v.reshape(B, S, ATTN_HEADS, ATTN_V_DIM)
        lam = (jnp.exp(jnp.sum(lambda_q1[l].astype(jnp.float32) * lambda_k1[l].astype(jnp.float32)))
               - jnp.exp(jnp.sum(lambda_q2[l].astype(jnp.float32) * lambda_k2[l].astype(jnp.float32)))
               + lambda_init)
        o_attn = diff_attention(q, k, v, q_norm_g[l], k_norm_g[l], rel_bias, lam,
                                attn_subln_g[l], lambda_init)
        o_rwkv = rwkv7_time_mix(h, r_p, kr_p, vr_p, mu_rkv[l], mu_wag[l], w0[l], w1[l], w2[l],
                                a0[l], a1[l], a2[l], g1[l], g2[l], k_k[l], k_a[l], r_k[l],
                                ln_x_g[l], ln_x_b[l])
        mix = jnp.concatenate([o_attn, o_rwkv], axis=-1) @ w_out[l]
        x = x + gate_m[:, None] * mix

        h2 = rmsnorm(x, norm_ffn_g[l]) * (1.0 + scale_f[:, None]) + shift_f[:, None]
        x = x + gate_f[:, None] * conv_glu_ffn(h2, w_up[l], conv_w[l], conv_b[l], w_down[l])
    return x
```

```python
import math
import contextlib
import numpy as np
import ml_dtypes
import concourse.bass as bass
import concourse.mybir as mybir
from concourse.bass_utils import run_bass_kernel_spmd

F32 = mybir.dt.float32
BF16 = mybir.dt.bfloat16
AF = mybir.ActivationFunctionType
ALU = mybir.AluOpType
AX = mybir.AxisListType

SEM_LIMIT = 30000
D = 2048
KC = 16
DFF = 5632
NFF = 44
NEG = -30000.0


class Sched:
    ENGS = ("pe", "act", "dve", "pool", "sp")

    def __init__(self, nc):
        self.nc = nc
        self.ops = []
        self.by_eng = {e: [] for e in self.ENGS}
        self.last_writer = {}
        self.readers = {}
        self.since_barrier = []
        self.stage = 0

    def op(self, eng, fn, reads=(), writes=(), dma=None, extra_deps=()):
        idx = len(self.ops)
        deps = set(extra_deps)
        raw = set()
        for r in reads:
            w = self.last_writer.get(r)
            if w is not None:
                deps.add(w)
                raw.add(w)
        for r in writes:
            w = self.last_writer.get(r)
            if w is not None:
                deps.add(w)
            for rd in self.readers.get(r, ()):
                deps.add(rd)
        deps.discard(idx)
        self.ops.append(dict(eng=eng, fn=fn, deps=deps, raw=raw, dma=dma, stage=self.stage,
                             pos=len(self.by_eng[eng]), force=bool(extra_deps)))
        self.by_eng[eng].append(idx)
        for r in writes:
            self.last_writer[r] = idx
            self.readers[r] = []
        for r in reads:
            if r not in writes:
                self.readers.setdefault(r, []).append(idx)
        if dma is not None:
            self.since_barrier.append(idx)
        return idx

    def barrier(self):
        last = [self.by_eng[e][-1] for e in self.ENGS if self.by_eng[e]]
        deps = set(last) | set(self.since_barrier)
        self.since_barrier = []
        for e in self.ENGS:
            self.op(e, lambda eng: eng.nop(), extra_deps=deps)
        self.stage += 1

    def emit(self):
        nc = self.nc
        ops = self.ops
        need = [[] for _ in ops]
        has_dep = [False] * len(ops)
        for i, o in enumerate(ops):
            for d in sorted(o["deps"]):
                p = ops[d]
                if p["dma"] is None and o["dma"] is None and p["eng"] == o["eng"] and not o["force"]:
                    if p["eng"] == "pe":
                        continue
                    if d in o["raw"] and o["pos"] - p["pos"] <= 2:
                        need[i].append(d)
                        has_dep[d] = True
                    continue
                if p["dma"] is None and p["eng"] == o["eng"] and o["force"]:
                    continue
                need[i].append(d)
                has_dep[d] = True
        eng_count = {e: 0 for e in self.ENGS}
        dma_count = {}
        slot_of = {}
        n_in_stage = {}
        for i, o in enumerate(ops):
            if o["dma"] is not None:
                sk_ = (o["stage"], o["dma"])
                if sk_ not in slot_of:
                    n_in_stage[o["stage"]] = n_in_stage.get(o["stage"], 0) + 1
                    slot_of[sk_] = n_in_stage[o["stage"]] - 1
                k = slot_of[sk_]
                c = dma_count.get(k, 0)
                per = SEM_LIMIT // 16
                o["sem"] = ("d", k, c // per)
                o["val"] = 16 * (c % per + 1)
                dma_count[k] = c + 1
            elif has_dep[i]:
                c = eng_count[o["eng"]]
                o["sem"] = ("e", o["eng"], c // SEM_LIMIT)
                o["val"] = c % SEM_LIMIT + 1
                eng_count[o["eng"]] = c + 1
        names = sorted({o["sem"] for o in ops if "sem" in o}, key=str)
        sems = {}
        for n_, sn in enumerate(names):
            sems[sn] = nc.alloc_semaphore(name=f"sm{n_}")
        self.n_sems = len(sems)

        def run_engine(eng_name):
            def body(e):
                known = {}
                for i in self.by_eng[eng_name]:
                    o = ops[i]
                    w = {}
                    for d in need[i]:
                        p = ops[d]
                        s, v = p["sem"], p["val"]
                        if w.get(s, 0) < v:
                            w[s] = v
                    for s, v in w.items():
                        if known.get(s, 0) >= v:
                            continue
                        e.wait_ge(sems[s], v)
                        known[s] = v
                    ins = o["fn"](e)
                    if "sem" in o:
                        ins.then_inc(sems[o["sem"]], 16 if o["dma"] is not None else 1)
            return body

        with nc.Block() as block:
            block.tensor(run_engine("pe"))
            block.scalar(run_engine("act"))
            block.vector(run_engine("dve"))
            block.gpsimd(run_engine("pool"))
            block.sync(run_engine("sp"))


class Ring:
    def __init__(self, name, tiles):
        self.name = name
        self.tiles = tiles
        self.i = 0

    def next(self):
        k = self.i % len(self.tiles)
        self.i += 1
        return self.tiles[k], f"{self.name}{k}"


def build_program(L, dbg=False, upto=9, cut=99, rwkv_only=False):
    nc = bass.Bass("TRN2", target_bir_lowering=False)
    S = Sched(nc)
    P0 = L // 2
    NT = 512
    NTILE = L // NT
    NBLK = L // 128
    MIXB = P0 // 128 - 1
    FULLT = (MIXB * 128) // NT
    NCH = L // 64
    MIXCH = MIXB * 2
    TOK0 = MIXB * 128
    NMIX = L - TOK0
    FFN0 = P0 - 2
    NFT = next(n for n in range(1, 64) if (L - FFN0) % n == 0 and (L - FFN0) // n <= 512)
    FT = (L - FFN0) // NFT

    if rwkv_only:
        MIXCH = 0

    def din(name, shape, dt=F32):
        if rwkv_only and name in ("x", "w_ada", "w_in", "w_out", "w_up", "w_down", "w_l"):
            return None
        return nc.dram_tensor(name, list(shape), dt, kind="ExternalInput").ap()

    def dscr(name, shape, dt):
        if rwkv_only and name[:-2] in ("kap", "bt", "kt", "rt", "bh", "kh", "vT", "WC", "bon", "g"):
            return nc.dram_tensor(name, list(shape), dt, kind="ExternalInput").ap()
        return nc.dram_tensor(name, list(shape), dt, kind="ExternalOutput" if dbg else "Internal").ap()

    x_d = din("x", [L, D])
    ccol_d = din("c_col", [128, 16])
    flag_d = din("flagv", [128, 2])
    wada_d = din("w_ada", [D, 6 * D])
    bada_d = din("b_ada", [1, 6 * D])
    gcol_d = din("gcols", [128, 16, 2])
    win_d = din("w_in", [D, 6144])
    wl_d = din("w_l", [D, 288])
    muw_d = din("mu_wag", [128, 16, 3])
    mur_d = din("mu_rkv", [128, 8, 3])
    rwv_d = din("rwvec", [128, 8, 5])
    ln64_d = din("ln64", [64, 16, 2])
    wa2_d = din("wa2", [128, 1024])
    g2_d = din("g2", [160, 1024])
    qkg_d = din("qkg", [128, 3])
    lam_d = din("lamrow", [1, 256])
    biasd_d = din("biasd", [128, 8, 128])
    biasp_d = din("biasp", [128, 8, 128])
    farc_d = din("farc", [128, 8])
    maskd_d = din("maskd", [128, 128])
    wout_d = din("w_out", [D, D])
    wup_d = din("w_up", [D, 2 * DFF])
    wdn_d = din("w_down", [DFF, D])
    convw_d = din("convw", [128, 88, 3])
    convb_d = din("convb", [128, 88])
    identf_d = din("identf", [128, 128])
    identb_d = din("identb", [128, 128], BF16)
    blk_d = din("blkones", [128, 128])
    rmask_d = din("rmasks", [64, 320])
    y_d = nc.dram_tensor("y", [P0, D], F32, kind="ExternalOutput").ap()

    qT_s = dscr("qT_s", [8, 128, L], BF16)
    kT_s = dscr("kT_s", [8, 128, L], BF16)
    v_s = dscr("v_s", [L, 1024], BF16)
    RW = ["kap", "bt", "kt", "rt", "bh", "kh", "vT"]
    rw_s = {n: dscr(n + "_s", [1024, L], BF16) for n in RW}
    WC_s = dscr("WC_s", [1024, NCH], F32)
    bon_s = dscr("bon_s", [1024, L], F32)
    g_s = dscr("g_s", [1024, L], F32)
    oT_s = dscr("oT_s", [D, L], BF16)
    xmid_s = dscr("xmid_s", [L, D], F32)
    h2T_s = dscr("h2T_s", [D, L], BF16)
    winb_s = nc.dram_tensor("winb_s", [48, 128, 2048], BF16, kind="Internal").ap()
    wupb_s = nc.dram_tensor("wupb_s", [88, 128, 2048], BF16, kind="Internal").ap()
    wdnb_s = nc.dram_tensor("wdnb_s", [DFF, D], BF16, kind="Internal").ap()
    woutb_s = nc.dram_tensor("woutb_s", [D, D], BF16, kind="Internal").ap()

    es = contextlib.ExitStack()
    with es:
        def T(name, shape, dt):
            return es.enter_context(nc.sbuf_tensor(name, list(shape), dt))

        psum = es.enter_context(nc.psum_tensor("ps", [128, 4096], F32))

        def bank(b, n=1):
            return psum[:, b * 512:(b + n) * 512]

        def PE(out, lhsT, rhs, start=True, stop=True, r=(), w=(), skip=False):
            if skip:
                S.op("pe", lambda e: e.matmul(out, lhsT=lhsT, rhs=rhs, start=start, stop=stop, skip_group_check=True), reads=r, writes=w)
            else:
                S.op("pe", lambda e: e.matmul(out, lhsT=lhsT, rhs=rhs, start=start, stop=stop), reads=r, writes=w)

        def PET(out, in_, ident, r=(), w=()):
            S.op("pe", lambda e: e.transpose(out, in_, ident), reads=r, writes=w)

        def ACT(out, in_, func, r=(), w=(), bias=None, scale=None, accum=None):
            kw = {}
            if bias is not None:
                kw["bias"] = bias
            if scale is not None:
                kw["scale"] = scale
            if accum is not None:
                kw["accum_out"] = accum
            S.op("act", lambda e: e.activation(out=out, in_=in_, func=func, **kw), reads=r, writes=w)

        def TT(eng, out, in0, in1, op, r=(), w=()):
            S.op(eng, lambda e: e.tensor_tensor(out=out, in0=in0, in1=in1, op=op), reads=r, writes=w)

        def TS(eng, out, in0, s1, op0, s2=None, op1=None, r=(), w=()):
            if op1 is None:
                S.op(eng, lambda e: e.tensor_scalar(out=out, in0=in0, scalar1=s1, scalar2=None, op0=op0), reads=r, writes=w)
            else:
                S.op(eng, lambda e: e.tensor_scalar(out=out, in0=in0, scalar1=s1, scalar2=s2, op0=op0, op1=op1), reads=r, writes=w)

        def STT(out, in0, scalar, in1, op0, op1, r=(), w=()):
            S.op("dve", lambda e: e.scalar_tensor_tensor(out=out, in0=in0, scalar=scalar, in1=in1, op0=op0, op1=op1), reads=r, writes=w)

        def CP(eng, out, in_, r=(), w=()):
            if eng == "act":
                ACT(out, in_, AF.Copy, r=r, w=w)
            else:
                S.op(eng, lambda e: e.tensor_copy(out=out, in_=in_), reads=r, writes=w)

        def MEMSET(eng, ap, val, w=()):
            S.op(eng, lambda e: e.memset(ap, val), writes=w)

        def DMA(out, in_, r=(), w=(), key="d"):
            S.op("sp", lambda e: e.dma_start(out=out, in_=in_), reads=r, writes=w, dma=key)

        def finish_partial():
            zz = T("zz", [128, D], F32)
            MEMSET("pool", zz[:], 0.0, w=["zz"])
            for i in range(P0 // 128):
                DMA(y_d[i * 128:(i + 1) * 128, :], zz[:], r=["zz"], w=[f"y{i}"], key="yz")
            S.barrier()
            S.emit()

        identF = T("identF", [128, 128], F32)
        identB = T("identB", [128, 128], BF16)
        blkF = T("blkF", [128, 128], F32)
        blkB = T("blkB", [128, 128], BF16)
        ones64 = T("ones64", [64, 64], F32)
        onesrow = T("onesrow", [1, 128], F32)
        flagv = T("flagv_s", [128, 2], F32)
        gcols = T("gcols_s", [128, 16, 2], F32)
        modcol = T("modcol", [128, 96], F32)
        nrm = T("nrm", [128, 16, 8], F32)
        gate_m = T("gate_m", [128, D], F32)
        gate_f = T("gate_f", [128, D], F32)
        mur = T("mur", [128, 8, 3], F32)
        omur = T("omur", [128, 8, 3], F32)
        rwv = T("rwv", [128, 8, 5], F32)
        ln64 = T("ln64_s", [64, 16, 2], F32)
        qkg = T("qkg_s", [128, 3], F32)
        qkgs = T("qkgs", [128, 3], F32)
        neglam = T("neglam", [128, 1], F32)
        farc = T("farc_s", [128, 8], F32)
        farcp = T("farcp", [128, 8], F32)
        Wla = T("Wla", [128, 16, 288], BF16)
        Wlb = T("Wlb", [128, 16, 288], BF16)
        wa2b = T("wa2b", [128, 1024], BF16)
        g2b = T("g2b", [128, 1024], BF16)
        g2c = T("g2c", [32, 1024], BF16)
        rmask = T("rmask", [64, 320], F32)
        carry = T("carry", [128, 32], F32)

        DMA(identF[:], identf_d, w=["identF"], key="c0")
        DMA(identB[:], identb_d, w=["identB"], key="c1")
        DMA(blkF[:], blk_d, w=["blkF"], key="c2")
        DMA(flagv[:], flag_d, w=["flagv"], key="c3")
        DMA(gcols[:], gcol_d, w=["gcols"], key="c4")
        DMA(mur[:], mur_d, w=["mur"], key="c5")
        DMA(rwv[:], rwv_d, w=["rwv"], key="c6")
        DMA(ln64[:], ln64_d, w=["ln64"], key="c7")
        DMA(qkg[:], qkg_d, w=["qkg"], key="c8")
        DMA(farc[:], farc_d, w=["farc"], key="c9")
        DMA(rmask[:], rmask_d, w=["rmask"], key="c10")
        CP("dve", blkB[:], blkF[:], r=["blkF"], w=["blkB"])
        MEMSET("pool", ones64[:], 1.0 / 64.0, w=["ones64"])
        MEMSET("pool", onesrow[:], 1.0, w=["onesrow"])
        MEMSET("pool", carry[:], 0.0, w=["carry"])
        TS("dve", omur[:], mur[:], -1.0, ALU.mult, 1.0, ALU.add, r=["mur"], w=["omur"])
        TS("dve", qkgs[:, 0:1], qkg[:, 0:1], 0.125, ALU.mult, r=["qkg"], w=["qkgs"])
        CP("dve", qkgs[:, 1:2], qkg[:, 1:2], r=["qkg", "qkgs"], w=["qkgs"])
        TS("dve", qkgs[:, 2:3], qkg[:, 2:3], 0.8, ALU.mult, r=["qkg", "qkgs"], w=["qkgs"])
        TS("dve", farcp[:], farc[:], flagv[:, 1:2], ALU.add, r=["farc", "flagv"], w=["farcp"])

        with (contextlib.ExitStack() if not rwkv_only else contextlib.nullcontext()) as st:
          if not rwkv_only:
              def T0(name, shape, dt):
                  return st.enter_context(nc.sbuf_tensor(name, list(shape), dt))
              ccol = T0("ccol", [128, 16], F32)
              cact = T0("cact", [128, 16], F32)
              modrow = T0("modrow", [1, 6 * D], F32)
              lamr = T0("lamr", [1, 256], F32)
              lamw = T0("lamw", [1, 136], F32)
              wst = [T0(f"wst{i}", [128, 16, 256], F32) for i in range(2)]
              wlf = T0("wlf", [128, 16, 288], F32)
              muw = T0("muw", [128, 16, 3], F32)
              omuw = T0("omuw", [128, 16, 3], F32)
              wa2f = T0("wa2f", [128, 1024], F32)
              g2f = T0("g2f", [128, 1024], F32)
              g2cf = T0("g2cf", [32, 1024], F32)
              one11 = T0("one11", [1, 1], F32)
              pcf = Ring("pcf", [T0(f"pcf{i}", [128, 16, 128], F32) for i in range(2)])
              pcb = Ring("pcb", [T0(f"pcb{i}", [128, 16, 128], BF16) for i in range(2)])
              winv0 = win_d.rearrange("(kc p) n -> p kc n", p=128)

              DMA(ccol[:], ccol_d, w=["ccol"], key="c0")
              DMA(modrow[:], bada_d, w=["modrow"], key="c1")
              DMA(lamr[:], lam_d, w=["lamr"], key="c2")
              DMA(wlf[:], wl_d.rearrange("(kc p) n -> p kc n", p=128), w=["wlf"], key="c3")
              DMA(muw[:], muw_d, w=["muw"], key="c4")
              DMA(wa2f[:], wa2_d, w=["wa2f"], key="c5")
              DMA(g2f[:], g2_d[0:128, :], w=["g2f"], key="c6")
              DMA(g2cf[:], g2_d[128:160, :], w=["g2cf"], key="c7")
              MEMSET("pool", one11[:], 1.0, w=["one11"])
              ACT(cact[:], ccol[:], AF.Silu, r=["ccol"], w=["cact"])
              wav = wada_d.rearrange("(p k) n -> p k n", k=16)
              for ct in range(48):
                  ws, wk = wst[ct % 2], f"wst{ct % 2}"
                  DMA(ws[:], wav[:, :, ct * 256:(ct + 1) * 256], w=[wk], key=wk)
                  pb = ct % 2
                  for k in range(16):
                      PE(bank(pb)[0:1, 0:256], cact[:, k:k + 1], ws[:, k, :], start=(k == 0), stop=(k == 15),
                         r=["cact", wk], w=[f"pb{pb}"])
                  TT("dve", modrow[0:1, ct * 256:(ct + 1) * 256], bank(pb)[0:1, 0:256], modrow[0:1, ct * 256:(ct + 1) * 256],
                     ALU.add, r=[f"pb{pb}", "modrow"], w=["modrow"])
                  pf, pfk = pcf.next()
                  DMA(pf[:], winv0[:, :, ct * 128:(ct + 1) * 128], w=[pfk], key=pfk)
                  pbt, pbk = pcb.next()
                  CP("pool", pbt[:], pf[:], r=[pfk], w=[pbk])
                  DMA(winb_s[ct], pbt[:].rearrange("p k n -> p (k n)"), r=[pbk], w=[f"winb_{ct}"], key=pbk)
              for j in range(96):
                  PE(bank(2)[:, j:j + 1], modrow[0:1, j * 128:(j + 1) * 128], one11[0:1, 0:1],
                     r=["modrow", "one11"], w=["pb2"])
              CP("dve", modcol[:], bank(2)[:, 0:96], r=["pb2"], w=["modcol"])
              for gi, (gt, off) in enumerate(((gate_m, 2 * D), (gate_f, 5 * D))):
                  for dt in range(4):
                      pb = 3 + (gi * 4 + dt) % 4
                      PE(bank(pb), onesrow[0:1, :], modrow[0:1, off + dt * 512: off + (dt + 1) * 512],
                         r=["modrow", "onesrow"], w=[f"pb{pb}"])
                      CP("act", gt[:, dt * 512:(dt + 1) * 512], bank(pb), r=[f"pb{pb}"], w=[f"gate{gi}"])
              for si, (gi, sc0, sh0) in enumerate(((0, 16, 0), (1, 64, 48))):
                  b4 = si * 4
                  STT(nrm[:, :, b4 + 0], modcol[:, sc0:sc0 + 16], 1.0, gcols[:, :, gi], ALU.add, ALU.mult,
                      r=["modcol", "gcols"], w=["nrm"])
                  CP("dve", nrm[:, :, b4 + 1], modcol[:, sh0:sh0 + 16], r=["modcol", "nrm"], w=["nrm"])
                  TS("dve", nrm[:, :, b4 + 2], nrm[:, :, b4 + 0], flagv[:, 0:1], ALU.mult, r=["nrm", "flagv"], w=["nrm"])
                  TS("dve", nrm[:, :, b4 + 3], nrm[:, :, b4 + 1], flagv[:, 0:1], ALU.mult, r=["nrm", "flagv"], w=["nrm"])
              TT("dve", lamw[0:1, 0:64], lamr[0:1, 0:64], lamr[0:1, 64:128], ALU.mult, r=["lamr"], w=["lamw"])
              TT("dve", lamw[0:1, 64:128], lamr[0:1, 128:192], lamr[0:1, 192:256], ALU.mult, r=["lamr", "lamw"], w=["lamw"])
              S.op("dve", lambda e: e.tensor_reduce(out=lamw[0:1, 128:130], in_=lamw[0:1, 0:128].rearrange("p (a b) -> p a b", a=2),
                                                    axis=AX.X, op=ALU.add), reads=["lamw"], writes=["lamw2"])
              ACT(lamw[0:1, 130:132], lamw[0:1, 128:130], AF.Exp, r=["lamw2"], w=["lamw3"])
              TT("dve", lamw[0:1, 132:133], lamw[0:1, 131:132], lamw[0:1, 130:131], ALU.subtract, r=["lamw3"], w=["lamw4"])
              TS("dve", lamw[0:1, 133:134], lamw[0:1, 132:133], -0.2, ALU.add, r=["lamw4"], w=["lamw5"])
              PE(bank(7)[:, 0:1], onesrow[0:1, :], lamw[0:1, 133:134], r=["lamw5", "onesrow"], w=["pb7"])
              CP("dve", neglam[:], bank(7)[:, 0:1], r=["pb7"], w=["neglam"])
              TS("dve", omuw[:], muw[:], -1.0, ALU.mult, 1.0, ALU.add, r=["muw"], w=["omuw"])
              for kc in range(16):
                  for (c0, c1, j) in ((0, 64, 0), (64, 128, 1), (128, 288, 2)):
                      TS("dve", Wla[:, kc, c0:c1], wlf[:, kc, c0:c1], omuw[:, kc, j:j + 1], ALU.mult, r=["wlf", "omuw"], w=["Wla"])
                      TS("pool", Wlb[:, kc, c0:c1], wlf[:, kc, c0:c1], muw[:, kc, j:j + 1], ALU.mult, r=["wlf", "muw"], w=["Wlb"])
              CP("pool", wa2b[:], wa2f[:], r=["wa2f"], w=["wa2b"])
              CP("pool", g2b[:], g2f[:], r=["g2f"], w=["g2b"])
              CP("pool", g2c[:], g2cf[:], r=["g2cf"], w=["g2c"])
              S.barrier()

        with (contextlib.ExitStack() if not rwkv_only else contextlib.nullcontext()) as st:
          if not rwkv_only:
              def T1(name, shape, dt):
                  return st.enter_context(nc.sbuf_tensor(name, list(shape), dt))
              xring = Ring("xs", [T1(f"xs{i}", [128, D], F32) for i in range(2)])
              xnring = Ring("xn", [T1(f"xn{i}", [128, D], F32) for i in range(1)])
              stat = Ring("stat", [T1(f"stat{i}", [128, 4], F32) for i in range(4)])
              hTs = [T1(f"hT{i}", [128, 16, NT], BF16) for i in range(1)]
              wbr = Ring("wb", [T1(f"wb{i}", [128, 16, 128], BF16) for i in range(4)])
              wkF = Ring("wkF", [T1(f"wkF{i}", [128, 516], F32) for i in range(34)])
              mixR = Ring("mixR", [T1(f"mixR{i}", [128, 516], F32) for i in range(6)])
              wkB = Ring("wkB", [T1(f"wkB{i}", [128, 512], BF16) for i in range(10)])
              hidA = T1("hidA", [128, NT], BF16)
              hidB = T1("hidB", [128, NT], BF16)
              hidC = T1("hidC", [32, NT], BF16)
              onesT = T1("onesT", [128, NT], F32)
              baseT = Ring("base", [T1(f"base{i}", [128, 8], F32) for i in range(3)])
              wcT = Ring("wc", [T1(f"wc{i}", [128, 8], F32) for i in range(3)])
              MEMSET("pool", onesT[:], 1.0, w=["onesT"])
              PSR = Ring("pb", [None] * 8)

              def nbank():
                  _, k = PSR.next()
                  return bank(int(k[2:])), k

              def load_w(c0):
                  wb, wbk = wbr.next()
                  DMA(wb[:].rearrange("p k n -> p (k n)"), winb_s[c0 // 128], r=[f"winb_{c0 // 128}"], w=[wbk], key=wbk)
                  return wb, wbk

              def proj(wb_ap, wbk, hT, hk, ncol=128):
                  pb, pk = nbank()
                  for kc in range(16):
                      PE(pb[0:ncol, :], wb_ap(kc), hT[:, kc, :], start=(kc == 0), stop=(kc == 15), r=[wbk] + hk, w=[pk])
                  return pb, pk

              cidx = [0]

              def shifted(pb, pk, ci):
                  t, tk = wkF.next()
                  ACT(t[:, 1:NT + 1], pb, AF.Copy, r=[pk], w=[tk])
                  CP("pool", t[:, 0:1], carry[:, ci:ci + 1], r=["carry%d" % ci, tk], w=[tk])
                  CP("pool", carry[:, ci:ci + 1], t[:, NT:NT + 1], r=[tk], w=["carry%d" % ci])
                  return t, tk

              for ti in range(NTILE if cut > 0 else 0):
                  kvonly = ti < FULLT
                  prefix = (ti * NT) < P0
                  hT, hk0 = hTs[0], "hT0"
                  hk = [hk0 + "d", hk0 + "a"]
                  nb = 2 if prefix else 0
                  for bi in range(NT // 128):
                      tok = ti * NT + bi * 128
                      xs, xk = xring.next()
                      DMA(xs[:], x_d[tok:tok + 128, :], w=[xk], key=xk)
                      sv, sk = stat.next()
                      xn, xnk = xnring.next()
                      ACT(xn[:], xs[:], AF.Square, r=[xk], w=[xnk, sk], accum=sv[:, 0:1])
                      ACT(sv[:, 1:2], sv[:, 0:1], AF.Ln, r=[sk], w=[sk + "b"], scale=1.0 / D, bias=1e-6)
                      ACT(sv[:, 2:3], sv[:, 1:2], AF.Exp, r=[sk + "b"], w=[sk + "c"], scale=-0.5)
                      TS("dve", xn[:], xs[:], sv[:, 2:3], ALU.mult, r=[xk, sk + "c"], w=[xnk])
                      for kg in range(4):
                          pb, pk = nbank()
                          for j in range(4):
                              kc = kg * 4 + j
                              PET(pb[:, j * 128:(j + 1) * 128], xn[:, kc * 128:(kc + 1) * 128], identF[:], r=[xnk, "identF"], w=[pk])
                          for j in range(4):
                              kc = kg * 4 + j
                              TS("dve", hT[:, kc, bi * 128:(bi + 1) * 128], pb[:, j * 128:(j + 1) * 128],
                                 nrm[:, kc, nb:nb + 1], ALU.mult, nrm[:, kc, nb + 1:nb + 2], ALU.add, r=[pk, "nrm"], w=[hk0 + "d"])
                  if cut <= 1:
                      continue
                  ci = 0
                  for (c0, nc_, hid, chunk) in ((0, 128, hidA, "A"), (128, 128, hidB, "B"), (256, 32, hidC, "C")):
                      if kvonly and chunk != "A":
                          ci += 1
                          continue
                      pa, pak = proj(lambda kc, c0=c0, nc_=nc_: Wla[:, kc, c0:c0 + nc_], "Wla", hT, hk, nc_)
                      pb_, pbk = proj(lambda kc, c0=c0, nc_=nc_: Wlb[:, kc, c0:c0 + nc_], "Wlb", hT, hk, nc_)
                      t, tk = wkF.next()
                      ACT(t[0:nc_, 1:NT + 1], pb_[0:nc_, :], AF.Copy, r=[pbk], w=[tk])
                      CP("pool", t[0:nc_, 0:1], carry[0:nc_, ci:ci + 1], r=["carry%d" % ci, tk], w=[tk])
                      CP("pool", carry[0:nc_, ci:ci + 1], t[0:nc_, NT:NT + 1], r=[tk], w=["carry%d" % ci])
                      u, uk = wkF.next()
                      TT("dve", u[0:nc_, 0:NT], pa[0:nc_, :], t[0:nc_, 0:NT], ALU.add, r=[pak, tk], w=[uk])
                      if chunk == "A":
                          ACT(hid[0:64, :], u[0:64, 0:NT], AF.Tanh, r=[uk], w=["hidA"])
                          ACT(hid[64:128, :], u[64:128, 0:NT], AF.Copy, r=[uk, "hidA"], w=["hidA"])
                      else:
                          ACT(hid[0:nc_, :], u[0:nc_, 0:NT], AF.Sigmoid, r=[uk], w=["hid" + chunk])
                      ci += 1
                  def rwA(fg):
                      f0 = fg * 128
                      cb = 3 + fg * 3
                      pk_ = {}
                      for wi, col0 in enumerate((3072, 4096, 5120)):
                          if kvonly and wi == 0:
                              continue
                          wb, wbk = load_w(col0 + f0)
                          pbx, pkx = proj(lambda kc, wb=wb: wb[:, kc, :], wbk, hT, hk)
                          m1, m1k = wkF.next()
                          ACT(m1[:, 0:NT], pbx, AF.Copy, r=[pkx, "omur"], w=[m1k], scale=omur[:, fg, wi:wi + 1])
                          t, tk = shifted(pbx, pkx, cb + wi)
                          m2, m2k = mixR.next()
                          STT(m2[:, 0:NT], t[:, 0:NT], mur[:, fg, wi:wi + 1], m1[:, 0:NT], ALU.mult, ALU.add, r=[tk, m1k, "mur"], w=[m2k])
                          pk_[wi] = (m2, m2k)
                      return pk_

                  def rwB(fg, pk_):
                      f0 = fg * 128
                      tsl = slice(ti * NT, (ti + 1) * NT)
                      pw, pwk = nbank()
                      PE(pw, wa2b[0:64, f0:f0 + 128], hidA[0:64, :], r=["wa2b", "hidA"], w=[pwk])
                      ld, ldk = wkF.next()
                      ACT(ld[:, 0:NT], pw, AF.Sigmoid, r=[pwk, "rwv"], w=[ldk], bias=rwv[:, fg, 0:1])
                      pa, pak = nbank()
                      PE(pa, wa2b[64:128, f0:f0 + 128], hidA[64:128, :], r=["wa2b", "hidA"], w=[pak])
                      av, avk = wkF.next()
                      ACT(av[:, 0:NT], pa, AF.Sigmoid, r=[pak, "rwv"], w=[avk], bias=rwv[:, fg, 1:2])
                      yield
                      ACT(ld[:, 0:NT], ld[:, 0:NT], AF.Copy, r=[ldk], w=[ldk], scale=-math.exp(-0.5))
                      kx, kxk = pk_[1]
                      vx, vxk = pk_[2]
                      k0, k0k = wkF.next()
                      ACT(k0[:, 0:NT], kx[:, 0:NT], AF.Copy, r=[kxk, "rwv"], w=[k0k], scale=rwv[:, fg, 2:3])
                      yield
                      Lr, Lrk = wkF.next()
                      S.op("dve", lambda e, Lr=Lr, ld=ld: e.tensor_tensor_scan(out=Lr[:, 0:NT], data0=onesT[:], data1=ld[:, 0:NT],
                                                                                 initial=0.0, op0=ALU.mult, op1=ALU.add),
                           reads=[ldk, "onesT"], writes=[Lrk])
                      sq, sqk = wkB.next()
                      ACT(sq[:], k0[:, 0:NT], AF.Square, r=[k0k], w=[sqk])
                      pss, pssk = nbank()
                      PE(pss, blkB[:], sq[:], r=["blkB", sqk], w=[pssk])
                      rn, rnk = wkF.next()
                      TS("dve", rn[:, 0:NT], pss, 1e-18, ALU.max, r=[pssk], w=[rnk])
                      yield
                      bs, bsk = baseT.next()
                      MEMSET("pool", bs[:, 0:1], 0.0, w=[bsk])
                      Lr3 = Lr[:, 0:NT].rearrange("p (c t) -> p c t", t=64)
                      CP("pool", bs[:, 1:8], Lr3[:, 0:7, 63], r=[Lrk, bsk], w=[bsk])
                      ACT(rn[:, 0:NT], rn[:, 0:NT], AF.Ln, r=[rnk], w=[rnk])
                      yield
                      TT("dve", Lr3, Lr3, bs[:, 0:8].unsqueeze(2).to_broadcast([128, 8, 64]), ALU.subtract, r=[Lrk, bsk], w=[Lrk])
                      ACT(rn[:, 0:NT], rn[:, 0:NT], AF.Exp, r=[rnk], w=[rnk], scale=-0.5)
                      yield
                      Wt, Wtk = wkF.next()
                      ACT(Wt[:, 0:NT], Lr[:, 0:NT], AF.Exp, r=[Lrk], w=[Wtk])
                      Wm, Wmk = wkF.next()
                      TT("pool", Wm[:, 0:NT], Lr[:, 0:NT], ld[:, 0:NT], ALU.subtract, r=[Lrk, ldk], w=[Wmk])
                      kn, knk = wkF.next()
                      TT("dve", kn[:, 0:NT], k0[:, 0:NT], rn[:, 0:NT], ALU.mult, r=[k0k, rnk], w=[knk])
                      yield
                      Wi, Wik = wkF.next()
                      ACT(Wi[:, 0:NT], Lr[:, 0:NT], AF.Exp, r=[Lrk], w=[Wik], scale=-1.0)
                      Wr, Wrk = wkF.next()
                      Wr3 = Wr[:, 0:NT].rearrange("p (c t) -> p c t", t=64)
                      TT("pool", Wr3, Lr3, Lr3[:, :, 63:64].to_broadcast([128, 8, 64]), ALU.subtract, r=[Lrk], w=[Wrk])
                      k2, k2k = wkF.next()
                      TS("dve", k2[:, 0:NT], av[:, 0:NT], -1.0, ALU.add, rwv[:, fg, 3:4], ALU.mult, r=[avk, "rwv"], w=[k2k])
                      yield
                      ACT(Wm[:, 0:NT], Wm[:, 0:NT], AF.Exp, r=[Wmk], w=[Wmk])
                      bb, bbk = wkF.next()
                      TT("pool", bb[:, 0:NT], kn[:, 0:NT], av[:, 0:NT], ALU.mult, r=[knk, avk], w=[bbk])
                      TT("dve", k2[:, 0:NT], k2[:, 0:NT], kx[:, 0:NT], ALU.mult, r=[k2k, kxk], w=[k2k])
                      yield
                      ACT(Wr[:, 0:NT], Wr[:, 0:NT], AF.Exp, r=[Wrk], w=[Wrk], scale=-1.0)
                      TT("pool", k2[:, 0:NT], k2[:, 0:NT], kx[:, 0:NT], ALU.add, r=[k2k, kxk], w=[k2k])
                      wc, wck = wcT.next()
                      Wt3 = Wt[:, 0:NT].rearrange("p (c t) -> p c t", t=64)
                      CP("pool", wc[:, 0:8], Wt3[:, :, 63], r=[Wtk], w=[wck])
                      DMA(WC_s[f0:f0 + 128, ti * 8:(ti + 1) * 8], wc[:, 0:8], r=[wck], w=[f"WC_{ti}_{fg}"], key=wck)
                      yield
                      outs = [("kap", kn, knk, Wm, Wmk, "dve"), ("bt", bb, bbk, Wi, Wik, "pool"), ("kt", k2, k2k, Wi, Wik, "dve"),
                              ("bh", bb, bbk, Wr, Wrk, "pool"), ("kh", k2, k2k, Wr, Wrk, "dve")]
                      if not kvonly:
                          rx, rxk = pk_[0]
                          outs.append(("rt", rx, rxk, Wt, Wtk, "pool"))
                      for (nm, a_, ak, b_, bk, eng) in outs:
                          o, ok = wkB.next()
                          TT(eng, o[:], a_[:, 0:NT], b_[:, 0:NT], ALU.mult, r=[ak, bk], w=[ok])
                          DMA(rw_s[nm][f0:f0 + 128, tsl], o[:], r=[ok], w=[f"{nm}_{ti}_{fg}"], key=ok)
                          yield
                      o, ok = wkB.next()
                      CP("pool", o[:], vx[:, 0:NT], r=[vxk], w=[ok])
                      DMA(rw_s["vT"][f0:f0 + 128, tsl], o[:], r=[ok], w=[f"vT_{ti}_{fg}"], key=ok)
                      if not kvonly:
                          rx, rxk = pk_[0]
                          bq, bqk = wkF.next()
                          STT(bq[:, 0:NT], rx[:, 0:NT], rwv[:, fg, 4:5], k2[:, 0:NT], ALU.mult, ALU.mult, r=[rxk, k2k, "rwv"], w=[bqk])
                          pbn, pbnk = nbank()
                          PE(pbn, blkF[:], bq[:, 0:NT], r=["blkF", bqk], w=[pbnk])
                          bo, bok = wkF.next()
                          TT("dve", bo[:, 0:NT], pbn, vx[:, 0:NT], ALU.mult, r=[pbnk, vxk], w=[bok])
                          DMA(bon_s[f0:f0 + 128, tsl], bo[:, 0:NT], r=[bok], w=[f"bon_{ti}_{fg}"], key=bok)
                          yield
                          pg, pgk = nbank()
                          PE(pg, g2b[:, f0:f0 + 128], hidB[:], start=True, stop=False, r=["g2b", "hidB"], w=[pgk])
                          PE(pg, g2c[:, f0:f0 + 128], hidC[:], start=False, stop=True, r=["g2c", "hidC"], w=[pgk])
                          go, gok = wkF.next()
                          ACT(go[:, 0:NT], pg, AF.Copy, r=[pgk], w=[gok])
                          DMA(g_s[f0:f0 + 128, tsl], go[:, 0:NT], r=[gok], w=[f"g_{ti}_{fg}"], key=gok)
                      yield

                  nfg = 8 if cut > 2 else 0
                  for fp_ in range(0, nfg, 2):
                      gens = [rwB(fg, rwA(fg)) for fg in (fp_, fp_ + 1)]
                      while gens:
                          for gen in list(gens):
                              try:
                                  next(gen)
                              except StopIteration:
                                  gens.remove(gen)
                  def qkA(h, which, col0, dst, gi):
                      wb, wbk = load_w(col0 + h * 128)
                      pbx, pkx = proj(lambda kc, wb=wb: wb[:, kc, :], wbk, hT, hk)
                      sq, sqk = wkB.next()
                      ACT(sq[:], pbx, AF.Square, r=[pkx], w=[sqk])
                      return (pbx, pkx, sq, sqk)

                  def qkB(h, which, col0, dst, gi, pbx, pkx, sq, sqk):
                      pss, pssk = nbank()
                      PE(pss, blkB[:], sq[:], r=["blkB", sqk], w=[pssk])
                      rn, rnk = wkF.next()
                      ACT(rn[:, 0:NT], pss, AF.Ln, r=[pssk], w=[rnk], scale=1.0 / 64.0, bias=1e-6)
                      ACT(rn[:, 0:NT], rn[:, 0:NT], AF.Exp, r=[rnk], w=[rnk], scale=-0.5)
                      o, ok = wkB.next()
                      STT(o[:], pbx, qkgs[:, gi:gi + 1], rn[:, 0:NT], ALU.mult, ALU.mult, r=[pkx, rnk, "qkgs"], w=[ok])
                      DMA(dst[h, :, ti * NT:(ti + 1) * NT], o[:], r=[ok], w=[f"{which}T_{ti}_{h}"], key=ok)

                  def vA(h):
                      wb, wbk = load_w(2048 + h * 128)
                      pbx, pkx = proj(lambda kc, wb=wb: wb[:, kc, :], wbk, hT, hk)
                      vb, vbk = wkB.next()
                      ACT(vb[:], pbx, AF.Copy, r=[pkx], w=[vbk])
                      return (vb, vbk)

                  def vB(h, vb, vbk):
                      pt, ptk = nbank()
                      ptb = pt.bitcast(BF16)
                      for bi in range(4):
                          PET(ptb[:, bi * 128:(bi + 1) * 128], vb[:, bi * 128:(bi + 1) * 128], identB[:], r=[vbk, "identB"], w=[ptk])
                      vo, vok = wkB.next()
                      CP("dve", vo[:], ptb[:, 0:512], r=[ptk], w=[vok])
                      DMA(v_s[ti * NT:(ti + 1) * NT, h * 128:(h + 1) * 128].rearrange("(b p) v -> p b v", p=128),
                          vo[:].rearrange("p (b v) -> p b v", b=4), r=[vok], w=[f"v_{ti}_{h}"], key=vok)

                  items = []
                  for h in range(8 if cut > 3 else 0):
                      items.append((qkA, qkB, (h, "k", 1024, kT_s, 1)))
                      if not kvonly:
                          items.append((qkA, qkB, (h, "q", 0, qT_s, 0)))
                      items.append((vA, vB, (h,)))
                  pendA = None
                  for n_, (fa, fb, args) in enumerate(items):
                      resA = fa(*args)
                      if pendA is not None:
                          pfb, pargs, pres = pendA
                          pfb(*pargs, *pres)
                      pendA = (fb, args, resA)
                  if pendA is not None:
                      pfb, pargs, pres = pendA
                      pfb(*pargs, *pres)
              S.barrier()

        if upto >= 2:
          with contextlib.ExitStack() as st:
            def T2(name, shape, dt):
                return st.enter_context(nc.sbuf_tensor(name, list(shape), dt))
            OPN = ["kap", "bt", "kt", "rt", "bh", "kh", "vT"]
            opr = Ring("opr", [{n: T2(f"op{i}_{n}", [64, 8, 64], BF16) for n in OPN} for i in range(4)])
            bgr = Ring("bgr", [(T2(f"bon{i}", [64, 8, 64], F32), T2(f"gg{i}", [64, 8, 64], F32)) for i in range(3)])
            I8 = T2("I8", [64, 8, 64], F32)
            WCall = T2("WCall", [64, 16, NCH], F32)
            ZF = [T2(f"ZF{g}", [64, 8, 64], F32) for g in range(2)]
            Zb = [T2(f"Zb{g}", [64, 8, 64], BF16) for g in range(2)]
            b16 = lambda nm, n: Ring(nm, [T2(f"{nm}{i}", [64, 8, 64], BF16) for i in range(n)])
            f32r = lambda nm, n: Ring(nm, [T2(f"{nm}{i}", [64, 8, 64], F32) for i in range(n)])
            Nr, NTr, Xbr = b16("Nr", 8), b16("NTr", 8), b16("Xbr", 8)
            ArbR, AukR, ArkR, TTr = b16("ArbR", 2), b16("AukR", 2), b16("ArkR", 2), b16("TTr", 2)
            BtR, KtR, VtR = b16("BtR", 2), b16("KtR", 2), b16("VtR", 2)
            RhR, UbR, OutR = b16("RhR", 2), b16("UbR", 2), b16("OutR", 3)
            yFr = f32r("yFr", 20)
            id64 = identB[0:64, 0:64]
            mST_n = rmask[:, 0:64].unsqueeze(1).to_broadcast([64, 8, 64])
            mIT = rmask[:, 64:128].unsqueeze(1).to_broadcast([64, 8, 64])
            mST = rmask[:, 128:192].unsqueeze(1).to_broadcast([64, 8, 64])
            mS_n = rmask[:, 256:320].unsqueeze(1).to_broadcast([64, 8, 64])
            for h in range(8):
                CP("pool", I8[:, h, :], identF[0:64, 0:64], r=["identF"], w=["I8"])
            DMA(WCall[:], WC_s.rearrange("(h d) c -> d h c", d=64),
                r=[f"WC_{ti}_{fg}" for ti in range(NTILE) for fg in range(8)], w=["WCall"], key="wcall")
            for g in range(2):
                MEMSET("pool", ZF[g][:], 0.0, w=[f"ZF{g}"])
                MEMSET("pool", Zb[g][:], 0.0, w=[f"Zb{g}"])
            PSR2 = Ring("pb", [None] * 6)

            def nb2():
                _, k = PSR2.next()
                return bank(int(k[2:]))[0:64, :].rearrange("p (h t) -> p h t", h=8), k

            def body(c, g):
                own = c >= MIXCH
                ti = (c * 64) // NT
                tsl = slice(c * 64, (c + 1) * 64)
                ops_, opk = opr.next()
                for n in OPN:
                    if n == "rt" and not own:
                        continue
                    srcs = [f"{n}_{ti}_{fg}" for fg in range(g * 4, g * 4 + 4)]
                    DMA(ops_[n][:], rw_s[n][g * 512:(g + 1) * 512, tsl].rearrange("(h d) t -> d h t", d=64),
                        r=srcs, w=[opk + n], key=opk + n)
                K_ = lambda n: opk + n
                kap, bt, kt, rt, bh, kh, vT = [ops_[n] for n in OPN]
                psx_i = 6 + g
                PSX = bank(psx_i)[0:64, :].rearrange("p (h t) -> p h t", h=8)
                PSXk = f"pb{psx_i}"
                if own:
                    (bo, gg), bgk = bgr.next()
                    DMA(bo[:], bon_s[g * 512:(g + 1) * 512, tsl].rearrange("(h d) t -> d h t", d=64),
                        r=[f"bon_{ti}_{fg}" for fg in range(g * 4, g * 4 + 4)], w=[bgk + "b"], key=bgk + "b")
                    DMA(gg[:], g_s[g * 512:(g + 1) * 512, tsl].rearrange("(h d) t -> d h t", d=64),
                        r=[f"g_{ti}_{fg}" for fg in range(g * 4, g * 4 + 4)], w=[bgk + "g"], key=bgk + "g")

                def mm8(ps, psk, lhs, lk, rhs, rk, start=True, stop=True):
                    for h in range(8):
                        PE(ps[:, h, :], lhs[:, h, :], rhs[:, h, :], start=(start and h == 0), stop=stop, r=[lk, rk], w=[psk], skip=True)

                p1, p1k = nb2()
                mm8(p1, p1k, bt, K_("bt"), kap, K_("kap"))
                NT0, NT0k = NTr.next()
                TT("dve", NT0[:], p1, mST_n, ALU.mult, r=[p1k, "rmask"], w=[NT0k])
                yield
                p3, p3k = nb2()
                mm8(p3, p3k, kap, K_("kap"), bt, K_("bt"))
                N0, N0k = Nr.next()
                TT("dve", N0[:], p3, mS_n, ALU.mult, r=[p3k, "rmask"], w=[N0k])
                X1, X1k = Xbr.next()
                TT("pool", X1[:], NT0[:], I8[:], ALU.add, r=[NT0k, "I8"], w=[X1k])
                yield
                p2, p2k = nb2()
                mm8(p2, p2k, kt, K_("kt"), kap, K_("kap"))
                Auk, Aukk = AukR.next()
                TT("dve", Auk[:], p2, mST, ALU.mult, r=[p2k, "rmask"], w=[Aukk])
                yield
                if own:
                    p4, p4k = nb2()
                    mm8(p4, p4k, bt, K_("bt"), rt, K_("rt"))
                    Arb, Arbk = ArbR.next()
                    TT("dve", Arb[:], p4, mIT, ALU.mult, r=[p4k, "rmask"], w=[Arbk])
                    yield
                    p5, p5k = nb2()
                    mm8(p5, p5k, kt, K_("kt"), rt, K_("rt"))
                    Ark, Arkk = ArkR.next()
                    TT("dve", Ark[:], p5, mIT, ALU.mult, r=[p5k, "rmask"], w=[Arkk])
                    yield
                for h in range(8):
                    PE(PSX[:, h, :], id64, X1[:, h, :], start=(h == 0), stop=True, r=["identB", X1k], w=[PSXk], skip=True)
                Ncur, Nk_, NTcur, NTk_ = N0, N0k, NT0, NT0k
                Xb, Xbk = X1, X1k
                tm = {}
                tlist = [(bh, K_("bh"), BtR), (kh, K_("kh"), KtR), (vT, K_("vT"), VtR)]
                for m in range(1, 6):
                    pa, pak = nb2()
                    mm8(pa, pak, NTcur, NTk_, Ncur, Nk_)
                    Nn, Nnk = Nr.next()
                    CP("act", Nn[:], pa, r=[pak], w=[Nnk])
                    if m <= 4:
                        pb_, pbk = nb2()
                        mm8(pb_, pbk, Ncur, Nk_, NTcur, NTk_)
                        NTn, NTnk = NTr.next()
                        CP("dve", NTn[:], pb_, r=[pbk], w=[NTnk])
                    if m >= 2:
                        Xb, Xbk = Xbr.next()
                        CP("act", Xb[:], PSX, r=[PSXk], w=[Xbk])
                    if tlist:
                        (src, sk_, ring_) = tlist.pop(0)
                        pt, ptk = nb2()
                        ptb = bank(int(ptk[2:]))[0:64, :].bitcast(BF16)[:, 0:512].rearrange("p (h t) -> p h t", h=8)
                        for h in range(8):
                            PET(ptb[:, h, :], src[:, h, :], id64, r=[sk_, "identB"], w=[ptk])
                        dst, dk_ = ring_.next()
                        CP("act", dst[:], ptb, r=[ptk], w=[dk_])
                        tm[sk_] = (dst, dk_)
                    yield
                    for h in range(8):
                        S.op("pe", lambda e, h=h, Nn=Nn, Xb=Xb, PSX=PSX: e.matmul(PSX[:, h, :], lhsT=Nn[:, h, :], rhs=Xb[:, h, :], start=False, stop=True, skip_group_check=True),
                             reads=[Nnk, Xbk], writes=[PSXk])
                    Ncur, Nk_ = Nn, Nnk
                    if m <= 4:
                        NTcur, NTk_ = NTn, NTnk
                    yield
                Bt_, Btk = tm[K_("bh")]
                Kt_, Ktk = tm[K_("kh")]
                Vt_, Vtk = tm[K_("vT")]
                TTt, TTk = TTr.next()
                CP("act", TTt[:], PSX, r=[PSXk], w=[TTk])
                yield
                Zk = f"Zb{g}"
                pr, prk = nb2()
                mm8(pr, prk, kap, K_("kap"), Zb[g], Zk, start=True, stop=False)
                mm8(pr, prk, Auk, Aukk, Vt_, Vtk, start=False, stop=True)
                Rh, Rhk = RhR.next()
                ACT(Rh[:], pr, AF.Copy, r=[prk], w=[Rhk], scale=-1.0)
                yield
                pu, puk = nb2()
                mm8(pu, puk, TTt, TTk, Rh, Rhk)
                Ub, Ubk = UbR.next()
                CP("dve", Ub[:], pu, r=[puk], w=[Ubk])
                yield
                if own:
                    py, pyk = nb2()
                    mm8(py, pyk, Zb[g], Zk, rt, K_("rt"), start=True, stop=False)
                    mm8(py, pyk, Ub, Ubk, Arb, Arbk, start=False, stop=False)
                    mm8(py, pyk, Vt_, Vtk, Ark, Arkk, start=False, stop=True)
                    yF, yFk = yFr.next()
                    ACT(yF[:], py, AF.Copy, r=[pyk], w=[yFk])
                    ysq, ysqk = yFr.next()
                    ACT(ysq[:], py, AF.Square, r=[pyk], w=[ysqk])
                    yield
                pz, pzk = nb2()
                mm8(pz, pzk, Bt_, Btk, Ub, Ubk, start=True, stop=False)
                mm8(pz, pzk, Kt_, Ktk, Vt_, Vtk, start=False, stop=True)
                TT("dve", ZF[g][:], ZF[g][:], WCall[:, g * 8:(g + 1) * 8, c:c + 1].to_broadcast([64, 8, 64]), ALU.mult,
                   r=[f"ZF{g}", "WCall"], w=[f"ZF{g}"])
                TT("dve", ZF[g][:], ZF[g][:], pz, ALU.add, r=[f"ZF{g}", pzk], w=[f"ZF{g}"])
                CP("pool", Zb[g][:], ZF[g][:], r=[f"ZF{g}"], w=[Zk])
                yield
                if own:
                    pm, pmk = nb2()
                    PE(bank(int(pmk[2:]))[0:64, :], ones64[:], yF[:].rearrange("p h t -> p (h t)"), r=["ones64", yFk], w=[pmk])
                    mean, meank = yFr.next()
                    ACT(mean[:], pm, AF.Copy, r=[pmk], w=[meank])
                    msq, msqk = yFr.next()
                    ACT(msq[:], pm, AF.Square, r=[pmk], w=[msqk])
                    yield
                    pq, pqk = nb2()
                    PE(bank(int(pqk[2:]))[0:64, :], ones64[:], ysq[:].rearrange("p h t -> p (h t)"), r=["ones64", ysqk], w=[pqk])
                    var, vark = yFr.next()
                    TT("dve", var[:], pq, msq[:], ALU.subtract, r=[pqk, msqk], w=[vark])
                    yield
                    ACT(var[:], var[:], AF.Ln, r=[vark], w=[vark], bias=64e-5)
                    ACT(var[:], var[:], AF.Exp, r=[vark], w=[vark], scale=-0.5)
                    yc, yck = yFr.next()
                    TT("pool", yc[:], yF[:], mean[:], ALU.subtract, r=[yFk, meank], w=[yck])
                    TT("pool", yc[:], yc[:], var[:], ALU.mult, r=[yck, vark], w=[yck])
                    TT("pool", yc[:], yc[:], ln64[:, g * 8:(g + 1) * 8, 0:1].to_broadcast([64, 8, 64]), ALU.mult, r=[yck, "ln64"], w=[yck])
                    TT("pool", yc[:], yc[:], ln64[:, g * 8:(g + 1) * 8, 1:2].to_broadcast([64, 8, 64]), ALU.add, r=[yck, "ln64"], w=[yck])
                    TT("pool", yc[:], yc[:], bo[:], ALU.add, r=[yck, bgk + "b"], w=[yck])
                    oo, ook = OutR.next()
                    TT("pool", oo[:], yc[:], gg[:], ALU.mult, r=[yck, bgk + "g"], w=[ook])
                    DMA(oT_s[1024 + g * 512:1024 + (g + 1) * 512, tsl].rearrange("(h d) t -> d h t", d=64), oo[:],
                        r=[ook], w=[f"orw_{c}_{g}"], key=ook)
                    yield

            for c in range(NCH):
                gens = [body(c, 0), body(c, 1)]
                while gens:
                    for gen in list(gens):
                        try:
                            next(gen)
                        except StopIteration:
                            gens.remove(gen)
            S.barrier()
        if upto >= 3:
          with contextlib.ExitStack() as st:
            def T3(name, shape, dt):
                return st.enter_context(nc.sbuf_tensor(name, list(shape), dt))
            KT = [T3(f"KT{i}", [128, L], BF16) for i in range(2)]
            VH = [T3(f"VH{i}", [128, NBLK, 130], BF16) for i in range(2)]
            QT = [T3(f"QT{i}", [128, NMIX], BF16) for i in range(2)]
            bdall = T3("bdall", [128, 8, 128], F32)
            bpall = T3("bpall", [128, 8, 128], F32)
            maskd = T3("maskd_s", [128, 128], F32)
            PTr = Ring("PTr", [T3(f"PT{i}", [128, 512], BF16) for i in range(8)])
            tmpr = Ring("tmpr", [T3(f"tmp{i}", [128, 128], F32) for i in range(8)])
            o_r = Ring("o_r", [T3(f"of{i}", [128, 128], F32) for i in range(4)])
            ob_r = Ring("ob_r", [T3(f"ob{i}", [128, 128], BF16) for i in range(4)])
            ot_r = Ring("ot_r", [T3(f"ot{i}", [128, 128], BF16) for i in range(4)])
            st_r = Ring("st_r", [T3(f"ast{i}", [128, 8], F32) for i in range(6)])
            junk3 = T3("junk3", [128, 128], F32)
            pcf3 = Ring("pcf3", [T3(f"pcf3{i}", [128, D], F32) for i in range(2)])
            pcb3 = Ring("pcb3", [T3(f"pcb3{i}", [128, D], BF16) for i in range(2)])
            wupv3 = wup_d.rearrange("(kc p) n -> p kc n", p=128)
            woutv3 = wout_d.rearrange("(kc p) n -> p kc n", p=128)
            woutbv3 = woutb_s.rearrange("(kc p) n -> p kc n", p=128)
            precast = ([("up", c) for c in range(88)] + [("dn", f) for f in range(NFF)] + [("out", kc) for kc in range(16)])

            def do_precast(n):
                for _ in range(n):
                    if not precast:
                        return
                    kind, ix = precast.pop(0)
                    pf, pfk = pcf3.next()
                    pbt, pbk = pcb3.next()
                    if kind == "up":
                        DMA(pf[:].rearrange("p (k n) -> p k n", k=16), wupv3[:, :, ix * 128:(ix + 1) * 128], w=[pfk], key=pfk)
                    elif kind == "dn":
                        DMA(pf[:], wdn_d[ix * 128:(ix + 1) * 128, :], w=[pfk], key=pfk)
                    else:
                        DMA(pf[:], woutv3[:, ix, :], w=[pfk], key=pfk)
                    CP("pool", pbt[:], pf[:], r=[pfk], w=[pbk])
                    if kind == "up":
                        DMA(wupb_s[ix], pbt[:], r=[pbk], w=[f"wupb_{ix}"], key=pbk)
                    elif kind == "dn":
                        DMA(wdnb_s[ix * 128:(ix + 1) * 128, :], pbt[:], r=[pbk], w=[f"wdnb_{ix}"], key=pbk)
                    else:
                        DMA(woutbv3[:, ix, :], pbt[:], r=[pbk], w=[f"woutb_{ix}"], key=pbk)
            DMA(bdall[:], biasd_d, w=["bdall"], key="c0")
            DMA(bpall[:], biasp_d, w=["bpall"], key="c1")
            DMA(maskd[:], maskd_d, w=["maskd"], key="c2")
            TT("dve", bdall[:], bdall[:], maskd[:].unsqueeze(1).to_broadcast([128, 8, 128]), ALU.add, r=["bdall", "maskd"], w=["bdall"])
            for i in range(2):
                MEMSET("pool", VH[i][:, :, 128:130], 1.0, w=[f"VH{i}"])
            PB0 = P0 // 128
            qgroups = {}
            for qb in range(MIXB, NBLK):
                qgroups.setdefault(qb // 4, []).append(qb)
            SCR = Ring("pb", [None] * 8)

            def sc_bank():
                while True:
                    _, k = SCR.next()
                    if int(k[2:]) >= 4:
                        return bank(int(k[2:])), k

            def kvq_keys(h):
                return ([f"kT_{ti}_{h}" for ti in range(NTILE)], [f"v_{ti}_{h}" for ti in range(NTILE)],
                        [f"qT_{ti}_{h}" for ti in range(FULLT, NTILE)])

            for h in range(8):
                hb = h % 2
                kk_, vk_, qk_ = kvq_keys(h)
                DMA(KT[hb][:], kT_s[h], r=kk_, w=[f"KT{hb}"], key=f"KT{hb}")
                DMA(VH[hb][:, :, 0:128], v_s[:, h * 128:(h + 1) * 128].rearrange("(b p) v -> p b v", p=128), r=vk_, w=[f"VH{hb}"], key=f"VH{hb}")
                DMA(QT[hb][:], qT_s[h, :, TOK0:L], r=qk_, w=[f"QT{hb}"], key=f"QT{hb}")
                for gi, qbs in sorted(qgroups.items()):
                    qb0, qb1 = qbs[0], qbs[-1]
                    nqb = len(qbs)
                    do_precast(4)
                    for b_ in range(4):
                        S.op("dve", lambda e, b_=b_: e.memset(bank(b_), 0.0), writes=[f"pb{b_}"])

                    def score(i, j):
                        lo = max(j, qb0)
                        c0 = (lo - qb0) * 128
                        ncol = (qb1 - lo + 1) * 128
                        qc0 = lo * 128 - TOK0
                        sb, sbk = sc_bank()
                        PE(sb[:, c0:c0 + ncol], KT[hb][i * 64:(i + 1) * 64, j * 128:(j + 1) * 128],
                           QT[hb][i * 64:(i + 1) * 64, qc0:qc0 + ncol], r=[f"KT{hb}", f"QT{hb}"], w=[sbk])
                        PT, PTk = PTr.next()
                        pref = j < PB0
                        for qb in range(lo, qb1 + 1):
                            cs = slice((qb - qb0) * 128, (qb - qb0 + 1) * 128)
                            if qb - j >= 2:
                                continue
                            tmp, tmpk = tmpr.next()
                            btile = bdall if qb == j else bpall
                            TT("dve", tmp[:], sb[:, cs], btile[:, h, :], ALU.add, r=[sbk, "bdall", "bpall"], w=[tmpk])
                            if pref:
                                ACT(PT[:, cs], tmp[:], AF.Exp, r=[tmpk, "flagv"], w=[PTk], bias=flagv[:, 1:2])
                            else:
                                ACT(PT[:, cs], tmp[:], AF.Exp, r=[tmpk], w=[PTk])
                        far0 = max(j + 2, qb0)
                        if far0 <= qb1:
                            fs = slice((far0 - qb0) * 128, (qb1 - qb0 + 1) * 128)
                            ACT(PT[:, fs], sb[:, fs], AF.Exp, r=[sbk, "farcp", "farc"], w=[PTk],
                                bias=(farcp[:, h:h + 1] if pref else farc[:, h:h + 1]))
                        return (i, j, lo, PT, PTk)

                    def pv(i, j, lo, PT, PTk):
                        for qb in range(lo, qb1 + 1):
                            ql = qb - qb0
                            ob_ = i * 2 + ql // 2
                            oc = (ql % 2) * 256
                            PE(bank(ob_)[:, oc:oc + 129], PT[:, ql * 128:(ql + 1) * 128], VH[hb][:, j, 0:129],
                               start=False, stop=True, r=[PTk, f"VH{hb}"], w=[f"pb{ob_}"], skip=True)

                    pend = []
                    for i in range(2):
                        for j in range(qb1 + 1):
                            pend.append(score(i, j))
                            if len(pend) > 2:
                                pv(*pend.pop(0))
                    while pend:
                        pv(*pend.pop(0))
                    for qb in qbs:
                        ql = qb - qb0
                        O0 = bank(0 + ql // 2)[:, (ql % 2) * 256:(ql % 2) * 256 + 129]
                        O1 = bank(2 + ql // 2)[:, (ql % 2) * 256:(ql % 2) * 256 + 129]
                        k0_, k1_ = f"pb{ql // 2}", f"pb{2 + ql // 2}"
                        sv, svk = st_r.next()
                        TS("dve", sv[:, 0:1], O0[:, 128:129], 1e-30, ALU.add, r=[k0_], w=[svk])
                        TS("dve", sv[:, 1:2], O1[:, 128:129], 1e-30, ALU.add, r=[k1_, svk], w=[svk])
                        S.op("dve", lambda e, sv=sv: e.reciprocal(out=sv[:, 2:4], in_=sv[:, 0:2]), reads=[svk], writes=[svk + "r"])
                        TT("dve", sv[:, 4:5], sv[:, 3:4], neglam[:], ALU.mult, r=[svk + "r", "neglam"], w=[svk + "n"])
                        of, ofk = o_r.next()
                        ACT(of[:], O0[:, 0:128], AF.Copy, r=[k0_, svk + "r"], w=[ofk], scale=sv[:, 2:3])
                        STT(of[:], O1[:, 0:128], sv[:, 4:5], of[:], ALU.mult, ALU.add, r=[k1_, svk + "n", ofk], w=[ofk])
                        ACT(junk3[:], of[:], AF.Square, r=[ofk], w=["junk3", svk + "s"], accum=sv[:, 5:6])
                        ACT(sv[:, 6:7], sv[:, 5:6], AF.Ln, r=[svk + "s"], w=[svk + "l"], scale=1.0 / 128.0, bias=1e-6)
                        ACT(sv[:, 7:8], sv[:, 6:7], AF.Exp, r=[svk + "l"], w=[svk + "e"], scale=-0.5)
                        ob, obk = ob_r.next()
                        TS("dve", ob[:], of[:], sv[:, 7:8], ALU.mult, r=[ofk, svk + "e"], w=[obk])
                        tb, tbk = sc_bank()
                        tbb = tb.bitcast(BF16)
                        PET(tbb[:, 0:128], ob[:], identB[:], r=[obk, "identB"], w=[tbk])
                        ot, otk = ot_r.next()
                        TS("dve", ot[:], tbb[:, 0:128], qkgs[:, 2:3], ALU.mult, r=[tbk, "qkgs"], w=[otk])
                        DMA(oT_s[h * 128:(h + 1) * 128, qb * 128:(qb + 1) * 128], ot[:], r=[otk], w=[f"oat_{h}_{qb}"], key=otk)
            do_precast(1000)
            S.barrier()
        if upto >= 4:
          with contextlib.ExitStack() as st:
            def T4(name, shape, dt):
                return st.enter_context(nc.sbuf_tensor(name, list(shape), dt))
            wo = T4("wo", [128, 16, D], BF16)
            oTr = Ring("oTb", [T4(f"oTb{i}", [128, 16, 128], BF16) for i in range(2)])
            x4r = Ring("x4", [T4(f"x4{i}", [128, D], F32) for i in range(2)])
            xmr = Ring("xm", [T4(f"xm{i}", [128, D], F32) for i in range(2)])
            xq = T4("xq", [128, D], F32)
            h2r = Ring("h2b", [T4(f"h2b{i}", [128, 16, 128], BF16) for i in range(2)])
            s4r = Ring("s4", [T4(f"s4{i}", [128, 4], F32) for i in range(4)])
            wov = woutb_s.rearrange("(kc p) n -> p kc n", p=128)
            for kc in range(16):
                DMA(wo[:, kc, :], wov[:, kc, :], r=[f"woutb_{kc}"], w=["wo%d" % (kc % 2)], key="wo%d" % (kc % 2))
            PS4 = Ring("pb", [None] * 8)
            oall = [f"orw_{c}_{g}" for c in range(MIXCH, NCH) for g in range(2)]
            for qb in range(MIXB, NBLK):
                tsl = slice(qb * 128, (qb + 1) * 128)
                oT, oTk = oTr.next()
                DMA(oT[:], oT_s.rearrange("(kc p) t -> p kc t", p=128)[:, :, tsl],
                    r=[f"oat_{h}_{qb}" for h in range(8)] + [f"orw_{c}_{g}" for c in (2 * qb, 2 * qb + 1) for g in range(2)],
                    w=[oTk], key=oTk)
                x4, x4k = x4r.next()
                DMA(x4[:], x_d[tsl, :], w=[x4k], key=x4k)
                xm, xmk = xmr.next()
                for dt in range(4):
                    _, pk = PS4.next()
                    pb = bank(int(pk[2:]))
                    dsl = slice(dt * 512, (dt + 1) * 512)
                    for kc in range(16):
                        PE(pb, oT[:, kc, :], wo[:, kc, dsl], start=(kc == 0), stop=(kc == 15), r=[oTk, "wo0", "wo1"], w=[pk])
                    TT("dve", xm[:, dsl], pb, gate_m[:, dsl], ALU.mult, r=[pk, "gate0"], w=[xmk + "a"])
                    TT("pool", xm[:, dsl], xm[:, dsl], x4[:, dsl], ALU.add, r=[xmk + "a", x4k], w=[xmk])
                DMA(xmid_s[tsl, :], xm[:], r=[xmk], w=[f"xmid_{qb}"], key=xmk)
                sv, svk = s4r.next()
                ACT(xq[:], xm[:], AF.Square, r=[xmk], w=["xq", svk], accum=sv[:, 0:1])
                ACT(sv[:, 1:2], sv[:, 0:1], AF.Ln, r=[svk], w=[svk + "b"], scale=1.0 / D, bias=1e-6)
                ACT(sv[:, 2:3], sv[:, 1:2], AF.Exp, r=[svk + "b"], w=[svk + "c"], scale=-0.5)
                TS("dve", xq[:], xm[:], sv[:, 2:3], ALU.mult, r=[xmk, svk + "c", "xq"], w=["xq"])
                h2, h2k = h2r.next()
                nb = 6 if qb * 128 < P0 else 4
                for kg in range(4):
                    _, pk = PS4.next()
                    pb = bank(int(pk[2:]))
                    for j in range(4):
                        kc = kg * 4 + j
                        PET(pb[:, j * 128:(j + 1) * 128], xq[:, kc * 128:(kc + 1) * 128], identF[:], r=["xq", "identF"], w=[pk])
                    for j in range(4):
                        kc = kg * 4 + j
                        TS("dve", h2[:, kc, :], pb[:, j * 128:(j + 1) * 128], nrm[:, kc, nb:nb + 1], ALU.mult,
                           nrm[:, kc, nb + 1:nb + 2], ALU.add, r=[pk, "nrm"], w=[h2k])
                DMA(h2T_s.rearrange("(kc p) t -> p kc t", p=128)[:, :, tsl], h2[:], r=[h2k], w=[f"h2T_{qb}"], key=h2k)
            S.barrier()

        if upto >= 5:
          with contextlib.ExitStack() as st:
            def T5(name, shape, dt):
                return st.enter_context(nc.sbuf_tensor(name, list(shape), dt))
            h2t = T5("h2t", [128, 16, FT], BF16)
            actT = T5("actT", [128, NFF, FT], BF16)
            wbr5 = Ring("wb5", [T5(f"wb5{i}", [128, 16, 128], BF16) for i in range(6)])
            usr = Ring("us", [T5(f"us{i}", [128, FT + 2], F32) for i in range(4)])
            tr5 = Ring("t5", [T5(f"t5{i}", [128, FT], F32) for i in range(4)])
            wdb = Ring("wdb", [T5(f"wdb{i}", [128, 1024], BF16) for i in range(6)])
            xm5 = Ring("xm5", [T5(f"xm5{i}", [128, 1024], F32) for i in range(2)])
            o5 = Ring("o5", [T5(f"o5{i}", [128, 512], F32) for i in range(3)])
            cw = T5("cw", [128, 88, 3], F32)
            cbv = T5("cbv", [128, 88], F32)
            car5 = T5("car5", [128, 88, 2], F32)
            DMA(cw[:], convw_d, w=["cw"], key="c0")
            DMA(cbv[:], convb_d, w=["cbv"], key="c1")
            MEMSET("pool", car5[:], 0.0, w=["car5"])
            PS5 = Ring("pb", [None] * 8)
            h2keys = [f"h2T_{qb}" for qb in range(MIXB, NBLK)]
            xmkeys = [f"xmid_{qb}" for qb in range(MIXB, NBLK)]
            subs = []
            o_ = 0
            while o_ < FT:
                subs.append((o_, min(128, FT - o_)))
                o_ += 128
            assert len(subs) * 2 <= 8
            for ft in range(NFT):
                t0 = FFN0 + ft * FT
                DMA(h2t[:], h2T_s.rearrange("(kc p) t -> p kc t", p=128)[:, :, t0:t0 + FT], r=h2keys, w=["h2t"], key="h2t")
                for f in range(NFF):
                    ys = {}
                    for which, col0, ci_ in (("g", f * 128, f), ("v", DFF + f * 128, NFF + f)):
                        wb, wbk = wbr5.next()
                        DMA(wb[:].rearrange("p k n -> p (k n)"), wupb_s[col0 // 128], r=[f"wupb_{col0 // 128}"], w=[wbk], key=wbk)
                        _, pk = PS5.next()
                        pb = bank(int(pk[2:]))
                        for kc in range(16):
                            PE(pb[:, 0:FT], wb[:, kc, :], h2t[:, kc, :], start=(kc == 0), stop=(kc == 15), r=[wbk, "h2t"], w=[pk])
                        us, usk = usr.next()
                        ACT(us[:, 2:FT + 2], pb[:, 0:FT], AF.Copy, r=[pk], w=[usk])
                        CP("pool", us[:, 0:2], car5[:, ci_, :], r=[f"car5_{ci_}", usk], w=[usk])
                        CP("pool", car5[:, ci_, :], us[:, FT:FT + 2], r=[usk], w=[f"car5_{ci_}"])
                        t, tk = tr5.next()
                        TS("dve", t[:], us[:, 2:FT + 2], cw[:, ci_, 2:3], ALU.mult, cbv[:, ci_:ci_ + 1], ALU.add, r=[usk, "cw", "cbv"], w=[tk])
                        STT(t[:], us[:, 1:FT + 1], cw[:, ci_, 1:2], t[:], ALU.mult, ALU.add, r=[usk, "cw", tk], w=[tk])
                        STT(t[:], us[:, 0:FT], cw[:, ci_, 0:1], t[:], ALU.mult, ALU.add, r=[usk, "cw", tk], w=[tk])
                        ys[which] = (t, tk)
                    (yg, ygk), (yv, yvk) = ys["g"], ys["v"]
                    ACT(yg[:], yg[:], AF.Silu, r=[ygk], w=[ygk])
                    TT("pool", actT[:, f, :], yg[:], yv[:], ALU.mult, r=[ygk, yvk], w=["actT"])
                for dh in range(2):
                    for f in range(NFF):
                        wb, wbk = wdb.next()
                        DMA(wb[:], wdnb_s[f * 128:(f + 1) * 128, dh * 1024:(dh + 1) * 1024], r=[f"wdnb_{f}"], w=[wbk], key=wbk)
                        for si, (so, sn) in enumerate(subs):
                            for dt in range(2):
                                bi_ = si * 2 + dt
                                PE(bank(bi_)[0:sn, :], actT[:, f, so:so + sn], wb[:, dt * 512:(dt + 1) * 512],
                                   start=(f == 0), stop=(f == NFF - 1), r=["actT", wbk], w=[f"pb{bi_}"])
                    for si, (so, sn) in enumerate(subs):
                        tk0 = t0 + so
                        xm, xmk = xm5.next()
                        DMA(xm[0:sn, :], xmid_s[tk0:tk0 + sn, dh * 1024:(dh + 1) * 1024], r=xmkeys, w=[xmk], key=xmk)
                        for dt in range(2):
                            bi_ = si * 2 + dt
                            dsl = slice(dh * 1024 + dt * 512, dh * 1024 + (dt + 1) * 512)
                            o, ok = o5.next()
                            TT("dve", o[0:sn, :], bank(bi_)[0:sn, :], gate_f[0:sn, dsl], ALU.mult, r=[f"pb{bi_}", "gate1"], w=[ok])
                            TT("pool", o[0:sn, :], o[0:sn, :], xm[0:sn, dt * 512:(dt + 1) * 512], ALU.add, r=[ok, xmk], w=[ok])
                            skip = max(0, P0 - tk0)
                            if skip < sn:
                                DMA(y_d[tk0 + skip - P0:tk0 + sn - P0, dsl], o[skip:sn, :], r=[ok], w=[f"y_{ft}_{si}_{dh}_{dt}"], key=ok)
            S.barrier()
        if upto <= 4:
            finish_partial()
            return nc
        S.emit()
    return nc


def _t5_bucket(n):
    n = np.maximum(n, 0)
    nf = np.maximum(n, 1).astype(np.float32)
    large = 16 + (np.log(nf / 16) / math.log(8) * 16).astype(np.int32)
    large = np.minimum(large, 31)
    return np.where(n < 16, n, large)


def host_maps(inp, SEQ):
    f32 = lambda a: np.ascontiguousarray(np.asarray(a, dtype=np.float32))
    x = f32(inp["x"])
    B = x.shape[0]
    P0 = SEQ // 2
    colk = lambda v: f32(np.asarray(v).reshape(-1, 128).T)
    rel_bias = f32(inp["rel_bias"])
    kk_, qq_ = np.meshgrid(np.arange(128), np.arange(128), indexing="ij")
    bd = rel_bias[_t5_bucket(qq_ - kk_)]
    bp = rel_bias[_t5_bucket(128 + qq_ - kk_)]
    maskd = np.where(qq_ >= kk_, 0.0, NEG).astype(np.float32)
    mS = np.tril(np.ones((64, 64), np.float32), -1)
    mI = np.tril(np.ones((64, 64), np.float32))
    rmasks = np.concatenate([-mS.T, mI.T, mS.T, mI.T, -mS], axis=1)
    blk = np.zeros((128, 128), np.float32)
    blk[:64, :64] = 1
    blk[64:, 64:] = 1
    common = {
        "w_ada": f32(inp["w_ada"][0]), "b_ada": f32(inp["b_ada"][0]).reshape(1, -1),
        "gcols": f32(np.stack([colk(inp["norm_mix_g"][0]), colk(inp["norm_ffn_g"][0])], -1)),
        "w_in": f32(inp["w_in"][0]),
        "w_l": f32(np.concatenate([inp["w1"][0], inp["a1"][0], inp["g1"][0]], 1)),
        "mu_wag": f32(np.stack([colk(inp["mu_wag"][0][j]) for j in range(3)], -1)),
        "mu_rkv": f32(np.stack([colk(inp["mu_rkv"][0][j]) for j in range(3)], -1)),
        "rwvec": f32(np.stack([colk(inp[k][0]) for k in ("w0", "a0", "k_k", "k_a", "r_k")], -1)),
        "ln64": f32(np.stack([np.asarray(inp["ln_x_g"][0]).reshape(16, 64).T, np.asarray(inp["ln_x_b"][0]).reshape(16, 64).T], -1)),
        "wa2": f32(np.concatenate([inp["w2"][0], inp["a2"][0]], 0)),
        "g2": f32(inp["g2"][0]),
        "qkg": f32(np.stack([np.tile(inp["q_norm_g"][0], 2), np.tile(inp["k_norm_g"][0], 2), inp["attn_subln_g"][0]], -1)),
        "lamrow": f32(np.concatenate([inp["lambda_q1"][0], inp["lambda_k1"][0], inp["lambda_q2"][0], inp["lambda_k2"][0]])).reshape(1, 256),
        "biasd": f32(np.transpose(bd, (0, 2, 1))), "biasp": f32(np.transpose(bp, (0, 2, 1))),
        "farc": f32(np.tile(rel_bias[31][None, :], (128, 1))),
        "maskd": maskd,
        "w_out": f32(inp["w_out"][0]), "w_up": f32(inp["w_up"][0]), "w_down": f32(inp["w_down"][0]),
        "convw": f32(np.transpose(np.asarray(inp["conv_w"][0]).reshape(3, 88, 128), (2, 1, 0))),
        "convb": colk(inp["conv_b"][0]),
        "identf": np.eye(128, dtype=np.float32), "identb": np.eye(128).astype(ml_dtypes.bfloat16),
        "blkones": blk, "rmasks": f32(rmasks),
    }
    maps = []
    for b in range(B):
        for half in range(2):
            m = dict(common)
            if half == 0:
                m["x"] = np.concatenate([np.zeros((P0, D), np.float32), x[b, :P0]], 0)
            else:
                m["x"] = x[b]
            m["c_col"] = f32(np.asarray(inp["c"][b]).reshape(128, 16))
            fl = np.zeros((128, 2), np.float32)
            fl[:, 0] = float(half)
            fl[:, 1] = 0.0 if half else NEG
            m["flagv"] = fl
            maps.append(m)
    return maps


_NC_CACHE = {}


def kernel(**inputs):
    x = np.asarray(inputs["x"])
    B, SEQ = x.shape[0], x.shape[1]
    if SEQ not in _NC_CACHE:
        _NC_CACHE[SEQ] = build_program(SEQ)
    nc = _NC_CACHE[SEQ]
    maps = host_maps(inputs, SEQ)
    res = run_bass_kernel_spmd(nc, maps, core_ids=list(range(len(maps))))
    P0 = SEQ // 2
    out = np.zeros((B, SEQ, D), np.float32)
    for b in range(B):
        for half in range(2):
            out[b, half * P0:(half + 1) * P0] = res.results[b * 2 + half]["y"]
    return out
```

```python
import math
import contextlib
import numpy as np
import ml_dtypes
import concourse.bass as bass
import concourse.mybir as mybir
from concourse.bass_utils import run_bass_kernel_spmd

F32 = mybir.dt.float32
BF16 = mybir.dt.bfloat16
AF = mybir.ActivationFunctionType
ALU = mybir.AluOpType
AX = mybir.AxisListType

SEM_LIMIT = 30000
D = 2048
KC = 16
DFF = 5632
NFF = 44
NEG = -30000.0


class Sched:
    ENGS = ("pe", "act", "dve", "pool", "sp")

    def __init__(self, nc):
        self.nc = nc
        self.ops = []
        self.by_eng = {e: [] for e in self.ENGS}
        self.last_writer = {}
        self.readers = {}
        self.since_barrier = []
        self.stage = 0

    def op(self, eng, fn, reads=(), writes=(), dma=None, extra_deps=()):
        idx = len(self.ops)
        deps = set(extra_deps)
        raw = set()
        for r in reads:
            w = self.last_writer.get(r)
            if w is not None:
                deps.add(w)
                raw.add(w)
        for r in writes:
            w = self.last_writer.get(r)
            if w is not None:
                deps.add(w)
            for rd in self.readers.get(r, ()):
                deps.add(rd)
        deps.discard(idx)
        self.ops.append(dict(eng=eng, fn=fn, deps=deps, raw=raw, dma=dma, stage=self.stage,
                             pos=len(self.by_eng[eng]), force=bool(extra_deps)))
        self.by_eng[eng].append(idx)
        for r in writes:
            self.last_writer[r] = idx
            self.readers[r] = []
        for r in reads:
            if r not in writes:
                self.readers.setdefault(r, []).append(idx)
        if dma is not None:
            self.since_barrier.append(idx)
        return idx

    def barrier(self):
        last = [self.by_eng[e][-1] for e in self.ENGS if self.by_eng[e]]
        deps = set(last) | set(self.since_barrier)
        self.since_barrier = []
        for e in self.ENGS:
            self.op(e, lambda eng: eng.nop(), extra_deps=deps)
        self.stage += 1

    def emit(self):
        nc = self.nc
        ops = self.ops
        need = [[] for _ in ops]
        has_dep = [False] * len(ops)
        for i, o in enumerate(ops):
            for d in sorted(o["deps"]):
                p = ops[d]
                if p["dma"] is None and o["dma"] is None and p["eng"] == o["eng"] and not o["force"]:
                    if p["eng"] == "pe":
                        continue
                    if d in o["raw"] and o["pos"] - p["pos"] <= 2:
                        need[i].append(d)
                        has_dep[d] = True
                    continue
                if p["dma"] is None and p["eng"] == o["eng"] and o["force"]:
                    continue
                need[i].append(d)
                has_dep[d] = True
        eng_count = {e: 0 for e in self.ENGS}
        dma_count = {}
        slot_of = {}
        n_in_stage = {}
        for i, o in enumerate(ops):
            if o["dma"] is not None:
                sk_ = (o["stage"], o["dma"])
                if sk_ not in slot_of:
                    n_in_stage[o["stage"]] = n_in_stage.get(o["stage"], 0) + 1
                    slot_of[sk_] = n_in_stage[o["stage"]] - 1
                k = slot_of[sk_]
                c = dma_count.get(k, 0)
                per = SEM_LIMIT // 16
                o["sem"] = ("d", k, c // per)
                o["val"] = 16 * (c % per + 1)
                dma_count[k] = c + 1
            elif has_dep[i]:
                c = eng_count[o["eng"]]
                o["sem"] = ("e", o["eng"], c // SEM_LIMIT)
                o["val"] = c % SEM_LIMIT + 1
                eng_count[o["eng"]] = c + 1
        names = sorted({o["sem"] for o in ops if "sem" in o}, key=str)
        sems = {}
        for n_, sn in enumerate(names):
            sems[sn] = nc.alloc_semaphore(name=f"sm{n_}")
        self.n_sems = len(sems)

        def run_engine(eng_name):
            def body(e):
                known = {}
                for i in self.by_eng[eng_name]:
                    o = ops[i]
                    w = {}
                    for d in need[i]:
                        p = ops[d]
                        s, v = p["sem"], p["val"]
                        if w.get(s, 0) < v:
                            w[s] = v
                    for s, v in w.items():
                        if known.get(s, 0) >= v:
                            continue
                        e.wait_ge(sems[s], v)
                        known[s] = v
                    ins = o["fn"](e)
                    if "sem" in o:
                        ins.then_inc(sems[o["sem"]], 16 if o["dma"] is not None else 1)
            return body

        with nc.Block() as block:
            block.tensor(run_engine("pe"))
            block.scalar(run_engine("act"))
            block.vector(run_engine("dve"))
            block.gpsimd(run_engine("pool"))
            block.sync(run_engine("sp"))


class Ring:
    def __init__(self, name, tiles):
        self.name = name
        self.tiles = tiles
        self.i = 0

    def next(self):
        k = self.i % len(self.tiles)
        self.i += 1
        return self.tiles[k], f"{self.name}{k}"


def build_program(L, dbg=False, upto=9, cut=99, rwkv_only=False):
    nc = bass.Bass("TRN2", target_bir_lowering=False)
    S = Sched(nc)
    P0 = L // 2
    NT = 512
    NTILE = L // NT
    NBLK = L // 128
    MIXB = P0 // 128 - 1
    FULLT = (MIXB * 128) // NT
    NCH = L // 64
    MIXCH = MIXB * 2
    TOK0 = MIXB * 128
    NMIX = L - TOK0
    FFN0 = P0 - 2
    NFT = next(n for n in range(1, 64) if (L - FFN0) % n == 0 and (L - FFN0) // n <= 512)
    FT = (L - FFN0) // NFT

    if rwkv_only:
        MIXCH = 0

    def din(name, shape, dt=F32):
        if rwkv_only and name in ("x", "w_ada", "w_in", "w_out", "w_up", "w_down", "w_l"):
            return None
        return nc.dram_tensor(name, list(shape), dt, kind="ExternalInput").ap()

    def dscr(name, shape, dt):
        if rwkv_only and name[:-2] in ("kap", "bt", "kt", "rt", "bh", "kh", "vT", "WC", "bon", "g"):
            return nc.dram_tensor(name, list(shape), dt, kind="ExternalInput").ap()
        return nc.dram_tensor(name, list(shape), dt, kind="ExternalOutput" if dbg else "Internal").ap()

    x_d = din("x", [L, D])
    ccol_d = din("c_col", [128, 16])
    flag_d = din("flagv", [128, 2])
    wada_d = din("w_ada", [D, 6 * D])
    bada_d = din("b_ada", [1, 6 * D])
    gcol_d = din("gcols", [128, 16, 2])
    win_d = din("w_in", [D, 6144])
    wl_d = din("w_l", [D, 288])
    muw_d = din("mu_wag", [128, 16, 3])
    mur_d = din("mu_rkv", [128, 8, 3])
    rwv_d = din("rwvec", [128, 8, 5])
    ln64_d = din("ln64", [64, 16, 2])
    wa2_d = din("wa2", [128, 1024])
    g2_d = din("g2", [160, 1024])
    qkg_d = din("qkg", [128, 3])
    lam_d = din("lamrow", [1, 256])
    biasd_d = din("biasd", [128, 8, 128])
    biasp_d = din("biasp", [128, 8, 128])
    farc_d = din("farc", [128, 8])
    maskd_d = din("maskd", [128, 128])
    wout_d = din("w_out", [D, D])
    wup_d = din("w_up", [D, 2 * DFF])
    wdn_d = din("w_down", [DFF, D])
    convw_d = din("convw", [128, 88, 3])
    convb_d = din("convb", [128, 88])
    identf_d = din("identf", [128, 128])
    identb_d = din("identb", [128, 128], BF16)
    blk_d = din("blkones", [128, 128])
    rmask_d = din("rmasks", [64, 320])
    y_d = nc.dram_tensor("y", [P0, D], F32, kind="ExternalOutput").ap()

    qT_s = dscr("qT_s", [8, 128, L], BF16)
    kT_s = dscr("kT_s", [8, 128, L], BF16)
    v_s = dscr("v_s", [L, 1024], BF16)
    RW = ["kap", "bt", "kt", "rt", "bh", "kh", "vT"]
    rw_s = {n: dscr(n + "_s", [1024, L], BF16) for n in RW}
    WC_s = dscr("WC_s", [1024, NCH], F32)
    bon_s = dscr("bon_s", [1024, L], F32)
    g_s = dscr("g_s", [1024, L], F32)
    oT_s = dscr("oT_s", [D, L], BF16)
    xmid_s = dscr("xmid_s", [L, D], F32)
    h2T_s = dscr("h2T_s", [D, L], BF16)
    winb_s = nc.dram_tensor("winb_s", [48, 128, 2048], BF16, kind="Internal").ap()
    wupb_s = nc.dram_tensor("wupb_s", [88, 128, 2048], BF16, kind="Internal").ap()
    wdnb_s = nc.dram_tensor("wdnb_s", [DFF, D], BF16, kind="Internal").ap()
    woutb_s = nc.dram_tensor("woutb_s", [D, D], BF16, kind="Internal").ap()

    es = contextlib.ExitStack()
    with es:
        def T(name, shape, dt):
            return es.enter_context(nc.sbuf_tensor(name, list(shape), dt))

        psum = es.enter_context(nc.psum_tensor("ps", [128, 4096], F32))

        def bank(b, n=1):
            return psum[:, b * 512:(b + n) * 512]

        def PE(out, lhsT, rhs, start=True, stop=True, r=(), w=(), skip=False):
            if skip:
                S.op("pe", lambda e: e.matmul(out, lhsT=lhsT, rhs=rhs, start=start, stop=stop, skip_group_check=True), reads=r, writes=w)
            else:
                S.op("pe", lambda e: e.matmul(out, lhsT=lhsT, rhs=rhs, start=start, stop=stop), reads=r, writes=w)

        def PET(out, in_, ident, r=(), w=()):
            S.op("pe", lambda e: e.transpose(out, in_, ident), reads=r, writes=w)

        def ACT(out, in_, func, r=(), w=(), bias=None, scale=None, accum=None):
            kw = {}
            if bias is not None:
                kw["bias"] = bias
            if scale is not None:
                kw["scale"] = scale
            if accum is not None:
                kw["accum_out"] = accum
            S.op("act", lambda e: e.activation(out=out, in_=in_, func=func, **kw), reads=r, writes=w)

        def TT(eng, out, in0, in1, op, r=(), w=()):
            S.op(eng, lambda e: e.tensor_tensor(out=out, in0=in0, in1=in1, op=op), reads=r, writes=w)

        def TS(eng, out, in0, s1, op0, s2=None, op1=None, r=(), w=()):
            if op1 is None:
                S.op(eng, lambda e: e.tensor_scalar(out=out, in0=in0, scalar1=s1, scalar2=None, op0=op0), reads=r, writes=w)
            else:
                S.op(eng, lambda e: e.tensor_scalar(out=out, in0=in0, scalar1=s1, scalar2=s2, op0=op0, op1=op1), reads=r, writes=w)

        def STT(out, in0, scalar, in1, op0, op1, r=(), w=()):
            S.op("dve", lambda e: e.scalar_tensor_tensor(out=out, in0=in0, scalar=scalar, in1=in1, op0=op0, op1=op1), reads=r, writes=w)

        def CP(eng, out, in_, r=(), w=()):
            if eng == "act":
                ACT(out, in_, AF.Copy, r=r, w=w)
            else:
                S.op(eng, lambda e: e.tensor_copy(out=out, in_=in_), reads=r, writes=w)

        def MEMSET(eng, ap, val, w=()):
            S.op(eng, lambda e: e.memset(ap, val), writes=w)

        def DMA(out, in_, r=(), w=(), key="d"):
            S.op("sp", lambda e: e.dma_start(out=out, in_=in_), reads=r, writes=w, dma=key)

        def finish_partial():
            zz = T("zz", [128, D], F32)
            MEMSET("pool", zz[:], 0.0, w=["zz"])
            for i in range(P0 // 128):
                DMA(y_d[i * 128:(i + 1) * 128, :], zz[:], r=["zz"], w=[f"y{i}"], key="yz")
            S.barrier()
            S.emit()

        identF = T("identF", [128, 128], F32)
        identB = T("identB", [128, 128], BF16)
        blkF = T("blkF", [128, 128], F32)
        blkB = T("blkB", [128, 128], BF16)
        ones64 = T("ones64", [64, 64], F32)
        onesrow = T("onesrow", [1, 128], F32)
        flagv = T("flagv_s", [128, 2], F32)
        gcols = T("gcols_s", [128, 16, 2], F32)
        modcol = T("modcol", [128, 96], F32)
        nrm = T("nrm", [128, 16, 8], F32)
        gate_m = T("gate_m", [128, D], F32)
        gate_f = T("gate_f", [128, D], F32)
        mur = T("mur", [128, 8, 3], F32)
        omur = T("omur", [128, 8, 3], F32)
        rwv = T("rwv", [128, 8, 5], F32)
        ln64 = T("ln64_s", [64, 16, 2], F32)
        qkg = T("qkg_s", [128, 3], F32)
        qkgs = T("qkgs", [128, 3], F32)
        neglam = T("neglam", [128, 1], F32)
        farc = T("farc_s", [128, 8], F32)
        farcp = T("farcp", [128, 8], F32)
        Wla = T("Wla", [128, 16, 288], BF16)
        Wlb = T("Wlb", [128, 16, 288], BF16)
        wa2b = T("wa2b", [128, 1024], BF16)
        g2b = T("g2b", [128, 1024], BF16)
        g2c = T("g2c", [32, 1024], BF16)
        rmask = T("rmask", [64, 320], F32)
        carry = T("carry", [128, 32], F32)

        DMA(identF[:], identf_d, w=["identF"], key="c0")
        DMA(identB[:], identb_d, w=["identB"], key="c1")
        DMA(blkF[:], blk_d, w=["blkF"], key="c2")
        DMA(flagv[:], flag_d, w=["flagv"], key="c3")
        DMA(gcols[:], gcol_d, w=["gcols"], key="c4")
        DMA(mur[:], mur_d, w=["mur"], key="c5")
        DMA(rwv[:], rwv_d, w=["rwv"], key="c6")
        DMA(ln64[:], ln64_d, w=["ln64"], key="c7")
        DMA(qkg[:], qkg_d, w=["qkg"], key="c8")
        DMA(farc[:], farc_d, w=["farc"], key="c9")
        DMA(rmask[:], rmask_d, w=["rmask"], key="c10")
        CP("dve", blkB[:], blkF[:], r=["blkF"], w=["blkB"])
        MEMSET("pool", ones64[:], 1.0 / 64.0, w=["ones64"])
        MEMSET("pool", onesrow[:], 1.0, w=["onesrow"])
        MEMSET("pool", carry[:], 0.0, w=["carry"])
        TS("dve", omur[:], mur[:], -1.0, ALU.mult, 1.0, ALU.add, r=["mur"], w=["omur"])
        TS("dve", qkgs[:, 0:1], qkg[:, 0:1], 0.125, ALU.mult, r=["qkg"], w=["qkgs"])
        CP("dve", qkgs[:, 1:2], qkg[:, 1:2], r=["qkg", "qkgs"], w=["qkgs"])
        TS("dve", qkgs[:, 2:3], qkg[:, 2:3], 0.8, ALU.mult, r=["qkg", "qkgs"], w=["qkgs"])
        TS("dve", farcp[:], farc[:], flagv[:, 1:2], ALU.add, r=["farc", "flagv"], w=["farcp"])

        with (contextlib.ExitStack() if not rwkv_only else contextlib.nullcontext()) as st:
          if not rwkv_only:
              def T0(name, shape, dt):
                  return st.enter_context(nc.sbuf_tensor(name, list(shape), dt))
              ccol = T0("ccol", [128, 16], F32)
              cact = T0("cact", [128, 16], F32)
              modrow = T0("modrow", [1, 6 * D], F32)
              lamr = T0("lamr", [1, 256], F32)
              lamw = T0("lamw", [1, 136], F32)
              wst = [T0(f"wst{i}", [128, 16, 256], F32) for i in range(2)]
              wlf = T0("wlf", [128, 16, 288], F32)
              muw = T0("muw", [128, 16, 3], F32)
              omuw = T0("omuw", [128, 16, 3], F32)
              wa2f = T0("wa2f", [128, 1024], F32)
              g2f = T0("g2f", [128, 1024], F32)
              g2cf = T0("g2cf", [32, 1024], F32)
              one11 = T0("one11", [1, 1], F32)
              pcf = Ring("pcf", [T0(f"pcf{i}", [128, 16, 128], F32) for i in range(2)])
              pcb = Ring("pcb", [T0(f"pcb{i}", [128, 16, 128], BF16) for i in range(2)])
              winv0 = win_d.rearrange("(kc p) n -> p kc n", p=128)

              DMA(ccol[:], ccol_d, w=["ccol"], key="c0")
              DMA(modrow[:], bada_d, w=["modrow"], key="c1")
              DMA(lamr[:], lam_d, w=["lamr"], key="c2")
              DMA(wlf[:], wl_d.rearrange("(kc p) n -> p kc n", p=128), w=["wlf"], key="c3")
              DMA(muw[:], muw_d, w=["muw"], key="c4")
              DMA(wa2f[:], wa2_d, w=["wa2f"], key="c5")
              DMA(g2f[:], g2_d[0:128, :], w=["g2f"], key="c6")
              DMA(g2cf[:], g2_d[128:160, :], w=["g2cf"], key="c7")
              MEMSET("pool", one11[:], 1.0, w=["one11"])
              ACT(cact[:], ccol[:], AF.Silu, r=["ccol"], w=["cact"])
              wav = wada_d.rearrange("(p k) n -> p k n", k=16)
              for ct in range(48):
                  ws, wk = wst[ct % 2], f"wst{ct % 2}"
                  DMA(ws[:], wav[:, :, ct * 256:(ct + 1) * 256], w=[wk], key=wk)
                  pb = ct % 2
                  for k in range(16):
                      PE(bank(pb)[0:1, 0:256], cact[:, k:k + 1], ws[:, k, :], start=(k == 0), stop=(k == 15),
                         r=["cact", wk], w=[f"pb{pb}"])
                  TT("dve", modrow[0:1, ct * 256:(ct + 1) * 256], bank(pb)[0:1, 0:256], modrow[0:1, ct * 256:(ct + 1) * 256],
                     ALU.add, r=[f"pb{pb}", "modrow"], w=["modrow"])
                  pf, pfk = pcf.next()
                  DMA(pf[:], winv0[:, :, ct * 128:(ct + 1) * 128], w=[pfk], key=pfk)
                  pbt, pbk = pcb.next()
                  CP("pool", pbt[:], pf[:], r=[pfk], w=[pbk])
                  DMA(winb_s[ct], pbt[:].rearrange("p k n -> p (k n)"), r=[pbk], w=[f"winb_{ct}"], key=pbk)
              for j in range(96):
                  PE(bank(2)[:, j:j + 1], modrow[0:1, j * 128:(j + 1) * 128], one11[0:1, 0:1],
                     r=["modrow", "one11"], w=["pb2"])
              CP("dve", modcol[:], bank(2)[:, 0:96], r=["pb2"], w=["modcol"])
              for gi, (gt, off) in enumerate(((gate_m, 2 * D), (gate_f, 5 * D))):
                  for dt in range(4):
                      pb = 3 + (gi * 4 + dt) % 4
                      PE(bank(pb), onesrow[0:1, :], modrow[0:1, off + dt * 512: off + (dt + 1) * 512],
                         r=["modrow", "onesrow"], w=[f"pb{pb}"])
                      CP("act", gt[:, dt * 512:(dt + 1) * 512], bank(pb), r=[f"pb{pb}"], w=[f"gate{gi}"])
              for si, (gi, sc0, sh0) in enumerate(((0, 16, 0), (1, 64, 48))):
                  b4 = si * 4
                  STT(nrm[:, :, b4 + 0], modcol[:, sc0:sc0 + 16], 1.0, gcols[:, :, gi], ALU.add, ALU.mult,
                      r=["modcol", "gcols"], w=["nrm"])
                  CP("dve", nrm[:, :, b4 + 1], modcol[:, sh0:sh0 + 16], r=["modcol", "nrm"], w=["nrm"])
                  TS("dve", nrm[:, :, b4 + 2], nrm[:, :, b4 + 0], flagv[:, 0:1], ALU.mult, r=["nrm", "flagv"], w=["nrm"])
                  TS("dve", nrm[:, :, b4 + 3], nrm[:, :, b4 + 1], flagv[:, 0:1], ALU.mult, r=["nrm", "flagv"], w=["nrm"])
              TT("dve", lamw[0:1, 0:64], lamr[0:1, 0:64], lamr[0:1, 64:128], ALU.mult, r=["lamr"], w=["lamw"])
              TT("dve", lamw[0:1, 64:128], lamr[0:1, 128:192], lamr[0:1, 192:256], ALU.mult, r=["lamr", "lamw"], w=["lamw"])
              S.op("dve", lambda e: e.tensor_reduce(out=lamw[0:1, 128:130], in_=lamw[0:1, 0:128].rearrange("p (a b) -> p a b", a=2),
                                                    axis=AX.X, op=ALU.add), reads=["lamw"], writes=["lamw2"])
              ACT(lamw[0:1, 130:132], lamw[0:1, 128:130], AF.Exp, r=["lamw2"], w=["lamw3"])
              TT("dve", lamw[0:1, 132:133], lamw[0:1, 131:132], lamw[0:1, 130:131], ALU.subtract, r=["lamw3"], w=["lamw4"])
              TS("dve", lamw[0:1, 133:134], lamw[0:1, 132:133], -0.2, ALU.add, r=["lamw4"], w=["lamw5"])
              PE(bank(7)[:, 0:1], onesrow[0:1, :], lamw[0:1, 133:134], r=["lamw5", "onesrow"], w=["pb7"])
              CP("dve", neglam[:], bank(7)[:, 0:1], r=["pb7"], w=["neglam"])
              TS("dve", omuw[:], muw[:], -1.0, ALU.mult, 1.0, ALU.add, r=["muw"], w=["omuw"])
              for kc in range(16):
                  for (c0, c1, j) in ((0, 64, 0), (64, 128, 1), (128, 288, 2)):
                      TS("dve", Wla[:, kc, c0:c1], wlf[:, kc, c0:c1], omuw[:, kc, j:j + 1], ALU.mult, r=["wlf", "omuw"], w=["Wla"])
                      TS("pool", Wlb[:, kc, c0:c1], wlf[:, kc, c0:c1], muw[:, kc, j:j + 1], ALU.mult, r=["wlf", "muw"], w=["Wlb"])
              CP("pool", wa2b[:], wa2f[:], r=["wa2f"], w=["wa2b"])
              CP("pool", g2b[:], g2f[:], r=["g2f"], w=["g2b"])
              CP("pool", g2c[:], g2cf[:], r=["g2cf"], w=["g2c"])
              S.barrier()

        with (contextlib.ExitStack() if not rwkv_only else contextlib.nullcontext()) as st:
          if not rwkv_only:
              def T1(name, shape, dt):
                  return st.enter_context(nc.sbuf_tensor(name, list(shape), dt))
              xring = Ring("xs", [T1(f"xs{i}", [128, D], F32) for i in range(2)])
              xnring = Ring("xn", [T1(f"xn{i}", [128, D], F32) for i in range(1)])
              stat = Ring("stat", [T1(f"stat{i}", [128, 4], F32) for i in range(4)])
              hTs = [T1(f"hT{i}", [128, 16, NT], BF16) for i in range(1)]
              wbr = Ring("wb", [T1(f"wb{i}", [128, 16, 128], BF16) for i in range(4)])
              wkF = Ring("wkF", [T1(f"wkF{i}", [128, 516], F32) for i in range(34)])
              mixR = Ring("mixR", [T1(f"mixR{i}", [128, 516], F32) for i in range(6)])
              wkB = Ring("wkB", [T1(f"wkB{i}", [128, 512], BF16) for i in range(10)])
              rnR = Ring("rnR", [T1(f"rnR{i}", [128, NT], F32) for i in range(2)])
              sqR = Ring("sqR", [T1(f"sqR{i}", [128, NT], BF16) for i in range(3)])
              hidA = T1("hidA", [128, NT], BF16)
              hidB = T1("hidB", [128, NT], BF16)
              hidC = T1("hidC", [32, NT], BF16)
              onesT = T1("onesT", [128, NT], F32)
              baseT = Ring("base", [T1(f"base{i}", [128, 8], F32) for i in range(3)])
              wcT = Ring("wc", [T1(f"wc{i}", [128, 8], F32) for i in range(3)])
              MEMSET("pool", onesT[:], 1.0, w=["onesT"])
              PSR = Ring("pb", [None] * 6)
              PSA = Ring("pa", [None] * 2)

              def nbank():
                  _, k = PSR.next()
                  return bank(int(k[2:])), k

              def abank():
                  _, k = PSA.next()
                  i_ = 6 + int(k[2:])
                  return bank(i_), f"pb{i_}"

              def load_w(c0):
                  wb, wbk = wbr.next()
                  DMA(wb[:].rearrange("p k n -> p (k n)"), winb_s[c0 // 128], r=[f"winb_{c0 // 128}"], w=[wbk], key=wbk)
                  return wb, wbk

              def proj(wb_ap, wbk, hT, hk, ncol=128, alloc=None):
                  pb, pk = (alloc or nbank)()
                  for kc in range(16):
                      PE(pb[0:ncol, :], wb_ap(kc), hT[:, kc, :], start=(kc == 0), stop=(kc == 15), r=[wbk] + hk, w=[pk])
                  return pb, pk

              cidx = [0]

              def shifted(pb, pk, ci):
                  t, tk = wkF.next()
                  ACT(t[:, 1:NT + 1], pb, AF.Copy, r=[pk], w=[tk])
                  CP("pool", t[:, 0:1], carry[:, ci:ci + 1], r=["carry%d" % ci, tk], w=[tk])
                  CP("pool", carry[:, ci:ci + 1], t[:, NT:NT + 1], r=[tk], w=["carry%d" % ci])
                  return t, tk

              for ti in range(NTILE if cut > 0 else 0):
                  kvonly = ti < FULLT
                  prefix = (ti * NT) < P0
                  hT, hk0 = hTs[0], "hT0"
                  hk = [hk0 + "d", hk0 + "a"]
                  nb = 2 if prefix else 0
                  for bi in range(NT // 128):
                      tok = ti * NT + bi * 128
                      xs, xk = xring.next()
                      DMA(xs[:], x_d[tok:tok + 128, :], w=[xk], key=xk)
                      sv, sk = stat.next()
                      xn, xnk = xnring.next()
                      ACT(xn[:], xs[:], AF.Square, r=[xk], w=[xnk, sk], accum=sv[:, 0:1])
                      ACT(sv[:, 1:2], sv[:, 0:1], AF.Ln, r=[sk], w=[sk + "b"], scale=1.0 / D, bias=1e-6)
                      ACT(sv[:, 2:3], sv[:, 1:2], AF.Exp, r=[sk + "b"], w=[sk + "c"], scale=-0.5)
                      TS("dve", xn[:], xs[:], sv[:, 2:3], ALU.mult, r=[xk, sk + "c"], w=[xnk])
                      for kg in range(4):
                          pb, pk = nbank()
                          for j in range(4):
                              kc = kg * 4 + j
                              PET(pb[:, j * 128:(j + 1) * 128], xn[:, kc * 128:(kc + 1) * 128], identF[:], r=[xnk, "identF"], w=[pk])
                          for j in range(4):
                              kc = kg * 4 + j
                              TS("dve", hT[:, kc, bi * 128:(bi + 1) * 128], pb[:, j * 128:(j + 1) * 128],
                                 nrm[:, kc, nb:nb + 1], ALU.mult, nrm[:, kc, nb + 1:nb + 2], ALU.add, r=[pk, "nrm"], w=[hk0 + "d"])
                  if cut <= 1:
                      continue
                  ci = 0
                  for (c0, nc_, hid, chunk) in ((0, 128, hidA, "A"), (128, 128, hidB, "B"), (256, 32, hidC, "C")):
                      if kvonly and chunk != "A":
                          ci += 1
                          continue
                      pa, pak = proj(lambda kc, c0=c0, nc_=nc_: Wla[:, kc, c0:c0 + nc_], "Wla", hT, hk, nc_)
                      pb_, pbk = proj(lambda kc, c0=c0, nc_=nc_: Wlb[:, kc, c0:c0 + nc_], "Wlb", hT, hk, nc_)
                      t, tk = wkF.next()
                      ACT(t[0:nc_, 1:NT + 1], pb_[0:nc_, :], AF.Copy, r=[pbk], w=[tk])
                      CP("pool", t[0:nc_, 0:1], carry[0:nc_, ci:ci + 1], r=["carry%d" % ci, tk], w=[tk])
                      CP("pool", carry[0:nc_, ci:ci + 1], t[0:nc_, NT:NT + 1], r=[tk], w=["carry%d" % ci])
                      u, uk = wkF.next()
                      TT("dve", u[0:nc_, 0:NT], pa[0:nc_, :], t[0:nc_, 0:NT], ALU.add, r=[pak, tk], w=[uk])
                      if chunk == "A":
                          ACT(hid[0:64, :], u[0:64, 0:NT], AF.Tanh, r=[uk], w=["hidA"])
                          ACT(hid[64:128, :], u[64:128, 0:NT], AF.Copy, r=[uk, "hidA"], w=["hidA"])
                      else:
                          ACT(hid[0:nc_, :], u[0:nc_, 0:NT], AF.Sigmoid, r=[uk], w=["hid" + chunk])
                      ci += 1
                  def rwA(fg):
                      f0 = fg * 128
                      cb = 3 + fg * 3
                      pk_ = {}
                      for wi, col0 in enumerate((3072, 4096, 5120)):
                          if kvonly and wi == 0:
                              continue
                          wb, wbk = load_w(col0 + f0)
                          pbx, pkx = proj(lambda kc, wb=wb: wb[:, kc, :], wbk, hT, hk)
                          m1, m1k = wkF.next()
                          ACT(m1[:, 0:NT], pbx, AF.Copy, r=[pkx, "omur"], w=[m1k], scale=omur[:, fg, wi:wi + 1])
                          t, tk = shifted(pbx, pkx, cb + wi)
                          m2, m2k = mixR.next()
                          STT(m2[:, 0:NT], t[:, 0:NT], mur[:, fg, wi:wi + 1], m1[:, 0:NT], ALU.mult, ALU.add, r=[tk, m1k, "mur"], w=[m2k])
                          pk_[wi] = (m2, m2k)
                      return pk_

                  def rwB(fg, pk_):
                      f0 = fg * 128
                      tsl = slice(ti * NT, (ti + 1) * NT)
                      pw, pwk = nbank()
                      PE(pw, wa2b[0:64, f0:f0 + 128], hidA[0:64, :], r=["wa2b", "hidA"], w=[pwk])
                      ld, ldk = wkF.next()
                      ACT(ld[:, 0:NT], pw, AF.Sigmoid, r=[pwk, "rwv"], w=[ldk], bias=rwv[:, fg, 0:1])
                      pa, pak = nbank()
                      PE(pa, wa2b[64:128, f0:f0 + 128], hidA[64:128, :], r=["wa2b", "hidA"], w=[pak])
                      av, avk = wkF.next()
                      ACT(av[:, 0:NT], pa, AF.Sigmoid, r=[pak, "rwv"], w=[avk], bias=rwv[:, fg, 1:2])
                      yield
                      ACT(ld[:, 0:NT], ld[:, 0:NT], AF.Copy, r=[ldk], w=[ldk], scale=-math.exp(-0.5))
                      kx, kxk = pk_[1]
                      vx, vxk = pk_[2]
                      k0, k0k = wkF.next()
                      ACT(k0[:, 0:NT], kx[:, 0:NT], AF.Copy, r=[kxk, "rwv"], w=[k0k], scale=rwv[:, fg, 2:3])
                      yield
                      Lr, Lrk = wkF.next()
                      S.op("dve", lambda e, Lr=Lr, ld=ld: e.tensor_tensor_scan(out=Lr[:, 0:NT], data0=onesT[:], data1=ld[:, 0:NT],
                                                                                 initial=0.0, op0=ALU.mult, op1=ALU.add),
                           reads=[ldk, "onesT"], writes=[Lrk])
                      sq, sqk = wkB.next()
                      ACT(sq[:], k0[:, 0:NT], AF.Square, r=[k0k], w=[sqk])
                      pss, pssk = nbank()
                      PE(pss, blkB[:], sq[:], r=["blkB", sqk], w=[pssk])
                      rn, rnk = wkF.next()
                      TS("dve", rn[:, 0:NT], pss, 1e-18, ALU.max, r=[pssk], w=[rnk])
                      yield
                      bs, bsk = baseT.next()
                      MEMSET("pool", bs[:, 0:1], 0.0, w=[bsk])
                      Lr3 = Lr[:, 0:NT].rearrange("p (c t) -> p c t", t=64)
                      CP("pool", bs[:, 1:8], Lr3[:, 0:7, 63], r=[Lrk, bsk], w=[bsk])
                      ACT(rn[:, 0:NT], rn[:, 0:NT], AF.Ln, r=[rnk], w=[rnk])
                      yield
                      TT("dve", Lr3, Lr3, bs[:, 0:8].unsqueeze(2).to_broadcast([128, 8, 64]), ALU.subtract, r=[Lrk, bsk], w=[Lrk])
                      ACT(rn[:, 0:NT], rn[:, 0:NT], AF.Exp, r=[rnk], w=[rnk], scale=-0.5)
                      yield
                      Wt, Wtk = wkF.next()
                      ACT(Wt[:, 0:NT], Lr[:, 0:NT], AF.Exp, r=[Lrk], w=[Wtk])
                      Wm, Wmk = wkF.next()
                      TT("pool", Wm[:, 0:NT], Lr[:, 0:NT], ld[:, 0:NT], ALU.subtract, r=[Lrk, ldk], w=[Wmk])
                      kn, knk = wkF.next()
                      TT("dve", kn[:, 0:NT], k0[:, 0:NT], rn[:, 0:NT], ALU.mult, r=[k0k, rnk], w=[knk])
                      yield
                      Wi, Wik = wkF.next()
                      ACT(Wi[:, 0:NT], Lr[:, 0:NT], AF.Exp, r=[Lrk], w=[Wik], scale=-1.0)
                      Wr, Wrk = wkF.next()
                      Wr3 = Wr[:, 0:NT].rearrange("p (c t) -> p c t", t=64)
                      TT("pool", Wr3, Lr3, Lr3[:, :, 63:64].to_broadcast([128, 8, 64]), ALU.subtract, r=[Lrk], w=[Wrk])
                      k2, k2k = wkF.next()
                      TS("dve", k2[:, 0:NT], av[:, 0:NT], -1.0, ALU.add, rwv[:, fg, 3:4], ALU.mult, r=[avk, "rwv"], w=[k2k])
                      yield
                      ACT(Wm[:, 0:NT], Wm[:, 0:NT], AF.Exp, r=[Wmk], w=[Wmk])
                      bb, bbk = wkF.next()
                      TT("pool", bb[:, 0:NT], kn[:, 0:NT], av[:, 0:NT], ALU.mult, r=[knk, avk], w=[bbk])
                      TT("dve", k2[:, 0:NT], k2[:, 0:NT], kx[:, 0:NT], ALU.mult, r=[k2k, kxk], w=[k2k])
                      yield
                      ACT(Wr[:, 0:NT], Wr[:, 0:NT], AF.Exp, r=[Wrk], w=[Wrk], scale=-1.0)
                      TT("pool", k2[:, 0:NT], k2[:, 0:NT], kx[:, 0:NT], ALU.add, r=[k2k, kxk], w=[k2k])
                      wc, wck = wcT.next()
                      Wt3 = Wt[:, 0:NT].rearrange("p (c t) -> p c t", t=64)
                      CP("pool", wc[:, 0:8], Wt3[:, :, 63], r=[Wtk], w=[wck])
                      DMA(WC_s[f0:f0 + 128, ti * 8:(ti + 1) * 8], wc[:, 0:8], r=[wck], w=[f"WC_{ti}_{fg}"], key=wck)
                      yield
                      outs = [("kap", kn, knk, Wm, Wmk, "dve"), ("bt", bb, bbk, Wi, Wik, "pool"), ("kt", k2, k2k, Wi, Wik, "dve"),
                              ("bh", bb, bbk, Wr, Wrk, "pool"), ("kh", k2, k2k, Wr, Wrk, "dve")]
                      if not kvonly:
                          rx, rxk = pk_[0]
                          outs.append(("rt", rx, rxk, Wt, Wtk, "pool"))
                      for (nm, a_, ak, b_, bk, eng) in outs:
                          o, ok = wkB.next()
                          TT(eng, o[:], a_[:, 0:NT], b_[:, 0:NT], ALU.mult, r=[ak, bk], w=[ok])
                          DMA(rw_s[nm][f0:f0 + 128, tsl], o[:], r=[ok], w=[f"{nm}_{ti}_{fg}"], key=ok)
                          yield
                      o, ok = wkB.next()
                      CP("pool", o[:], vx[:, 0:NT], r=[vxk], w=[ok])
                      DMA(rw_s["vT"][f0:f0 + 128, tsl], o[:], r=[ok], w=[f"vT_{ti}_{fg}"], key=ok)
                      if not kvonly:
                          rx, rxk = pk_[0]
                          bq, bqk = wkF.next()
                          STT(bq[:, 0:NT], rx[:, 0:NT], rwv[:, fg, 4:5], k2[:, 0:NT], ALU.mult, ALU.mult, r=[rxk, k2k, "rwv"], w=[bqk])
                          pbn, pbnk = nbank()
                          PE(pbn, blkF[:], bq[:, 0:NT], r=["blkF", bqk], w=[pbnk])
                          bo, bok = wkF.next()
                          TT("dve", bo[:, 0:NT], pbn, vx[:, 0:NT], ALU.mult, r=[pbnk, vxk], w=[bok])
                          DMA(bon_s[f0:f0 + 128, tsl], bo[:, 0:NT], r=[bok], w=[f"bon_{ti}_{fg}"], key=bok)
                          yield
                          pg, pgk = nbank()
                          PE(pg, g2b[:, f0:f0 + 128], hidB[:], start=True, stop=False, r=["g2b", "hidB"], w=[pgk])
                          PE(pg, g2c[:, f0:f0 + 128], hidC[:], start=False, stop=True, r=["g2c", "hidC"], w=[pgk])
                          go, gok = wkF.next()
                          ACT(go[:, 0:NT], pg, AF.Copy, r=[pgk], w=[gok])
                          DMA(g_s[f0:f0 + 128, tsl], go[:, 0:NT], r=[gok], w=[f"g_{ti}_{fg}"], key=gok)
                      yield

                  def qkA(h, which, col0, dst, gi):
                      wb, wbk = load_w(col0 + h * 128)
                      pbx, pkx = proj(lambda kc, wb=wb: wb[:, kc, :], wbk, hT, hk, alloc=abank)
                      sq, sqk = sqR.next()
                      ACT(sq[:], pbx, AF.Square, r=[pkx], w=[sqk])
                      return (pbx, pkx, sq, sqk)

                  def qkB(h, which, col0, dst, gi, pbx, pkx, sq, sqk):
                      pss, pssk = nbank()
                      PE(pss, blkB[:], sq[:], r=["blkB", sqk], w=[pssk])
                      rn, rnk = rnR.next()
                      ACT(rn[:, 0:NT], pss, AF.Ln, r=[pssk], w=[rnk], scale=1.0 / 64.0, bias=1e-6)
                      ACT(rn[:, 0:NT], rn[:, 0:NT], AF.Exp, r=[rnk], w=[rnk], scale=-0.5)
                      o, ok = wkB.next()
                      STT(o[:], pbx, qkgs[:, gi:gi + 1], rn[:, 0:NT], ALU.mult, ALU.mult, r=[pkx, rnk, "qkgs"], w=[ok])
                      DMA(dst[h, :, ti * NT:(ti + 1) * NT], o[:], r=[ok], w=[f"{which}T_{ti}_{h}"], key=ok)

                  def vA(h):
                      wb, wbk = load_w(2048 + h * 128)
                      pbx, pkx = proj(lambda kc, wb=wb: wb[:, kc, :], wbk, hT, hk, alloc=abank)
                      vb, vbk = sqR.next()
                      ACT(vb[:], pbx, AF.Copy, r=[pkx], w=[vbk])
                      return (vb, vbk)

                  def vB(h, vb, vbk):
                      pt, ptk = nbank()
                      ptb = pt.bitcast(BF16)
                      for bi in range(4):
                          PET(ptb[:, bi * 128:(bi + 1) * 128], vb[:, bi * 128:(bi + 1) * 128], identB[:], r=[vbk, "identB"], w=[ptk])
                      vo, vok = wkB.next()
                      CP("dve", vo[:], ptb[:, 0:512], r=[ptk], w=[vok])
                      DMA(v_s[ti * NT:(ti + 1) * NT, h * 128:(h + 1) * 128].rearrange("(b p) v -> p b v", p=128),
                          vo[:].rearrange("p (b v) -> p b v", b=4), r=[vok], w=[f"v_{ti}_{h}"], key=vok)

                  def attn_gen():
                      items = []
                      for h in range(8 if cut > 3 else 0):
                          items.append((qkA, qkB, (h, "k", 1024, kT_s, 1)))
                          if not kvonly:
                              items.append((qkA, qkB, (h, "q", 0, qT_s, 0)))
                          items.append((vA, vB, (h,)))
                      pendA = None
                      for (fa, fb, args) in items:
                          resA = fa(*args)
                          if pendA is not None:
                              pfb, pargs, pres = pendA
                              pfb(*pargs, *pres)
                          pendA = (fb, args, resA)
                          yield
                      if pendA is not None:
                          pfb, pargs, pres = pendA
                          pfb(*pargs, *pres)
                      yield

                  ag = attn_gen()
                  ag_live = True
                  nfg = 8 if cut > 2 else 0
                  for fp_ in range(0, nfg, 2):
                      gens = [rwB(fg, rwA(fg)) for fg in (fp_, fp_ + 1)]
                      rounds = 0
                      while gens:
                          for gen in list(gens):
                              try:
                                  next(gen)
                              except StopIteration:
                                  gens.remove(gen)
                          rounds += 1
                          if ag_live and rounds % 3 == 0:
                              try:
                                  next(ag)
                              except StopIteration:
                                  ag_live = False
                  while ag_live:
                      try:
                          next(ag)
                      except StopIteration:
                          ag_live = False
              S.barrier()

        if upto >= 2:
          with contextlib.ExitStack() as st:
            def T2(name, shape, dt):
                return st.enter_context(nc.sbuf_tensor(name, list(shape), dt))
            OPN = ["kap", "bt", "kt", "rt", "bh", "kh", "vT"]
            opr = Ring("opr", [{n: T2(f"op{i}_{n}", [64, 8, 64], BF16) for n in OPN} for i in range(4)])
            bgr = Ring("bgr", [(T2(f"bon{i}", [64, 8, 64], F32), T2(f"gg{i}", [64, 8, 64], F32)) for i in range(3)])
            I8 = T2("I8", [64, 8, 64], F32)
            WCall = T2("WCall", [64, 16, NCH], F32)
            ZF = [T2(f"ZF{g}", [64, 8, 64], F32) for g in range(2)]
            Zb = [T2(f"Zb{g}", [64, 8, 64], BF16) for g in range(2)]
            b16 = lambda nm, n: Ring(nm, [T2(f"{nm}{i}", [64, 8, 64], BF16) for i in range(n)])
            f32r = lambda nm, n: Ring(nm, [T2(f"{nm}{i}", [64, 8, 64], F32) for i in range(n)])
            Nr, NTr, Xbr = b16("Nr", 8), b16("NTr", 8), b16("Xbr", 8)
            ArbR, AukR, ArkR, TTr = b16("ArbR", 2), b16("AukR", 2), b16("ArkR", 2), b16("TTr", 2)
            BtR, KtR, VtR = b16("BtR", 2), b16("KtR", 2), b16("VtR", 2)
            RhR, UbR, OutR = b16("RhR", 2), b16("UbR", 2), b16("OutR", 3)
            yFr = f32r("yFr", 20)
            id64 = identB[0:64, 0:64]
            mST_n = rmask[:, 0:64].unsqueeze(1).to_broadcast([64, 8, 64])
            mIT = rmask[:, 64:128].unsqueeze(1).to_broadcast([64, 8, 64])
            mST = rmask[:, 128:192].unsqueeze(1).to_broadcast([64, 8, 64])
            mS_n = rmask[:, 256:320].unsqueeze(1).to_broadcast([64, 8, 64])
            for h in range(8):
                CP("pool", I8[:, h, :], identF[0:64, 0:64], r=["identF"], w=["I8"])
            DMA(WCall[:], WC_s.rearrange("(h d) c -> d h c", d=64),
                r=[f"WC_{ti}_{fg}" for ti in range(NTILE) for fg in range(8)], w=["WCall"], key="wcall")
            for g in range(2):
                MEMSET("pool", ZF[g][:], 0.0, w=[f"ZF{g}"])
                MEMSET("pool", Zb[g][:], 0.0, w=[f"Zb{g}"])
            PSR2 = Ring("pb", [None] * 6)

            def nb2():
                _, k = PSR2.next()
                return bank(int(k[2:]))[0:64, :].rearrange("p (h t) -> p h t", h=8), k

            def body(c, g):
                own = c >= MIXCH
                ti = (c * 64) // NT
                tsl = slice(c * 64, (c + 1) * 64)
                ops_, opk = opr.next()
                for n in OPN:
                    if n == "rt" and not own:
                        continue
                    srcs = [f"{n}_{ti}_{fg}" for fg in range(g * 4, g * 4 + 4)]
                    DMA(ops_[n][:], rw_s[n][g * 512:(g + 1) * 512, tsl].rearrange("(h d) t -> d h t", d=64),
                        r=srcs, w=[opk + n], key=opk + n)
                K_ = lambda n: opk + n
                kap, bt, kt, rt, bh, kh, vT = [ops_[n] for n in OPN]
                psx_i = 6 + g
                PSX = bank(psx_i)[0:64, :].rearrange("p (h t) -> p h t", h=8)
                PSXk = f"pb{psx_i}"
                if own:
                    (bo, gg), bgk = bgr.next()
                    DMA(bo[:], bon_s[g * 512:(g + 1) * 512, tsl].rearrange("(h d) t -> d h t", d=64),
                        r=[f"bon_{ti}_{fg}" for fg in range(g * 4, g * 4 + 4)], w=[bgk + "b"], key=bgk + "b")
                    DMA(gg[:], g_s[g * 512:(g + 1) * 512, tsl].rearrange("(h d) t -> d h t", d=64),
                        r=[f"g_{ti}_{fg}" for fg in range(g * 4, g * 4 + 4)], w=[bgk + "g"], key=bgk + "g")

                def mm8(ps, psk, lhs, lk, rhs, rk, start=True, stop=True):
                    for h in range(8):
                        PE(ps[:, h, :], lhs[:, h, :], rhs[:, h, :], start=(start and h == 0), stop=stop, r=[lk, rk], w=[psk], skip=True)

                p1, p1k = nb2()
                mm8(p1, p1k, bt, K_("bt"), kap, K_("kap"))
                NT0, NT0k = NTr.next()
                TT("dve", NT0[:], p1, mST_n, ALU.mult, r=[p1k, "rmask"], w=[NT0k])
                yield
                p3, p3k = nb2()
                mm8(p3, p3k, kap, K_("kap"), bt, K_("bt"))
                N0, N0k = Nr.next()
                TT("dve", N0[:], p3, mS_n, ALU.mult, r=[p3k, "rmask"], w=[N0k])
                X1, X1k = Xbr.next()
                TT("pool", X1[:], NT0[:], I8[:], ALU.add, r=[NT0k, "I8"], w=[X1k])
                yield
                p2, p2k = nb2()
                mm8(p2, p2k, kt, K_("kt"), kap, K_("kap"))
                Auk, Aukk = AukR.next()
                TT("dve", Auk[:], p2, mST, ALU.mult, r=[p2k, "rmask"], w=[Aukk])
                yield
                if own:
                    p4, p4k = nb2()
                    mm8(p4, p4k, bt, K_("bt"), rt, K_("rt"))
                    Arb, Arbk = ArbR.next()
                    TT("dve", Arb[:], p4, mIT, ALU.mult, r=[p4k, "rmask"], w=[Arbk])
                    yield
                    p5, p5k = nb2()
                    mm8(p5, p5k, kt, K_("kt"), rt, K_("rt"))
                    Ark, Arkk = ArkR.next()
                    TT("dve", Ark[:], p5, mIT, ALU.mult, r=[p5k, "rmask"], w=[Arkk])
                    yield
                for h in range(8):
                    PE(PSX[:, h, :], id64, X1[:, h, :], start=(h == 0), stop=True, r=["identB", X1k], w=[PSXk], skip=True)
                Ncur, Nk_, NTcur, NTk_ = N0, N0k, NT0, NT0k
                Xb, Xbk = X1, X1k
                tm = {}
                tlist = [(bh, K_("bh"), BtR), (kh, K_("kh"), KtR), (vT, K_("vT"), VtR)]
                for m in range(1, 6):
                    pa, pak = nb2()
                    mm8(pa, pak, NTcur, NTk_, Ncur, Nk_)
                    Nn, Nnk = Nr.next()
                    CP("act", Nn[:], pa, r=[pak], w=[Nnk])
                    if m <= 4:
                        pb_, pbk = nb2()
                        mm8(pb_, pbk, Ncur, Nk_, NTcur, NTk_)
                        NTn, NTnk = NTr.next()
                        CP("dve", NTn[:], pb_, r=[pbk], w=[NTnk])
                    if m >= 2:
                        Xb, Xbk = Xbr.next()
                        CP("act", Xb[:], PSX, r=[PSXk], w=[Xbk])
                    if tlist:
                        (src, sk_, ring_) = tlist.pop(0)
                        pt, ptk = nb2()
                        ptb = bank(int(ptk[2:]))[0:64, :].bitcast(BF16)[:, 0:512].rearrange("p (h t) -> p h t", h=8)
                        for h in range(8):
                            PET(ptb[:, h, :], src[:, h, :], id64, r=[sk_, "identB"], w=[ptk])
                        dst, dk_ = ring_.next()
                        CP("act", dst[:], ptb, r=[ptk], w=[dk_])
                        tm[sk_] = (dst, dk_)
                    yield
                    for h in range(8):
                        S.op("pe", lambda e, h=h, Nn=Nn, Xb=Xb, PSX=PSX: e.matmul(PSX[:, h, :], lhsT=Nn[:, h, :], rhs=Xb[:, h, :], start=False, stop=True, skip_group_check=True),
                             reads=[Nnk, Xbk], writes=[PSXk])
                    Ncur, Nk_ = Nn, Nnk
                    if m <= 4:
                        NTcur, NTk_ = NTn, NTnk
                    yield
                Bt_, Btk = tm[K_("bh")]
                Kt_, Ktk = tm[K_("kh")]
                Vt_, Vtk = tm[K_("vT")]
                TTt, TTk = TTr.next()
                CP("act", TTt[:], PSX, r=[PSXk], w=[TTk])
                yield
                Zk = f"Zb{g}"
                pr, prk = nb2()
                mm8(pr, prk, kap, K_("kap"), Zb[g], Zk, start=True, stop=False)
                mm8(pr, prk, Auk, Aukk, Vt_, Vtk, start=False, stop=True)
                Rh, Rhk = RhR.next()
                ACT(Rh[:], pr, AF.Copy, r=[prk], w=[Rhk], scale=-1.0)
                yield
                pu, puk = nb2()
                mm8(pu, puk, TTt, TTk, Rh, Rhk)
                Ub, Ubk = UbR.next()
                CP("dve", Ub[:], pu, r=[puk], w=[Ubk])
                yield
                if own:
                    py, pyk = nb2()
                    mm8(py, pyk, Zb[g], Zk, rt, K_("rt"), start=True, stop=False)
                    mm8(py, pyk, Ub, Ubk, Arb, Arbk, start=False, stop=False)
                    mm8(py, pyk, Vt_, Vtk, Ark, Arkk, start=False, stop=True)
                    yF, yFk = yFr.next()
                    ACT(yF[:], py, AF.Copy, r=[pyk], w=[yFk])
                    ysq, ysqk = yFr.next()
                    ACT(ysq[:], py, AF.Square, r=[pyk], w=[ysqk])
                    yield
                pz, pzk = nb2()
                mm8(pz, pzk, Bt_, Btk, Ub, Ubk, start=True, stop=False)
                mm8(pz, pzk, Kt_, Ktk, Vt_, Vtk, start=False, stop=True)
                TT("dve", ZF[g][:], ZF[g][:], WCall[:, g * 8:(g + 1) * 8, c:c + 1].to_broadcast([64, 8, 64]), ALU.mult,
                   r=[f"ZF{g}", "WCall"], w=[f"ZF{g}"])
                TT("dve", ZF[g][:], ZF[g][:], pz, ALU.add, r=[f"ZF{g}", pzk], w=[f"ZF{g}"])
                CP("pool", Zb[g][:], ZF[g][:], r=[f"ZF{g}"], w=[Zk])
                yield
                if own:
                    pm, pmk = nb2()
                    PE(bank(int(pmk[2:]))[0:64, :], ones64[:], yF[:].rearrange("p h t -> p (h t)"), r=["ones64", yFk], w=[pmk])
                    mean, meank = yFr.next()
                    ACT(mean[:], pm, AF.Copy, r=[pmk], w=[meank])
                    msq, msqk = yFr.next()
                    ACT(msq[:], pm, AF.Square, r=[pmk], w=[msqk])
                    yield
                    pq, pqk = nb2()
                    PE(bank(int(pqk[2:]))[0:64, :], ones64[:], ysq[:].rearrange("p h t -> p (h t)"), r=["ones64", ysqk], w=[pqk])
                    var, vark = yFr.next()
                    TT("dve", var[:], pq, msq[:], ALU.subtract, r=[pqk, msqk], w=[vark])
                    yield
                    ACT(var[:], var[:], AF.Ln, r=[vark], w=[vark], bias=64e-5)
                    ACT(var[:], var[:], AF.Exp, r=[vark], w=[vark], scale=-0.5)
                    yc, yck = yFr.next()
                    TT("pool", yc[:], yF[:], mean[:], ALU.subtract, r=[yFk, meank], w=[yck])
                    TT("pool", yc[:], yc[:], var[:], ALU.mult, r=[yck, vark], w=[yck])
                    TT("pool", yc[:], yc[:], ln64[:, g * 8:(g + 1) * 8, 0:1].to_broadcast([64, 8, 64]), ALU.mult, r=[yck, "ln64"], w=[yck])
                    TT("pool", yc[:], yc[:], ln64[:, g * 8:(g + 1) * 8, 1:2].to_broadcast([64, 8, 64]), ALU.add, r=[yck, "ln64"], w=[yck])
                    TT("pool", yc[:], yc[:], bo[:], ALU.add, r=[yck, bgk + "b"], w=[yck])
                    oo, ook = OutR.next()
                    TT("pool", oo[:], yc[:], gg[:], ALU.mult, r=[yck, bgk + "g"], w=[ook])
                    DMA(oT_s[1024 + g * 512:1024 + (g + 1) * 512, tsl].rearrange("(h d) t -> d h t", d=64), oo[:],
                        r=[ook], w=[f"orw_{c}_{g}"], key=ook)
                    yield

            for c in range(NCH):
                gens = [body(c, 0), body(c, 1)]
                while gens:
                    for gen in list(gens):
                        try:
                            next(gen)
                        except StopIteration:
                            gens.remove(gen)
            S.barrier()
        if upto >= 3:
          with contextlib.ExitStack() as st:
            def T3(name, shape, dt):
                return st.enter_context(nc.sbuf_tensor(name, list(shape), dt))
            KT = [T3(f"KT{i}", [128, L], BF16) for i in range(2)]
            VH = [T3(f"VH{i}", [128, NBLK, 130], BF16) for i in range(2)]
            QT = [T3(f"QT{i}", [128, NMIX], BF16) for i in range(2)]
            bdall = T3("bdall", [128, 8, 128], F32)
            bpall = T3("bpall", [128, 8, 128], F32)
            maskd = T3("maskd_s", [128, 128], F32)
            PTr = Ring("PTr", [T3(f"PT{i}", [128, 512], BF16) for i in range(8)])
            tmpr = Ring("tmpr", [T3(f"tmp{i}", [128, 128], F32) for i in range(8)])
            o_r = Ring("o_r", [T3(f"of{i}", [128, 128], F32) for i in range(4)])
            ob_r = Ring("ob_r", [T3(f"ob{i}", [128, 128], BF16) for i in range(4)])
            ot_r = Ring("ot_r", [T3(f"ot{i}", [128, 128], BF16) for i in range(4)])
            st_r = Ring("st_r", [T3(f"ast{i}", [128, 8], F32) for i in range(6)])
            junk3 = T3("junk3", [128, 128], F32)
            pcf3 = Ring("pcf3", [T3(f"pcf3{i}", [128, D], F32) for i in range(2)])
            pcb3 = Ring("pcb3", [T3(f"pcb3{i}", [128, D], BF16) for i in range(2)])
            wupv3 = wup_d.rearrange("(kc p) n -> p kc n", p=128)
            woutv3 = wout_d.rearrange("(kc p) n -> p kc n", p=128)
            woutbv3 = woutb_s.rearrange("(kc p) n -> p kc n", p=128)
            precast = ([("up", c) for c in range(88)] + [("dn", f) for f in range(NFF)] + [("out", kc) for kc in range(16)])

            def do_precast(n):
                for _ in range(n):
                    if not precast:
                        return
                    kind, ix = precast.pop(0)
                    pf, pfk = pcf3.next()
                    pbt, pbk = pcb3.next()
                    if kind == "up":
                        DMA(pf[:].rearrange("p (k n) -> p k n", k=16), wupv3[:, :, ix * 128:(ix + 1) * 128], w=[pfk], key=pfk)
                    elif kind == "dn":
                        DMA(pf[:], wdn_d[ix * 128:(ix + 1) * 128, :], w=[pfk], key=pfk)
                    else:
                        DMA(pf[:], woutv3[:, ix, :], w=[pfk], key=pfk)
                    CP("pool", pbt[:], pf[:], r=[pfk], w=[pbk])
                    if kind == "up":
                        DMA(wupb_s[ix], pbt[:], r=[pbk], w=[f"wupb_{ix}"], key=pbk)
                    elif kind == "dn":
                        DMA(wdnb_s[ix * 128:(ix + 1) * 128, :], pbt[:], r=[pbk], w=[f"wdnb_{ix}"], key=pbk)
                    else:
                        DMA(woutbv3[:, ix, :], pbt[:], r=[pbk], w=[f"woutb_{ix}"], key=pbk)
            DMA(bdall[:], biasd_d, w=["bdall"], key="c0")
            DMA(bpall[:], biasp_d, w=["bpall"], key="c1")
            DMA(maskd[:], maskd_d, w=["maskd"], key="c2")
            TT("dve", bdall[:], bdall[:], maskd[:].unsqueeze(1).to_broadcast([128, 8, 128]), ALU.add, r=["bdall", "maskd"], w=["bdall"])
            for i in range(2):
                MEMSET("pool", VH[i][:, :, 128:130], 1.0, w=[f"VH{i}"])
            PB0 = P0 // 128
            qgroups = {}
            for qb in range(MIXB, NBLK):
                qgroups.setdefault(qb // 4, []).append(qb)
            SCR = Ring("pb", [None] * 8)

            def sc_bank():
                while True:
                    _, k = SCR.next()
                    if int(k[2:]) >= 4:
                        return bank(int(k[2:])), k

            def kvq_keys(h):
                return ([f"kT_{ti}_{h}" for ti in range(NTILE)], [f"v_{ti}_{h}" for ti in range(NTILE)],
                        [f"qT_{ti}_{h}" for ti in range(FULLT, NTILE)])

            for h in range(8):
                hb = h % 2
                kk_, vk_, qk_ = kvq_keys(h)
                DMA(KT[hb][:], kT_s[h], r=kk_, w=[f"KT{hb}"], key=f"KT{hb}")
                DMA(VH[hb][:, :, 0:128], v_s[:, h * 128:(h + 1) * 128].rearrange("(b p) v -> p b v", p=128), r=vk_, w=[f"VH{hb}"], key=f"VH{hb}")
                DMA(QT[hb][:], qT_s[h, :, TOK0:L], r=qk_, w=[f"QT{hb}"], key=f"QT{hb}")
                for gi, qbs in sorted(qgroups.items()):
                    qb0, qb1 = qbs[0], qbs[-1]
                    nqb = len(qbs)
                    do_precast(4)
                    for b_ in range(4):
                        S.op("dve", lambda e, b_=b_: e.memset(bank(b_), 0.0), writes=[f"pb{b_}"])

                    def score(i, j):
                        lo = max(j, qb0)
                        c0 = (lo - qb0) * 128
                        ncol = (qb1 - lo + 1) * 128
                        qc0 = lo * 128 - TOK0
                        sb, sbk = sc_bank()
                        PE(sb[:, c0:c0 + ncol], KT[hb][i * 64:(i + 1) * 64, j * 128:(j + 1) * 128],
                           QT[hb][i * 64:(i + 1) * 64, qc0:qc0 + ncol], r=[f"KT{hb}", f"QT{hb}"], w=[sbk])
                        PT, PTk = PTr.next()
                        pref = j < PB0
                        for qb in range(lo, qb1 + 1):
                            cs = slice((qb - qb0) * 128, (qb - qb0 + 1) * 128)
                            if qb - j >= 2:
                                continue
                            tmp, tmpk = tmpr.next()
                            btile = bdall if qb == j else bpall
                            TT("dve", tmp[:], sb[:, cs], btile[:, h, :], ALU.add, r=[sbk, "bdall", "bpall"], w=[tmpk])
                            if pref:
                                ACT(PT[:, cs], tmp[:], AF.Exp, r=[tmpk, "flagv"], w=[PTk], bias=flagv[:, 1:2])
                            else:
                                ACT(PT[:, cs], tmp[:], AF.Exp, r=[tmpk], w=[PTk])
                        far0 = max(j + 2, qb0)
                        if far0 <= qb1:
                            fs = slice((far0 - qb0) * 128, (qb1 - qb0 + 1) * 128)
                            ACT(PT[:, fs], sb[:, fs], AF.Exp, r=[sbk, "farcp", "farc"], w=[PTk],
                                bias=(farcp[:, h:h + 1] if pref else farc[:, h:h + 1]))
                        return (i, j, lo, PT, PTk)

                    def pv(i, j, lo, PT, PTk):
                        for qb in range(lo, qb1 + 1):
                            ql = qb - qb0
                            ob_ = i * 2 + ql // 2
                            oc = (ql % 2) * 256
                            PE(bank(ob_)[:, oc:oc + 129], PT[:, ql * 128:(ql + 1) * 128], VH[hb][:, j, 0:129],
                               start=False, stop=True, r=[PTk, f"VH{hb}"], w=[f"pb{ob_}"], skip=True)

                    pend = []
                    for i in range(2):
                        for j in range(qb1 + 1):
                            pend.append(score(i, j))
                            if len(pend) > 2:
                                pv(*pend.pop(0))
                    while pend:
                        pv(*pend.pop(0))
                    for qb in qbs:
                        ql = qb - qb0
                        O0 = bank(0 + ql // 2)[:, (ql % 2) * 256:(ql % 2) * 256 + 129]
                        O1 = bank(2 + ql // 2)[:, (ql % 2) * 256:(ql % 2) * 256 + 129]
                        k0_, k1_ = f"pb{ql // 2}", f"pb{2 + ql // 2}"
                        sv, svk = st_r.next()
                        TS("dve", sv[:, 0:1], O0[:, 128:129], 1e-30, ALU.add, r=[k0_], w=[svk])
                        TS("dve", sv[:, 1:2], O1[:, 128:129], 1e-30, ALU.add, r=[k1_, svk], w=[svk])
                        S.op("dve", lambda e, sv=sv: e.reciprocal(out=sv[:, 2:4], in_=sv[:, 0:2]), reads=[svk], writes=[svk + "r"])
                        TT("dve", sv[:, 4:5], sv[:, 3:4], neglam[:], ALU.mult, r=[svk + "r", "neglam"], w=[svk + "n"])
                        of, ofk = o_r.next()
                        ACT(of[:], O0[:, 0:128], AF.Copy, r=[k0_, svk + "r"], w=[ofk], scale=sv[:, 2:3])
                        STT(of[:], O1[:, 0:128], sv[:, 4:5], of[:], ALU.mult, ALU.add, r=[k1_, svk + "n", ofk], w=[ofk])
                        ACT(junk3[:], of[:], AF.Square, r=[ofk], w=["junk3", svk + "s"], accum=sv[:, 5:6])
                        ACT(sv[:, 6:7], sv[:, 5:6], AF.Ln, r=[svk + "s"], w=[svk + "l"], scale=1.0 / 128.0, bias=1e-6)
                        ACT(sv[:, 7:8], sv[:, 6:7], AF.Exp, r=[svk + "l"], w=[svk + "e"], scale=-0.5)
                        ob, obk = ob_r.next()
                        TS("dve", ob[:], of[:], sv[:, 7:8], ALU.mult, r=[ofk, svk + "e"], w=[obk])
                        tb, tbk = sc_bank()
                        tbb = tb.bitcast(BF16)
                        PET(tbb[:, 0:128], ob[:], identB[:], r=[obk, "identB"], w=[tbk])
                        ot, otk = ot_r.next()
                        TS("dve", ot[:], tbb[:, 0:128], qkgs[:, 2:3], ALU.mult, r=[tbk, "qkgs"], w=[otk])
                        DMA(oT_s[h * 128:(h + 1) * 128, qb * 128:(qb + 1) * 128], ot[:], r=[otk], w=[f"oat_{h}_{qb}"], key=otk)
            do_precast(1000)
            S.barrier()
        if upto >= 4:
          with contextlib.ExitStack() as st:
            def T4(name, shape, dt):
                return st.enter_context(nc.sbuf_tensor(name, list(shape), dt))
            wo = T4("wo", [128, 16, D], BF16)
            oTr = Ring("oTb", [T4(f"oTb{i}", [128, 16, 128], BF16) for i in range(2)])
            x4r = Ring("x4", [T4(f"x4{i}", [128, D], F32) for i in range(2)])
            xmr = Ring("xm", [T4(f"xm{i}", [128, D], F32) for i in range(2)])
            xq = T4("xq", [128, D], F32)
            h2r = Ring("h2b", [T4(f"h2b{i}", [128, 16, 128], BF16) for i in range(2)])
            s4r = Ring("s4", [T4(f"s4{i}", [128, 4], F32) for i in range(4)])
            wov = woutb_s.rearrange("(kc p) n -> p kc n", p=128)
            for kc in range(16):
                DMA(wo[:, kc, :], wov[:, kc, :], r=[f"woutb_{kc}"], w=["wo%d" % (kc % 2)], key="wo%d" % (kc % 2))
            PS4 = Ring("pb", [None] * 8)
            oall = [f"orw_{c}_{g}" for c in range(MIXCH, NCH) for g in range(2)]
            for qb in range(MIXB, NBLK):
                tsl = slice(qb * 128, (qb + 1) * 128)
                oT, oTk = oTr.next()
                DMA(oT[:], oT_s.rearrange("(kc p) t -> p kc t", p=128)[:, :, tsl],
                    r=[f"oat_{h}_{qb}" for h in range(8)] + [f"orw_{c}_{g}" for c in (2 * qb, 2 * qb + 1) for g in range(2)],
                    w=[oTk], key=oTk)
                x4, x4k = x4r.next()
                DMA(x4[:], x_d[tsl, :], w=[x4k], key=x4k)
                xm, xmk = xmr.next()
                for dt in range(4):
                    _, pk = PS4.next()
                    pb = bank(int(pk[2:]))
                    dsl = slice(dt * 512, (dt + 1) * 512)
                    for kc in range(16):
                        PE(pb, oT[:, kc, :], wo[:, kc, dsl], start=(kc == 0), stop=(kc == 15), r=[oTk, "wo0", "wo1"], w=[pk])
                    TT("dve", xm[:, dsl], pb, gate_m[:, dsl], ALU.mult, r=[pk, "gate0"], w=[xmk + "a"])
                    TT("pool", xm[:, dsl], xm[:, dsl], x4[:, dsl], ALU.add, r=[xmk + "a", x4k], w=[xmk])
                DMA(xmid_s[tsl, :], xm[:], r=[xmk], w=[f"xmid_{qb}"], key=xmk)
                sv, svk = s4r.next()
                ACT(xq[:], xm[:], AF.Square, r=[xmk], w=["xq", svk], accum=sv[:, 0:1])
                ACT(sv[:, 1:2], sv[:, 0:1], AF.Ln, r=[svk], w=[svk + "b"], scale=1.0 / D, bias=1e-6)
                ACT(sv[:, 2:3], sv[:, 1:2], AF.Exp, r=[svk + "b"], w=[svk + "c"], scale=-0.5)
                TS("dve", xq[:], xm[:], sv[:, 2:3], ALU.mult, r=[xmk, svk + "c", "xq"], w=["xq"])
                h2, h2k = h2r.next()
                nb = 6 if qb * 128 < P0 else 4
                for kg in range(4):
                    _, pk = PS4.next()
                    pb = bank(int(pk[2:]))
                    for j in range(4):
                        kc = kg * 4 + j
                        PET(pb[:, j * 128:(j + 1) * 128], xq[:, kc * 128:(kc + 1) * 128], identF[:], r=["xq", "identF"], w=[pk])
                    for j in range(4):
                        kc = kg * 4 + j
                        TS("dve", h2[:, kc, :], pb[:, j * 128:(j + 1) * 128], nrm[:, kc, nb:nb + 1], ALU.mult,
                           nrm[:, kc, nb + 1:nb + 2], ALU.add, r=[pk, "nrm"], w=[h2k])
                DMA(h2T_s.rearrange("(kc p) t -> p kc t", p=128)[:, :, tsl], h2[:], r=[h2k], w=[f"h2T_{qb}"], key=h2k)
            S.barrier()

        if upto >= 5:
          with contextlib.ExitStack() as st:
            def T5(name, shape, dt):
                return st.enter_context(nc.sbuf_tensor(name, list(shape), dt))
            h2t = T5("h2t", [128, 16, FT], BF16)
            actT = T5("actT", [128, NFF, FT], BF16)
            wbr5 = Ring("wb5", [T5(f"wb5{i}", [128, 16, 128], BF16) for i in range(6)])
            usr = Ring("us", [T5(f"us{i}", [128, FT + 2], F32) for i in range(4)])
            tr5 = Ring("t5", [T5(f"t5{i}", [128, FT], F32) for i in range(4)])
            wdb = Ring("wdb", [T5(f"wdb{i}", [128, 1024], BF16) for i in range(6)])
            xm5 = Ring("xm5", [T5(f"xm5{i}", [128, 1024], F32) for i in range(2)])
            o5 = Ring("o5", [T5(f"o5{i}", [128, 512], F32) for i in range(3)])
            cw = T5("cw", [128, 88, 3], F32)
            cbv = T5("cbv", [128, 88], F32)
            car5 = T5("car5", [128, 88, 2], F32)
            DMA(cw[:], convw_d, w=["cw"], key="c0")
            DMA(cbv[:], convb_d, w=["cbv"], key="c1")
            MEMSET("pool", car5[:], 0.0, w=["car5"])
            PS5 = Ring("pb", [None] * 8)
            h2keys = [f"h2T_{qb}" for qb in range(MIXB, NBLK)]
            xmkeys = [f"xmid_{qb}" for qb in range(MIXB, NBLK)]
            subs = []
            o_ = 0
            while o_ < FT:
                subs.append((o_, min(128, FT - o_)))
                o_ += 128
            assert len(subs) * 2 <= 8
            for ft in range(NFT):
                t0 = FFN0 + ft * FT
                DMA(h2t[:], h2T_s.rearrange("(kc p) t -> p kc t", p=128)[:, :, t0:t0 + FT], r=h2keys, w=["h2t"], key="h2t")
                for f in range(NFF):
                    ys = {}
                    for which, col0, ci_ in (("g", f * 128, f), ("v", DFF + f * 128, NFF + f)):
                        wb, wbk = wbr5.next()
                        DMA(wb[:].rearrange("p k n -> p (k n)"), wupb_s[col0 // 128], r=[f"wupb_{col0 // 128}"], w=[wbk], key=wbk)
                        _, pk = PS5.next()
                        pb = bank(int(pk[2:]))
                        for kc in range(16):
                            PE(pb[:, 0:FT], wb[:, kc, :], h2t[:, kc, :], start=(kc == 0), stop=(kc == 15), r=[wbk, "h2t"], w=[pk])
                        us, usk = usr.next()
                        ACT(us[:, 2:FT + 2], pb[:, 0:FT], AF.Copy, r=[pk], w=[usk])
                        CP("pool", us[:, 0:2], car5[:, ci_, :], r=[f"car5_{ci_}", usk], w=[usk])
                        CP("pool", car5[:, ci_, :], us[:, FT:FT + 2], r=[usk], w=[f"car5_{ci_}"])
                        t, tk = tr5.next()
                        TS("dve", t[:], us[:, 2:FT + 2], cw[:, ci_, 2:3], ALU.mult, cbv[:, ci_:ci_ + 1], ALU.add, r=[usk, "cw", "cbv"], w=[tk])
                        STT(t[:], us[:, 1:FT + 1], cw[:, ci_, 1:2], t[:], ALU.mult, ALU.add, r=[usk, "cw", tk], w=[tk])
                        STT(t[:], us[:, 0:FT], cw[:, ci_, 0:1], t[:], ALU.mult, ALU.add, r=[usk, "cw", tk], w=[tk])
                        ys[which] = (t, tk)
                    (yg, ygk), (yv, yvk) = ys["g"], ys["v"]
                    ACT(yg[:], yg[:], AF.Silu, r=[ygk], w=[ygk])
                    TT("pool", actT[:, f, :], yg[:], yv[:], ALU.mult, r=[ygk, yvk], w=["actT"])
                for dh in range(2):
                    for f in range(NFF):
                        wb, wbk = wdb.next()
                        DMA(wb[:], wdnb_s[f * 128:(f + 1) * 128, dh * 1024:(dh + 1) * 1024], r=[f"wdnb_{f}"], w=[wbk], key=wbk)
                        for si, (so, sn) in enumerate(subs):
                            for dt in range(2):
                                bi_ = si * 2 + dt
                                PE(bank(bi_)[0:sn, :], actT[:, f, so:so + sn], wb[:, dt * 512:(dt + 1) * 512],
                                   start=(f == 0), stop=(f == NFF - 1), r=["actT", wbk], w=[f"pb{bi_}"])
                    for si, (so, sn) in enumerate(subs):
                        tk0 = t0 + so
                        xm, xmk = xm5.next()
                        DMA(xm[0:sn, :], xmid_s[tk0:tk0 + sn, dh * 1024:(dh + 1) * 1024], r=xmkeys, w=[xmk], key=xmk)
                        for dt in range(2):
                            bi_ = si * 2 + dt
                            dsl = slice(dh * 1024 + dt * 512, dh * 1024 + (dt + 1) * 512)
                            o, ok = o5.next()
                            TT("dve", o[0:sn, :], bank(bi_)[0:sn, :], gate_f[0:sn, dsl], ALU.mult, r=[f"pb{bi_}", "gate1"], w=[ok])
                            TT("pool", o[0:sn, :], o[0:sn, :], xm[0:sn, dt * 512:(dt + 1) * 512], ALU.add, r=[ok, xmk], w=[ok])
                            skip = max(0, P0 - tk0)
                            if skip < sn:
                                DMA(y_d[tk0 + skip - P0:tk0 + sn - P0, dsl], o[skip:sn, :], r=[ok], w=[f"y_{ft}_{si}_{dh}_{dt}"], key=ok)
            S.barrier()
        if upto <= 4:
            finish_partial()
            return nc
        S.emit()
    return nc


def _t5_bucket(n):
    n = np.maximum(n, 0)
    nf = np.maximum(n, 1).astype(np.float32)
    large = 16 + (np.log(nf / 16) / math.log(8) * 16).astype(np.int32)
    large = np.minimum(large, 31)
    return np.where(n < 16, n, large)


def host_maps(inp, SEQ):
    f32 = lambda a: np.ascontiguousarray(np.asarray(a, dtype=np.float32))
    x = f32(inp["x"])
    B = x.shape[0]
    P0 = SEQ // 2
    colk = lambda v: f32(np.asarray(v).reshape(-1, 128).T)
    rel_bias = f32(inp["rel_bias"])
    kk_, qq_ = np.meshgrid(np.arange(128), np.arange(128), indexing="ij")
    bd = rel_bias[_t5_bucket(qq_ - kk_)]
    bp = rel_bias[_t5_bucket(128 + qq_ - kk_)]
    maskd = np.where(qq_ >= kk_, 0.0, NEG).astype(np.float32)
    mS = np.tril(np.ones((64, 64), np.float32), -1)
    mI = np.tril(np.ones((64, 64), np.float32))
    rmasks = np.concatenate([-mS.T, mI.T, mS.T, mI.T, -mS], axis=1)
    blk = np.zeros((128, 128), np.float32)
    blk[:64, :64] = 1
    blk[64:, 64:] = 1
    common = {
        "w_ada": f32(inp["w_ada"][0]), "b_ada": f32(inp["b_ada"][0]).reshape(1, -1),
        "gcols": f32(np.stack([colk(inp["norm_mix_g"][0]), colk(inp["norm_ffn_g"][0])], -1)),
        "w_in": f32(inp["w_in"][0]),
        "w_l": f32(np.concatenate([inp["w1"][0], inp["a1"][0], inp["g1"][0]], 1)),
        "mu_wag": f32(np.stack([colk(inp["mu_wag"][0][j]) for j in range(3)], -1)),
        "mu_rkv": f32(np.stack([colk(inp["mu_rkv"][0][j]) for j in range(3)], -1)),
        "rwvec": f32(np.stack([colk(inp[k][0]) for k in ("w0", "a0", "k_k", "k_a", "r_k")], -1)),
        "ln64": f32(np.stack([np.asarray(inp["ln_x_g"][0]).reshape(16, 64).T, np.asarray(inp["ln_x_b"][0]).reshape(16, 64).T], -1)),
        "wa2": f32(np.concatenate([inp["w2"][0], inp["a2"][0]], 0)),
        "g2": f32(inp["g2"][0]),
        "qkg": f32(np.stack([np.tile(inp["q_norm_g"][0], 2), np.tile(inp["k_norm_g"][0], 2), inp["attn_subln_g"][0]], -1)),
        "lamrow": f32(np.concatenate([inp["lambda_q1"][0], inp["lambda_k1"][0], inp["lambda_q2"][0], inp["lambda_k2"][0]])).reshape(1, 256),
        "biasd": f32(np.transpose(bd, (0, 2, 1))), "biasp": f32(np.transpose(bp, (0, 2, 1))),
        "farc": f32(np.tile(rel_bias[31][None, :], (128, 1))),
        "maskd": maskd,
        "w_out": f32(inp["w_out"][0]), "w_up": f32(inp["w_up"][0]), "w_down": f32(inp["w_down"][0]),
        "convw": f32(np.transpose(np.asarray(inp["conv_w"][0]).reshape(3, 88, 128), (2, 1, 0))),
        "convb": colk(inp["conv_b"][0]),
        "identf": np.eye(128, dtype=np.float32), "identb": np.eye(128).astype(ml_dtypes.bfloat16),
        "blkones": blk, "rmasks": f32(rmasks),
    }
    maps = []
    for b in range(B):
        for half in range(2):
            m = dict(common)
            if half == 0:
                m["x"] = np.concatenate([np.zeros((P0, D), np.float32), x[b, :P0]], 0)
            else:
                m["x"] = x[b]
            m["c_col"] = f32(np.asarray(inp["c"][b]).reshape(128, 16))
            fl = np.zeros((128, 2), np.float32)
            fl[:, 0] = float(half)
            fl[:, 1] = 0.0 if half else NEG
            m["flagv"] = fl
            maps.append(m)
    return maps


_NC_CACHE = {}


def kernel(**inputs):
    x = np.asarray(inputs["x"])
    B, SEQ = x.shape[0], x.shape[1]
    if SEQ not in _NC_CACHE:
        _NC_CACHE[SEQ] = build_program(SEQ)
    nc = _NC_CACHE[SEQ]
    maps = host_maps(inputs, SEQ)
    res = run_bass_kernel_spmd(nc, maps, core_ids=list(range(len(maps))))
    P0 = SEQ // 2
    out = np.zeros((B, SEQ, D), np.float32)
    for b in range(B):
        for half in range(2):
            out[b, half * P0:(half + 1) * P0] = res.results[b * 2 + half]["y"]
    return out
```

```python
import math
import contextlib
import numpy as np
import ml_dtypes
import concourse.bass as bass
import concourse.mybir as mybir
from concourse.bass_utils import run_bass_kernel_spmd

F32 = mybir.dt.float32
BF16 = mybir.dt.bfloat16
AF = mybir.ActivationFunctionType
ALU = mybir.AluOpType
AX = mybir.AxisListType

SEM_LIMIT = 30000
D = 2048
KC = 16
DFF = 5632
NFF = 44
NEG = -30000.0


class Sched:
    ENGS = ("pe", "act", "dve", "pool", "sp")

    def __init__(self, nc):
        self.nc = nc
        self.ops = []
        self.by_eng = {e: [] for e in self.ENGS}
        self.last_writer = {}
        self.readers = {}
        self.since_barrier = []
        self.stage = 0

    def op(self, eng, fn, reads=(), writes=(), dma=None, extra_deps=()):
        idx = len(self.ops)
        deps = set(extra_deps)
        raw = set()
        for r in reads:
            w = self.last_writer.get(r)
            if w is not None:
                deps.add(w)
                raw.add(w)
        for r in writes:
            w = self.last_writer.get(r)
            if w is not None:
                deps.add(w)
            for rd in self.readers.get(r, ()):
                deps.add(rd)
        deps.discard(idx)
        self.ops.append(dict(eng=eng, fn=fn, deps=deps, raw=raw, dma=dma, stage=self.stage,
                             pos=len(self.by_eng[eng]), force=bool(extra_deps)))
        self.by_eng[eng].append(idx)
        for r in writes:
            self.last_writer[r] = idx
            self.readers[r] = []
        for r in reads:
            if r not in writes:
                self.readers.setdefault(r, []).append(idx)
        if dma is not None:
            self.since_barrier.append(idx)
        return idx

    def barrier(self):
        last = [self.by_eng[e][-1] for e in self.ENGS if self.by_eng[e]]
        deps = set(last) | set(self.since_barrier)
        self.since_barrier = []
        for e in self.ENGS:
            self.op(e, lambda eng: eng.nop(), extra_deps=deps)
        self.stage += 1

    def emit(self):
        nc = self.nc
        ops = self.ops
        need = [[] for _ in ops]
        has_dep = [False] * len(ops)
        for i, o in enumerate(ops):
            for d in sorted(o["deps"]):
                p = ops[d]
                if p["dma"] is None and o["dma"] is None and p["eng"] == o["eng"] and not o["force"]:
                    if p["eng"] == "pe":
                        continue
                    if d in o["raw"] and o["pos"] - p["pos"] <= 2:
                        need[i].append(d)
                        has_dep[d] = True
                    continue
                if p["dma"] is None and p["eng"] == o["eng"] and o["force"]:
                    continue
                need[i].append(d)
                has_dep[d] = True
        eng_count = {e: 0 for e in self.ENGS}
        dma_count = {}
        slot_of = {}
        n_in_stage = {}
        for i, o in enumerate(ops):
            if o["dma"] is not None:
                sk_ = (o["stage"], o["dma"])
                if sk_ not in slot_of:
                    n_in_stage[o["stage"]] = n_in_stage.get(o["stage"], 0) + 1
                    slot_of[sk_] = n_in_stage[o["stage"]] - 1
                k = slot_of[sk_]
                c = dma_count.get(k, 0)
                per = SEM_LIMIT // 16
                o["sem"] = ("d", k, c // per)
                o["val"] = 16 * (c % per + 1)
                dma_count[k] = c + 1
            elif has_dep[i]:
                c = eng_count[o["eng"]]
                o["sem"] = ("e", o["eng"], c // SEM_LIMIT)
                o["val"] = c % SEM_LIMIT + 1
                eng_count[o["eng"]] = c + 1
        names = sorted({o["sem"] for o in ops if "sem" in o}, key=str)
        sems = {}
        for n_, sn in enumerate(names):
            sems[sn] = nc.alloc_semaphore(name=f"sm{n_}")
        self.n_sems = len(sems)

        def run_engine(eng_name):
            def body(e):
                known = {}
                for i in self.by_eng[eng_name]:
                    o = ops[i]
                    w = {}
                    for d in need[i]:
                        p = ops[d]
                        s, v = p["sem"], p["val"]
                        if w.get(s, 0) < v:
                            w[s] = v
                    for s, v in w.items():
                        if known.get(s, 0) >= v:
                            continue
                        e.wait_ge(sems[s], v)
                        known[s] = v
                    ins = o["fn"](e)
                    if "sem" in o:
                        ins.then_inc(sems[o["sem"]], 16 if o["dma"] is not None else 1)
            return body

        with nc.Block() as block:
            block.tensor(run_engine("pe"))
            block.scalar(run_engine("act"))
            block.vector(run_engine("dve"))
            block.gpsimd(run_engine("pool"))
            block.sync(run_engine("sp"))


class Ring:
    def __init__(self, name, tiles):
        self.name = name
        self.tiles = tiles
        self.i = 0

    def next(self):
        k = self.i % len(self.tiles)
        self.i += 1
        return self.tiles[k], f"{self.name}{k}"


def build_program(L, dbg=False, upto=9, cut=99, rwkv_only=False):
    nc = bass.Bass("TRN2", target_bir_lowering=False)
    S = Sched(nc)
    P0 = L // 2
    NT = 512
    NTILE = L // NT
    NBLK = L // 128
    MIXB = P0 // 128 - 1
    FULLT = (MIXB * 128) // NT
    NCH = L // 64
    MIXCH = MIXB * 2
    TOK0 = MIXB * 128
    NMIX = L - TOK0
    FFN0 = P0 - 2
    NFT = next(n for n in range(1, 64) if (L - FFN0) % n == 0 and (L - FFN0) // n <= 512)
    FT = (L - FFN0) // NFT

    if rwkv_only:
        MIXCH = 0

    def din(name, shape, dt=F32):
        if rwkv_only and name in ("x", "w_ada", "w_in", "w_out", "w_up", "w_down", "w_l"):
            return None
        return nc.dram_tensor(name, list(shape), dt, kind="ExternalInput").ap()

    def dscr(name, shape, dt):
        if rwkv_only and name[:-2] in ("kap", "bt", "kt", "rt", "bh", "kh", "vT", "WC", "bon", "g"):
            return nc.dram_tensor(name, list(shape), dt, kind="ExternalInput").ap()
        return nc.dram_tensor(name, list(shape), dt, kind="ExternalOutput" if dbg else "Internal").ap()

    x_d = din("x", [L, D])
    ccol_d = din("c_col", [128, 16])
    flag_d = din("flagv", [128, 2])
    wada_d = din("w_ada", [D, 6 * D])
    bada_d = din("b_ada", [1, 6 * D])
    gcol_d = din("gcols", [128, 16, 2])
    win_d = din("w_in", [D, 6144])
    wl_d = din("w_l", [D, 288])
    muw_d = din("mu_wag", [128, 16, 3])
    mur_d = din("mu_rkv", [128, 8, 3])
    rwv_d = din("rwvec", [128, 8, 5])
    ln64_d = din("ln64", [64, 16, 2])
    wa2_d = din("wa2", [128, 1024])
    g2_d = din("g2", [160, 1024])
    qkg_d = din("qkg", [128, 3])
    lam_d = din("lamrow", [1, 256])
    biasd_d = din("biasd", [128, 8, 128])
    biasp_d = din("biasp", [128, 8, 128])
    farc_d = din("farc", [128, 8])
    maskd_d = din("maskd", [128, 128])
    wout_d = din("w_out", [D, D])
    wup_d = din("w_up", [D, 2 * DFF])
    wdn_d = din("w_down", [DFF, D])
    convw_d = din("convw", [128, 88, 3])
    convb_d = din("convb", [128, 88])
    identf_d = din("identf", [128, 128])
    identb_d = din("identb", [128, 128], BF16)
    blk_d = din("blkones", [128, 128])
    rmask_d = din("rmasks", [64, 320])
    y_d = nc.dram_tensor("y", [P0, D], F32, kind="ExternalOutput").ap()

    qT_s = dscr("qT_s", [8, 128, L], BF16)
    kT_s = dscr("kT_s", [8, 128, L], BF16)
    v_s = dscr("v_s", [L, 1024], BF16)
    RW = ["kap", "bt", "kt", "rt", "bh", "kh", "vT"]
    rw_s = {n: dscr(n + "_s", [1024, L], BF16) for n in RW}
    WC_s = dscr("WC_s", [1024, NCH], F32)
    bon_s = dscr("bon_s", [1024, L], F32)
    g_s = dscr("g_s", [1024, L], F32)
    oT_s = dscr("oT_s", [D, L], BF16)
    xmid_s = dscr("xmid_s", [L, D], F32)
    h2T_s = dscr("h2T_s", [D, L], BF16)
    winb_s = nc.dram_tensor("winb_s", [48, 128, 2048], BF16, kind="Internal").ap()
    wupb_s = nc.dram_tensor("wupb_s", [88, 128, 2048], BF16, kind="Internal").ap()
    wdnb_s = nc.dram_tensor("wdnb_s", [DFF, D], BF16, kind="Internal").ap()
    woutb_s = nc.dram_tensor("woutb_s", [D, D], BF16, kind="Internal").ap()

    es = contextlib.ExitStack()
    with es:
        def T(name, shape, dt):
            return es.enter_context(nc.sbuf_tensor(name, list(shape), dt))

        psum = es.enter_context(nc.psum_tensor("ps", [128, 4096], F32))

        def bank(b, n=1):
            return psum[:, b * 512:(b + n) * 512]

        def PE(out, lhsT, rhs, start=True, stop=True, r=(), w=(), skip=False):
            if skip:
                S.op("pe", lambda e: e.matmul(out, lhsT=lhsT, rhs=rhs, start=start, stop=stop, skip_group_check=True), reads=r, writes=w)
            else:
                S.op("pe", lambda e: e.matmul(out, lhsT=lhsT, rhs=rhs, start=start, stop=stop), reads=r, writes=w)

        def PET(out, in_, ident, r=(), w=()):
            S.op("pe", lambda e: e.transpose(out, in_, ident), reads=r, writes=w)

        def ACT(out, in_, func, r=(), w=(), bias=None, scale=None, accum=None):
            kw = {}
            if bias is not None:
                kw["bias"] = bias
            if scale is not None:
                kw["scale"] = scale
            if accum is not None:
                kw["accum_out"] = accum
            S.op("act", lambda e: e.activation(out=out, in_=in_, func=func, **kw), reads=r, writes=w)

        def TT(eng, out, in0, in1, op, r=(), w=()):
            S.op(eng, lambda e: e.tensor_tensor(out=out, in0=in0, in1=in1, op=op), reads=r, writes=w)

        def TS(eng, out, in0, s1, op0, s2=None, op1=None, r=(), w=()):
            if op1 is None:
                S.op(eng, lambda e: e.tensor_scalar(out=out, in0=in0, scalar1=s1, scalar2=None, op0=op0), reads=r, writes=w)
            else:
                S.op(eng, lambda e: e.tensor_scalar(out=out, in0=in0, scalar1=s1, scalar2=s2, op0=op0, op1=op1), reads=r, writes=w)

        def STT(out, in0, scalar, in1, op0, op1, r=(), w=()):
            S.op("dve", lambda e: e.scalar_tensor_tensor(out=out, in0=in0, scalar=scalar, in1=in1, op0=op0, op1=op1), reads=r, writes=w)

        def CP(eng, out, in_, r=(), w=()):
            if eng == "act":
                ACT(out, in_, AF.Copy, r=r, w=w)
            else:
                S.op(eng, lambda e: e.tensor_copy(out=out, in_=in_), reads=r, writes=w)

        def MEMSET(eng, ap, val, w=()):
            S.op(eng, lambda e: e.memset(ap, val), writes=w)

        def DMA(out, in_, r=(), w=(), key="d"):
            S.op("sp", lambda e: e.dma_start(out=out, in_=in_), reads=r, writes=w, dma=key)

        def finish_partial():
            zz = T("zz", [128, D], F32)
            MEMSET("pool", zz[:], 0.0, w=["zz"])
            for i in range(P0 // 128):
                DMA(y_d[i * 128:(i + 1) * 128, :], zz[:], r=["zz"], w=[f"y{i}"], key="yz")
            S.barrier()
            S.emit()

        identF = T("identF", [128, 128], F32)
        identB = T("identB", [128, 128], BF16)
        blkF = T("blkF", [128, 128], F32)
        blkB = T("blkB", [128, 128], BF16)
        ones64 = T("ones64", [64, 64], F32)
        onesrow = T("onesrow", [1, 128], F32)
        flagv = T("flagv_s", [128, 2], F32)
        gcols = T("gcols_s", [128, 16, 2], F32)
        modcol = T("modcol", [128, 96], F32)
        nrm = T("nrm", [128, 16, 8], F32)
        gate_m = T("gate_m", [128, D], F32)
        gate_f = T("gate_f", [128, D], F32)
        mur = T("mur", [128, 8, 3], F32)
        omur = T("omur", [128, 8, 3], F32)
        rwv = T("rwv", [128, 8, 5], F32)
        ln64 = T("ln64_s", [64, 16, 2], F32)
        qkg = T("qkg_s", [128, 3], F32)
        qkgs = T("qkgs", [128, 3], F32)
        neglam = T("neglam", [128, 1], F32)
        farc = T("farc_s", [128, 8], F32)
        farcp = T("farcp", [128, 8], F32)
        Wla = T("Wla", [128, 16, 288], BF16)
        Wlb = T("Wlb", [128, 16, 288], BF16)
        wa2b = T("wa2b", [128, 1024], BF16)
        g2b = T("g2b", [128, 1024], BF16)
        g2c = T("g2c", [32, 1024], BF16)
        rmask = T("rmask", [64, 320], F32)
        carry = T("carry", [128, 32], F32)

        DMA(identF[:], identf_d, w=["identF"], key="c0")
        DMA(identB[:], identb_d, w=["identB"], key="c1")
        DMA(blkF[:], blk_d, w=["blkF"], key="c2")
        DMA(flagv[:], flag_d, w=["flagv"], key="c3")
        DMA(gcols[:], gcol_d, w=["gcols"], key="c4")
        DMA(mur[:], mur_d, w=["mur"], key="c5")
        DMA(rwv[:], rwv_d, w=["rwv"], key="c6")
        DMA(ln64[:], ln64_d, w=["ln64"], key="c7")
        DMA(qkg[:], qkg_d, w=["qkg"], key="c8")
        DMA(farc[:], farc_d, w=["farc"], key="c9")
        DMA(rmask[:], rmask_d, w=["rmask"], key="c10")
        CP("dve", blkB[:], blkF[:], r=["blkF"], w=["blkB"])
        MEMSET("pool", ones64[:], 1.0 / 64.0, w=["ones64"])
        MEMSET("pool", onesrow[:], 1.0, w=["onesrow"])
        MEMSET("pool", carry[:], 0.0, w=["carry"])
        TS("dve", omur[:], mur[:], -1.0, ALU.mult, 1.0, ALU.add, r=["mur"], w=["omur"])
        TS("dve", qkgs[:, 0:1], qkg[:, 0:1], 0.125, ALU.mult, r=["qkg"], w=["qkgs"])
        CP("dve", qkgs[:, 1:2], qkg[:, 1:2], r=["qkg", "qkgs"], w=["qkgs"])
        TS("dve", qkgs[:, 2:3], qkg[:, 2:3], 0.8, ALU.mult, r=["qkg", "qkgs"], w=["qkgs"])
        TS("dve", farcp[:], farc[:], flagv[:, 1:2], ALU.add, r=["farc", "flagv"], w=["farcp"])

        with (contextlib.ExitStack() if not rwkv_only else contextlib.nullcontext()) as st:
          if not rwkv_only:
              def T0(name, shape, dt):
                  return st.enter_context(nc.sbuf_tensor(name, list(shape), dt))
              ccol = T0("ccol", [128, 16], F32)
              cact = T0("cact", [128, 16], F32)
              modrow = T0("modrow", [1, 6 * D], F32)
              lamr = T0("lamr", [1, 256], F32)
              lamw = T0("lamw", [1, 136], F32)
              wst = [T0(f"wst{i}", [128, 16, 256], F32) for i in range(2)]
              wlf = T0("wlf", [128, 16, 288], F32)
              muw = T0("muw", [128, 16, 3], F32)
              omuw = T0("omuw", [128, 16, 3], F32)
              wa2f = T0("wa2f", [128, 1024], F32)
              g2f = T0("g2f", [128, 1024], F32)
              g2cf = T0("g2cf", [32, 1024], F32)
              one11 = T0("one11", [1, 1], F32)
              pcf = Ring("pcf", [T0(f"pcf{i}", [128, 16, 128], F32) for i in range(2)])
              pcb = Ring("pcb", [T0(f"pcb{i}", [128, 16, 128], BF16) for i in range(2)])
              winv0 = win_d.rearrange("(kc p) n -> p kc n", p=128)

              DMA(ccol[:], ccol_d, w=["ccol"], key="c0")
              DMA(modrow[:], bada_d, w=["modrow"], key="c1")
              DMA(lamr[:], lam_d, w=["lamr"], key="c2")
              DMA(wlf[:], wl_d.rearrange("(kc p) n -> p kc n", p=128), w=["wlf"], key="c3")
              DMA(muw[:], muw_d, w=["muw"], key="c4")
              DMA(wa2f[:], wa2_d, w=["wa2f"], key="c5")
              DMA(g2f[:], g2_d[0:128, :], w=["g2f"], key="c6")
              DMA(g2cf[:], g2_d[128:160, :], w=["g2cf"], key="c7")
              MEMSET("pool", one11[:], 1.0, w=["one11"])
              ACT(cact[:], ccol[:], AF.Silu, r=["ccol"], w=["cact"])
              wav = wada_d.rearrange("(p k) n -> p k n", k=16)
              for ct in range(48):
                  ws, wk = wst[ct % 2], f"wst{ct % 2}"
                  DMA(ws[:], wav[:, :, ct * 256:(ct + 1) * 256], w=[wk], key=wk)
                  pb = ct % 2
                  for k in range(16):
                      PE(bank(pb)[0:1, 0:256], cact[:, k:k + 1], ws[:, k, :], start=(k == 0), stop=(k == 15),
                         r=["cact", wk], w=[f"pb{pb}"])
                  TT("dve", modrow[0:1, ct * 256:(ct + 1) * 256], bank(pb)[0:1, 0:256], modrow[0:1, ct * 256:(ct + 1) * 256],
                     ALU.add, r=[f"pb{pb}", "modrow"], w=["modrow"])
                  pf, pfk = pcf.next()
                  DMA(pf[:], winv0[:, :, ct * 128:(ct + 1) * 128], w=[pfk], key=pfk)
                  pbt, pbk = pcb.next()
                  CP("pool", pbt[:], pf[:], r=[pfk], w=[pbk])
                  DMA(winb_s[ct], pbt[:].rearrange("p k n -> p (k n)"), r=[pbk], w=[f"winb_{ct}"], key=pbk)
              for j in range(96):
                  PE(bank(2)[:, j:j + 1], modrow[0:1, j * 128:(j + 1) * 128], one11[0:1, 0:1],
                     r=["modrow", "one11"], w=["pb2"])
              CP("dve", modcol[:], bank(2)[:, 0:96], r=["pb2"], w=["modcol"])
              for gi, (gt, off) in enumerate(((gate_m, 2 * D), (gate_f, 5 * D))):
                  for dt in range(4):
                      pb = 3 + (gi * 4 + dt) % 4
                      PE(bank(pb), onesrow[0:1, :], modrow[0:1, off + dt * 512: off + (dt + 1) * 512],
                         r=["modrow", "onesrow"], w=[f"pb{pb}"])
                      CP("act", gt[:, dt * 512:(dt + 1) * 512], bank(pb), r=[f"pb{pb}"], w=[f"gate{gi}"])
              for si, (gi, sc0, sh0) in enumerate(((0, 16, 0), (1, 64, 48))):
                  b4 = si * 4
                  STT(nrm[:, :, b4 + 0], modcol[:, sc0:sc0 + 16], 1.0, gcols[:, :, gi], ALU.add, ALU.mult,
                      r=["modcol", "gcols"], w=["nrm"])
                  CP("dve", nrm[:, :, b4 + 1], modcol[:, sh0:sh0 + 16], r=["modcol", "nrm"], w=["nrm"])
                  TS("dve", nrm[:, :, b4 + 2], nrm[:, :, b4 + 0], flagv[:, 0:1], ALU.mult, r=["nrm", "flagv"], w=["nrm"])
                  TS("dve", nrm[:, :, b4 + 3], nrm[:, :, b4 + 1], flagv[:, 0:1], ALU.mult, r=["nrm", "flagv"], w=["nrm"])
              TT("dve", lamw[0:1, 0:64], lamr[0:1, 0:64], lamr[0:1, 64:128], ALU.mult, r=["lamr"], w=["lamw"])
              TT("dve", lamw[0:1, 64:128], lamr[0:1, 128:192], lamr[0:1, 192:256], ALU.mult, r=["lamr", "lamw"], w=["lamw"])
              S.op("dve", lambda e: e.tensor_reduce(out=lamw[0:1, 128:130], in_=lamw[0:1, 0:128].rearrange("p (a b) -> p a b", a=2),
                                                    axis=AX.X, op=ALU.add), reads=["lamw"], writes=["lamw2"])
              ACT(lamw[0:1, 130:132], lamw[0:1, 128:130], AF.Exp, r=["lamw2"], w=["lamw3"])
              TT("dve", lamw[0:1, 132:133], lamw[0:1, 131:132], lamw[0:1, 130:131], ALU.subtract, r=["lamw3"], w=["lamw4"])
              TS("dve", lamw[0:1, 133:134], lamw[0:1, 132:133], -0.2, ALU.add, r=["lamw4"], w=["lamw5"])
              PE(bank(7)[:, 0:1], onesrow[0:1, :], lamw[0:1, 133:134], r=["lamw5", "onesrow"], w=["pb7"])
              CP("dve", neglam[:], bank(7)[:, 0:1], r=["pb7"], w=["neglam"])
              TS("dve", omuw[:], muw[:], -1.0, ALU.mult, 1.0, ALU.add, r=["muw"], w=["omuw"])
              for kc in range(16):
                  for (c0, c1, j) in ((0, 64, 0), (64, 128, 1), (128, 288, 2)):
                      TS("dve", Wla[:, kc, c0:c1], wlf[:, kc, c0:c1], omuw[:, kc, j:j + 1], ALU.mult, r=["wlf", "omuw"], w=["Wla"])
                      TS("pool", Wlb[:, kc, c0:c1], wlf[:, kc, c0:c1], muw[:, kc, j:j + 1], ALU.mult, r=["wlf", "muw"], w=["Wlb"])
              CP("pool", wa2b[:], wa2f[:], r=["wa2f"], w=["wa2b"])
              CP("pool", g2b[:], g2f[:], r=["g2f"], w=["g2b"])
              CP("pool", g2c[:], g2cf[:], r=["g2cf"], w=["g2c"])
              S.barrier()

        with (contextlib.ExitStack() if not rwkv_only else contextlib.nullcontext()) as st:
          if not rwkv_only:
              def T1(name, shape, dt):
                  return st.enter_context(nc.sbuf_tensor(name, list(shape), dt))
              xring = Ring("xs", [T1(f"xs{i}", [128, D], F32) for i in range(2)])
              xnring = Ring("xn", [T1(f"xn{i}", [128, D], F32) for i in range(1)])
              stat = Ring("stat", [T1(f"stat{i}", [128, 4], F32) for i in range(4)])
              hTs = [T1(f"hT{i}", [128, 16, NT], BF16) for i in range(1)]
              wbr = Ring("wb", [T1(f"wb{i}", [128, 16, 128], BF16) for i in range(4)])
              wkF = Ring("wkF", [T1(f"wkF{i}", [128, 516], F32) for i in range(34)])
              mixR = Ring("mixR", [T1(f"mixR{i}", [128, 516], F32) for i in range(6)])
              wkB = Ring("wkB", [T1(f"wkB{i}", [128, 512], BF16) for i in range(10)])
              rnR = Ring("rnR", [T1(f"rnR{i}", [128, NT], F32) for i in range(2)])
              sqR = Ring("sqR", [T1(f"sqR{i}", [128, NT], BF16) for i in range(3)])
              hidA = T1("hidA", [128, NT], BF16)
              hidB = T1("hidB", [128, NT], BF16)
              hidC = T1("hidC", [32, NT], BF16)
              onesT = T1("onesT", [128, NT], F32)
              baseT = Ring("base", [T1(f"base{i}", [128, 8], F32) for i in range(3)])
              wcT = Ring("wc", [T1(f"wc{i}", [128, 8], F32) for i in range(3)])
              MEMSET("pool", onesT[:], 1.0, w=["onesT"])
              PSR = Ring("pb", [None] * 6)
              PSA = Ring("pa", [None] * 2)

              def nbank():
                  _, k = PSR.next()
                  return bank(int(k[2:])), k

              def abank():
                  _, k = PSA.next()
                  i_ = 6 + int(k[2:])
                  return bank(i_), f"pb{i_}"

              def load_w(c0):
                  wb, wbk = wbr.next()
                  DMA(wb[:].rearrange("p k n -> p (k n)"), winb_s[c0 // 128], r=[f"winb_{c0 // 128}"], w=[wbk], key=wbk)
                  return wb, wbk

              def proj(wb_ap, wbk, hT, hk, ncol=128, alloc=None):
                  pb, pk = (alloc or nbank)()
                  for kc in range(16):
                      PE(pb[0:ncol, :], wb_ap(kc), hT[:, kc, :], start=(kc == 0), stop=(kc == 15), r=[wbk] + hk, w=[pk])
                  return pb, pk

              cidx = [0]

              def shifted(pb, pk, ci):
                  t, tk = wkF.next()
                  ACT(t[:, 1:NT + 1], pb, AF.Copy, r=[pk], w=[tk])
                  CP("pool", t[:, 0:1], carry[:, ci:ci + 1], r=["carry%d" % ci, tk], w=[tk])
                  CP("pool", carry[:, ci:ci + 1], t[:, NT:NT + 1], r=[tk], w=["carry%d" % ci])
                  return t, tk

              for ti in range(NTILE if cut > 0 else 0):
                  kvonly = ti < FULLT
                  prefix = (ti * NT) < P0
                  hT, hk0 = hTs[0], "hT0"
                  hk = [hk0 + "d", hk0 + "a"]
                  nb = 2 if prefix else 0
                  for bi in range(NT // 128):
                      tok = ti * NT + bi * 128
                      xs, xk = xring.next()
                      DMA(xs[:], x_d[tok:tok + 128, :], w=[xk], key=xk)
                      sv, sk = stat.next()
                      xn, xnk = xnring.next()
                      ACT(xn[:], xs[:], AF.Square, r=[xk], w=[xnk, sk], accum=sv[:, 0:1])
                      ACT(sv[:, 1:2], sv[:, 0:1], AF.Ln, r=[sk], w=[sk + "b"], scale=1.0 / D, bias=1e-6)
                      ACT(sv[:, 2:3], sv[:, 1:2], AF.Exp, r=[sk + "b"], w=[sk + "c"], scale=-0.5)
                      TS("dve", xn[:], xs[:], sv[:, 2:3], ALU.mult, r=[xk, sk + "c"], w=[xnk])
                      for kg in range(4):
                          pb, pk = nbank()
                          for j in range(4):
                              kc = kg * 4 + j
                              PET(pb[:, j * 128:(j + 1) * 128], xn[:, kc * 128:(kc + 1) * 128], identF[:], r=[xnk, "identF"], w=[pk])
                          for j in range(4):
                              kc = kg * 4 + j
                              TS("dve", hT[:, kc, bi * 128:(bi + 1) * 128], pb[:, j * 128:(j + 1) * 128],
                                 nrm[:, kc, nb:nb + 1], ALU.mult, nrm[:, kc, nb + 1:nb + 2], ALU.add, r=[pk, "nrm"], w=[hk0 + "d"])
                  if cut <= 1:
                      continue
                  ci = 0
                  for (c0, nc_, hid, chunk) in ((0, 128, hidA, "A"), (128, 128, hidB, "B"), (256, 32, hidC, "C")):
                      if kvonly and chunk != "A":
                          ci += 1
                          continue
                      pa, pak = proj(lambda kc, c0=c0, nc_=nc_: Wla[:, kc, c0:c0 + nc_], "Wla", hT, hk, nc_)
                      pb_, pbk = proj(lambda kc, c0=c0, nc_=nc_: Wlb[:, kc, c0:c0 + nc_], "Wlb", hT, hk, nc_)
                      t, tk = wkF.next()
                      ACT(t[0:nc_, 1:NT + 1], pb_[0:nc_, :], AF.Copy, r=[pbk], w=[tk])
                      CP("pool", t[0:nc_, 0:1], carry[0:nc_, ci:ci + 1], r=["carry%d" % ci, tk], w=[tk])
                      CP("pool", carry[0:nc_, ci:ci + 1], t[0:nc_, NT:NT + 1], r=[tk], w=["carry%d" % ci])
                      u, uk = wkF.next()
                      TT("dve", u[0:nc_, 0:NT], pa[0:nc_, :], t[0:nc_, 0:NT], ALU.add, r=[pak, tk], w=[uk])
                      if chunk == "A":
                          ACT(hid[0:64, :], u[0:64, 0:NT], AF.Tanh, r=[uk], w=["hidA"])
                          ACT(hid[64:128, :], u[64:128, 0:NT], AF.Copy, r=[uk, "hidA"], w=["hidA"])
                      else:
                          ACT(hid[0:nc_, :], u[0:nc_, 0:NT], AF.Sigmoid, r=[uk], w=["hid" + chunk])
                      ci += 1
                  def rwA(fg):
                      f0 = fg * 128
                      cb = 3 + fg * 3
                      pk_ = {}
                      for wi, col0 in enumerate((3072, 4096, 5120)):
                          if kvonly and wi == 0:
                              continue
                          wb, wbk = load_w(col0 + f0)
                          pbx, pkx = proj(lambda kc, wb=wb: wb[:, kc, :], wbk, hT, hk)
                          m1, m1k = wkF.next()
                          ACT(m1[:, 0:NT], pbx, AF.Copy, r=[pkx, "omur"], w=[m1k], scale=omur[:, fg, wi:wi + 1])
                          t, tk = shifted(pbx, pkx, cb + wi)
                          m2, m2k = mixR.next()
                          STT(m2[:, 0:NT], t[:, 0:NT], mur[:, fg, wi:wi + 1], m1[:, 0:NT], ALU.mult, ALU.add, r=[tk, m1k, "mur"], w=[m2k])
                          pk_[wi] = (m2, m2k)
                      return pk_

                  def rwB(fg, pk_):
                      f0 = fg * 128
                      tsl = slice(ti * NT, (ti + 1) * NT)
                      pw, pwk = nbank()
                      PE(pw, wa2b[0:64, f0:f0 + 128], hidA[0:64, :], r=["wa2b", "hidA"], w=[pwk])
                      ld, ldk = wkF.next()
                      ACT(ld[:, 0:NT], pw, AF.Sigmoid, r=[pwk, "rwv"], w=[ldk], bias=rwv[:, fg, 0:1])
                      pa, pak = nbank()
                      PE(pa, wa2b[64:128, f0:f0 + 128], hidA[64:128, :], r=["wa2b", "hidA"], w=[pak])
                      av, avk = wkF.next()
                      ACT(av[:, 0:NT], pa, AF.Sigmoid, r=[pak, "rwv"], w=[avk], bias=rwv[:, fg, 1:2])
                      yield
                      ACT(ld[:, 0:NT], ld[:, 0:NT], AF.Copy, r=[ldk], w=[ldk], scale=-math.exp(-0.5))
                      kx, kxk = pk_[1]
                      vx, vxk = pk_[2]
                      k0, k0k = wkF.next()
                      ACT(k0[:, 0:NT], kx[:, 0:NT], AF.Copy, r=[kxk, "rwv"], w=[k0k], scale=rwv[:, fg, 2:3])
                      yield
                      Lr, Lrk = wkF.next()
                      S.op("dve", lambda e, Lr=Lr, ld=ld: e.tensor_tensor_scan(out=Lr[:, 0:NT], data0=onesT[:], data1=ld[:, 0:NT],
                                                                                 initial=0.0, op0=ALU.mult, op1=ALU.add),
                           reads=[ldk, "onesT"], writes=[Lrk])
                      sq, sqk = wkB.next()
                      ACT(sq[:], k0[:, 0:NT], AF.Square, r=[k0k], w=[sqk])
                      pss, pssk = nbank()
                      PE(pss, blkB[:], sq[:], r=["blkB", sqk], w=[pssk])
                      rn, rnk = wkF.next()
                      TS("dve", rn[:, 0:NT], pss, 1e-18, ALU.max, r=[pssk], w=[rnk])
                      yield
                      bs, bsk = baseT.next()
                      MEMSET("pool", bs[:, 0:1], 0.0, w=[bsk])
                      Lr3 = Lr[:, 0:NT].rearrange("p (c t) -> p c t", t=64)
                      CP("pool", bs[:, 1:8], Lr3[:, 0:7, 63], r=[Lrk, bsk], w=[bsk])
                      ACT(rn[:, 0:NT], rn[:, 0:NT], AF.Ln, r=[rnk], w=[rnk])
                      yield
                      TT("dve", Lr3, Lr3, bs[:, 0:8].unsqueeze(2).to_broadcast([128, 8, 64]), ALU.subtract, r=[Lrk, bsk], w=[Lrk])
                      ACT(rn[:, 0:NT], rn[:, 0:NT], AF.Exp, r=[rnk], w=[rnk], scale=-0.5)
                      yield
                      Wt, Wtk = wkF.next()
                      ACT(Wt[:, 0:NT], Lr[:, 0:NT], AF.Exp, r=[Lrk], w=[Wtk])
                      Wm, Wmk = wkF.next()
                      TT("pool", Wm[:, 0:NT], Lr[:, 0:NT], ld[:, 0:NT], ALU.subtract, r=[Lrk, ldk], w=[Wmk])
                      kn, knk = wkF.next()
                      TT("dve", kn[:, 0:NT], k0[:, 0:NT], rn[:, 0:NT], ALU.mult, r=[k0k, rnk], w=[knk])
                      yield
                      Wi, Wik = wkF.next()
                      ACT(Wi[:, 0:NT], Lr[:, 0:NT], AF.Exp, r=[Lrk], w=[Wik], scale=-1.0)
                      Wr, Wrk = wkF.next()
                      Wr3 = Wr[:, 0:NT].rearrange("p (c t) -> p c t", t=64)
                      TT("pool", Wr3, Lr3, Lr3[:, :, 63:64].to_broadcast([128, 8, 64]), ALU.subtract, r=[Lrk], w=[Wrk])
                      k2, k2k = wkF.next()
                      TS("dve", k2[:, 0:NT], av[:, 0:NT], -1.0, ALU.add, rwv[:, fg, 3:4], ALU.mult, r=[avk, "rwv"], w=[k2k])
                      yield
                      ACT(Wm[:, 0:NT], Wm[:, 0:NT], AF.Exp, r=[Wmk], w=[Wmk])
                      bb, bbk = wkF.next()
                      TT("pool", bb[:, 0:NT], kn[:, 0:NT], av[:, 0:NT], ALU.mult, r=[knk, avk], w=[bbk])
                      TT("dve", k2[:, 0:NT], k2[:, 0:NT], kx[:, 0:NT], ALU.mult, r=[k2k, kxk], w=[k2k])
                      yield
                      ACT(Wr[:, 0:NT], Wr[:, 0:NT], AF.Exp, r=[Wrk], w=[Wrk], scale=-1.0)
                      TT("pool", k2[:, 0:NT], k2[:, 0:NT], kx[:, 0:NT], ALU.add, r=[k2k, kxk], w=[k2k])
                      wc, wck = wcT.next()
                      Wt3 = Wt[:, 0:NT].rearrange("p (c t) -> p c t", t=64)
                      CP("pool", wc[:, 0:8], Wt3[:, :, 63], r=[Wtk], w=[wck])
                      DMA(WC_s[f0:f0 + 128, ti * 8:(ti + 1) * 8], wc[:, 0:8], r=[wck], w=[f"WC_{ti}_{fg}"], key=wck)
                      yield
                      outs = [("kap", kn, knk, Wm, Wmk, "dve"), ("bt", bb, bbk, Wi, Wik, "pool"), ("kt", k2, k2k, Wi, Wik, "dve"),
                              ("bh", bb, bbk, Wr, Wrk, "pool"), ("kh", k2, k2k, Wr, Wrk, "dve")]
                      if not kvonly:
                          rx, rxk = pk_[0]
                          outs.append(("rt", rx, rxk, Wt, Wtk, "pool"))
                      for (nm, a_, ak, b_, bk, eng) in outs:
                          o, ok = wkB.next()
                          TT(eng, o[:], a_[:, 0:NT], b_[:, 0:NT], ALU.mult, r=[ak, bk], w=[ok])
                          DMA(rw_s[nm][f0:f0 + 128, tsl], o[:], r=[ok], w=[f"{nm}_{ti}_{fg}"], key=ok)
                          yield
                      o, ok = wkB.next()
                      CP("pool", o[:], vx[:, 0:NT], r=[vxk], w=[ok])
                      DMA(rw_s["vT"][f0:f0 + 128, tsl], o[:], r=[ok], w=[f"vT_{ti}_{fg}"], key=ok)
                      if not kvonly:
                          rx, rxk = pk_[0]
                          bq, bqk = wkF.next()
                          STT(bq[:, 0:NT], rx[:, 0:NT], rwv[:, fg, 4:5], k2[:, 0:NT], ALU.mult, ALU.mult, r=[rxk, k2k, "rwv"], w=[bqk])
                          pbn, pbnk = nbank()
                          PE(pbn, blkF[:], bq[:, 0:NT], r=["blkF", bqk], w=[pbnk])
                          bo, bok = wkF.next()
                          TT("dve", bo[:, 0:NT], pbn, vx[:, 0:NT], ALU.mult, r=[pbnk, vxk], w=[bok])
                          DMA(bon_s[f0:f0 + 128, tsl], bo[:, 0:NT], r=[bok], w=[f"bon_{ti}_{fg}"], key=bok)
                          yield
                          pg, pgk = nbank()
                          PE(pg, g2b[:, f0:f0 + 128], hidB[:], start=True, stop=False, r=["g2b", "hidB"], w=[pgk])
                          PE(pg, g2c[:, f0:f0 + 128], hidC[:], start=False, stop=True, r=["g2c", "hidC"], w=[pgk])
                          go, gok = wkF.next()
                          ACT(go[:, 0:NT], pg, AF.Copy, r=[pgk], w=[gok])
                          DMA(g_s[f0:f0 + 128, tsl], go[:, 0:NT], r=[gok], w=[f"g_{ti}_{fg}"], key=gok)
                      yield

                  def qkA(h, which, col0, dst, gi):
                      wb, wbk = load_w(col0 + h * 128)
                      pbx, pkx = proj(lambda kc, wb=wb: wb[:, kc, :], wbk, hT, hk, alloc=abank)
                      sq, sqk = sqR.next()
                      ACT(sq[:], pbx, AF.Square, r=[pkx], w=[sqk])
                      return (pbx, pkx, sq, sqk)

                  def qkB(h, which, col0, dst, gi, pbx, pkx, sq, sqk):
                      pss, pssk = nbank()
                      PE(pss, blkB[:], sq[:], r=["blkB", sqk], w=[pssk])
                      rn, rnk = rnR.next()
                      ACT(rn[:, 0:NT], pss, AF.Ln, r=[pssk], w=[rnk], scale=1.0 / 64.0, bias=1e-6)
                      ACT(rn[:, 0:NT], rn[:, 0:NT], AF.Exp, r=[rnk], w=[rnk], scale=-0.5)
                      o, ok = wkB.next()
                      STT(o[:], pbx, qkgs[:, gi:gi + 1], rn[:, 0:NT], ALU.mult, ALU.mult, r=[pkx, rnk, "qkgs"], w=[ok])
                      DMA(dst[h, :, ti * NT:(ti + 1) * NT], o[:], r=[ok], w=[f"{which}T_{ti}_{h}"], key=ok)

                  def vA(h):
                      wb, wbk = load_w(2048 + h * 128)
                      pbx, pkx = proj(lambda kc, wb=wb: wb[:, kc, :], wbk, hT, hk, alloc=abank)
                      vb, vbk = sqR.next()
                      ACT(vb[:], pbx, AF.Copy, r=[pkx], w=[vbk])
                      return (vb, vbk)

                  def vB(h, vb, vbk):
                      pt, ptk = nbank()
                      ptb = pt.bitcast(BF16)
                      for bi in range(4):
                          PET(ptb[:, bi * 128:(bi + 1) * 128], vb[:, bi * 128:(bi + 1) * 128], identB[:], r=[vbk, "identB"], w=[ptk])
                      vo, vok = wkB.next()
                      CP("dve", vo[:], ptb[:, 0:512], r=[ptk], w=[vok])
                      DMA(v_s[ti * NT:(ti + 1) * NT, h * 128:(h + 1) * 128].rearrange("(b p) v -> p b v", p=128),
                          vo[:].rearrange("p (b v) -> p b v", b=4), r=[vok], w=[f"v_{ti}_{h}"], key=vok)

                  def attn_gen():
                      items = []
                      for h in range(8 if cut > 3 else 0):
                          items.append((qkA, qkB, (h, "k", 1024, kT_s, 1)))
                          if not kvonly:
                              items.append((qkA, qkB, (h, "q", 0, qT_s, 0)))
                          items.append((vA, vB, (h,)))
                      pendA = None
                      for (fa, fb, args) in items:
                          resA = fa(*args)
                          if pendA is not None:
                              pfb, pargs, pres = pendA
                              pfb(*pargs, *pres)
                          pendA = (fb, args, resA)
                          yield
                      if pendA is not None:
                          pfb, pargs, pres = pendA
                          pfb(*pargs, *pres)
                      yield

                  ag = attn_gen()
                  ag_live = True
                  nfg = 8 if cut > 2 else 0
                  for fp_ in range(0, nfg, 2):
                      gens = [rwB(fg, rwA(fg)) for fg in (fp_, fp_ + 1)]
                      rounds = 0
                      while gens:
                          for gen in list(gens):
                              try:
                                  next(gen)
                              except StopIteration:
                                  gens.remove(gen)
                          rounds += 1
                          if ag_live and rounds % 3 == 0:
                              try:
                                  next(ag)
                              except StopIteration:
                                  ag_live = False
                  while ag_live:
                      try:
                          next(ag)
                      except StopIteration:
                          ag_live = False
              S.barrier()

        if upto >= 2:
          with contextlib.ExitStack() as st:
            def T2(name, shape, dt):
                return st.enter_context(nc.sbuf_tensor(name, list(shape), dt))
            OPN = ["kap", "bt", "kt", "rt", "bh", "kh", "vT"]
            opr = Ring("opr", [{n: T2(f"op{i}_{n}", [64, 8, 64], BF16) for n in OPN} for i in range(4)])
            bgr = Ring("bgr", [(T2(f"bon{i}", [64, 8, 64], F32), T2(f"gg{i}", [64, 8, 64], F32)) for i in range(3)])
            I8 = T2("I8", [64, 8, 64], F32)
            WCall = T2("WCall", [64, 16, NCH], F32)
            ZF = [T2(f"ZF{g}", [64, 8, 64], F32) for g in range(2)]
            Zb = [T2(f"Zb{g}", [64, 8, 64], BF16) for g in range(2)]
            b16 = lambda nm, n: Ring(nm, [T2(f"{nm}{i}", [64, 8, 64], BF16) for i in range(n)])
            f32r = lambda nm, n: Ring(nm, [T2(f"{nm}{i}", [64, 8, 64], F32) for i in range(n)])
            Nr, NTr, Xbr = b16("Nr", 8), b16("NTr", 8), b16("Xbr", 8)
            ArbR, AukR, ArkR, TTr = b16("ArbR", 2), b16("AukR", 2), b16("ArkR", 2), b16("TTr", 2)
            BtR, KtR, VtR = b16("BtR", 2), b16("KtR", 2), b16("VtR", 2)
            RhR, UbR, OutR = b16("RhR", 2), b16("UbR", 2), b16("OutR", 3)
            yFr = f32r("yFr", 20)
            id64 = identB[0:64, 0:64]
            mST_n = rmask[:, 0:64].unsqueeze(1).to_broadcast([64, 8, 64])
            mIT = rmask[:, 64:128].unsqueeze(1).to_broadcast([64, 8, 64])
            mST = rmask[:, 128:192].unsqueeze(1).to_broadcast([64, 8, 64])
            mS_n = rmask[:, 256:320].unsqueeze(1).to_broadcast([64, 8, 64])
            for h in range(8):
                CP("pool", I8[:, h, :], identF[0:64, 0:64], r=["identF"], w=["I8"])
            DMA(WCall[:], WC_s.rearrange("(h d) c -> d h c", d=64),
                r=[f"WC_{ti}_{fg}" for ti in range(NTILE) for fg in range(8)], w=["WCall"], key="wcall")
            for g in range(2):
                MEMSET("pool", ZF[g][:], 0.0, w=[f"ZF{g}"])
                MEMSET("pool", Zb[g][:], 0.0, w=[f"Zb{g}"])
            PSR2 = Ring("pb", [None] * 6)

            def nb2():
                _, k = PSR2.next()
                return bank(int(k[2:]))[0:64, :].rearrange("p (h t) -> p h t", h=8), k

            def body(c, g):
                own = c >= MIXCH
                ti = (c * 64) // NT
                tsl = slice(c * 64, (c + 1) * 64)
                ops_, opk = opr.next()
                for n in OPN:
                    if n == "rt" and not own:
                        continue
                    srcs = [f"{n}_{ti}_{fg}" for fg in range(g * 4, g * 4 + 4)]
                    DMA(ops_[n][:], rw_s[n][g * 512:(g + 1) * 512, tsl].rearrange("(h d) t -> d h t", d=64),
                        r=srcs, w=[opk + n], key=opk + n)
                K_ = lambda n: opk + n
                kap, bt, kt, rt, bh, kh, vT = [ops_[n] for n in OPN]
                psx_i = 6 + g
                PSX = bank(psx_i)[0:64, :].rearrange("p (h t) -> p h t", h=8)
                PSXk = f"pb{psx_i}"
                if own:
                    (bo, gg), bgk = bgr.next()
                    DMA(bo[:], bon_s[g * 512:(g + 1) * 512, tsl].rearrange("(h d) t -> d h t", d=64),
                        r=[f"bon_{ti}_{fg}" for fg in range(g * 4, g * 4 + 4)], w=[bgk + "b"], key=bgk + "b")
                    DMA(gg[:], g_s[g * 512:(g + 1) * 512, tsl].rearrange("(h d) t -> d h t", d=64),
                        r=[f"g_{ti}_{fg}" for fg in range(g * 4, g * 4 + 4)], w=[bgk + "g"], key=bgk + "g")

                def mm8(ps, psk, lhs, lk, rhs, rk, start=True, stop=True):
                    for h in range(8):
                        PE(ps[:, h, :], lhs[:, h, :], rhs[:, h, :], start=(start and h == 0), stop=stop, r=[lk, rk], w=[psk], skip=True)

                p1, p1k = nb2()
                mm8(p1, p1k, bt, K_("bt"), kap, K_("kap"))
                NT0, NT0k = NTr.next()
                TT("dve", NT0[:], p1, mST_n, ALU.mult, r=[p1k, "rmask"], w=[NT0k])
                yield
                p3, p3k = nb2()
                mm8(p3, p3k, kap, K_("kap"), bt, K_("bt"))
                N0, N0k = Nr.next()
                TT("dve", N0[:], p3, mS_n, ALU.mult, r=[p3k, "rmask"], w=[N0k])
                X1, X1k = Xbr.next()
                TT("pool", X1[:], NT0[:], I8[:], ALU.add, r=[NT0k, "I8"], w=[X1k])
                yield
                p2, p2k = nb2()
                mm8(p2, p2k, kt, K_("kt"), kap, K_("kap"))
                Auk, Aukk = AukR.next()
                TT("dve", Auk[:], p2, mST, ALU.mult, r=[p2k, "rmask"], w=[Aukk])
                yield
                if own:
                    p4, p4k = nb2()
                    mm8(p4, p4k, bt, K_("bt"), rt, K_("rt"))
                    Arb, Arbk = ArbR.next()
                    TT("dve", Arb[:], p4, mIT, ALU.mult, r=[p4k, "rmask"], w=[Arbk])
                    yield
                    p5, p5k = nb2()
                    mm8(p5, p5k, kt, K_("kt"), rt, K_("rt"))
                    Ark, Arkk = ArkR.next()
                    TT("dve", Ark[:], p5, mIT, ALU.mult, r=[p5k, "rmask"], w=[Arkk])
                    yield
                for h in range(8):
                    PE(PSX[:, h, :], id64, X1[:, h, :], start=(h == 0), stop=True, r=["identB", X1k], w=[PSXk], skip=True)
                Ncur, Nk_, NTcur, NTk_ = N0, N0k, NT0, NT0k
                Xb, Xbk = X1, X1k
                tm = {}
                tlist = [(bh, K_("bh"), BtR), (kh, K_("kh"), KtR), (vT, K_("vT"), VtR)]
                for m in range(1, 6):
                    pa, pak = nb2()
                    mm8(pa, pak, NTcur, NTk_, Ncur, Nk_)
                    Nn, Nnk = Nr.next()
                    CP("act", Nn[:], pa, r=[pak], w=[Nnk])
                    if m <= 4:
                        pb_, pbk = nb2()
                        mm8(pb_, pbk, Ncur, Nk_, NTcur, NTk_)
                        NTn, NTnk = NTr.next()
                        CP("dve", NTn[:], pb_, r=[pbk], w=[NTnk])
                    if m >= 2:
                        Xb, Xbk = Xbr.next()
                        CP("act", Xb[:], PSX, r=[PSXk], w=[Xbk])
                    if tlist:
                        (src, sk_, ring_) = tlist.pop(0)
                        pt, ptk = nb2()
                        ptb = bank(int(ptk[2:]))[0:64, :].bitcast(BF16)[:, 0:512].rearrange("p (h t) -> p h t", h=8)
                        for h in range(8):
                            PET(ptb[:, h, :], src[:, h, :], id64, r=[sk_, "identB"], w=[ptk])
                        dst, dk_ = ring_.next()
                        CP("act", dst[:], ptb, r=[ptk], w=[dk_])
                        tm[sk_] = (dst, dk_)
                    yield
                    for h in range(8):
                        S.op("pe", lambda e, h=h, Nn=Nn, Xb=Xb, PSX=PSX: e.matmul(PSX[:, h, :], lhsT=Nn[:, h, :], rhs=Xb[:, h, :], start=False, stop=True, skip_group_check=True),
                             reads=[Nnk, Xbk], writes=[PSXk])
                    Ncur, Nk_ = Nn, Nnk
                    if m <= 4:
                        NTcur, NTk_ = NTn, NTnk
                    yield
                Bt_, Btk = tm[K_("bh")]
                Kt_, Ktk = tm[K_("kh")]
                Vt_, Vtk = tm[K_("vT")]
                TTt, TTk = TTr.next()
                CP("act", TTt[:], PSX, r=[PSXk], w=[TTk])
                yield
                Zk = f"Zb{g}"
                pr, prk = nb2()
                mm8(pr, prk, kap, K_("kap"), Zb[g], Zk, start=True, stop=False)
                mm8(pr, prk, Auk, Aukk, Vt_, Vtk, start=False, stop=True)
                Rh, Rhk = RhR.next()
                ACT(Rh[:], pr, AF.Copy, r=[prk], w=[Rhk], scale=-1.0)
                yield
                pu, puk = nb2()
                mm8(pu, puk, TTt, TTk, Rh, Rhk)
                Ub, Ubk = UbR.next()
                CP("dve", Ub[:], pu, r=[puk], w=[Ubk])
                yield
                if own:
                    py, pyk = nb2()
                    mm8(py, pyk, Zb[g], Zk, rt, K_("rt"), start=True, stop=False)
                    mm8(py, pyk, Ub, Ubk, Arb, Arbk, start=False, stop=False)
                    mm8(py, pyk, Vt_, Vtk, Ark, Arkk, start=False, stop=True)
                    yF, yFk = yFr.next()
                    ACT(yF[:], py, AF.Copy, r=[pyk], w=[yFk])
                    ysq, ysqk = yFr.next()
                    ACT(ysq[:], py, AF.Square, r=[pyk], w=[ysqk])
                    yield
                pz, pzk = nb2()
                mm8(pz, pzk, Bt_, Btk, Ub, Ubk, start=True, stop=False)
                mm8(pz, pzk, Kt_, Ktk, Vt_, Vtk, start=False, stop=True)
                TT("dve", ZF[g][:], ZF[g][:], WCall[:, g * 8:(g + 1) * 8, c:c + 1].to_broadcast([64, 8, 64]), ALU.mult,
                   r=[f"ZF{g}", "WCall"], w=[f"ZF{g}"])
                TT("dve", ZF[g][:], ZF[g][:], pz, ALU.add, r=[f"ZF{g}", pzk], w=[f"ZF{g}"])
                CP("pool", Zb[g][:], ZF[g][:], r=[f"ZF{g}"], w=[Zk])
                yield
                if own:
                    pm, pmk = nb2()
                    PE(bank(int(pmk[2:]))[0:64, :], ones64[:], yF[:].rearrange("p h t -> p (h t)"), r=["ones64", yFk], w=[pmk])
                    mean, meank = yFr.next()
                    ACT(mean[:], pm, AF.Copy, r=[pmk], w=[meank])
                    msq, msqk = yFr.next()
                    ACT(msq[:], pm, AF.Square, r=[pmk], w=[msqk])
                    yield
                    pq, pqk = nb2()
                    PE(bank(int(pqk[2:]))[0:64, :], ones64[:], ysq[:].rearrange("p h t -> p (h t)"), r=["ones64", ysqk], w=[pqk])
                    var, vark = yFr.next()
                    TT("dve", var[:], pq, msq[:], ALU.subtract, r=[pqk, msqk], w=[vark])
                    yield
                    ACT(var[:], var[:], AF.Ln, r=[vark], w=[vark], bias=64e-5)
                    ACT(var[:], var[:], AF.Exp, r=[vark], w=[vark], scale=-0.5)
                    yc, yck = yFr.next()
                    TT("pool", yc[:], yF[:], mean[:], ALU.subtract, r=[yFk, meank], w=[yck])
                    TT("pool", yc[:], yc[:], var[:], ALU.mult, r=[yck, vark], w=[yck])
                    TT("pool", yc[:], yc[:], ln64[:, g * 8:(g + 1) * 8, 0:1].to_broadcast([64, 8, 64]), ALU.mult, r=[yck, "ln64"], w=[yck])
                    TT("pool", yc[:], yc[:], ln64[:, g * 8:(g + 1) * 8, 1:2].to_broadcast([64, 8, 64]), ALU.add, r=[yck, "ln64"], w=[yck])
                    TT("pool", yc[:], yc[:], bo[:], ALU.add, r=[yck, bgk + "b"], w=[yck])
                    oo, ook = OutR.next()
                    TT("pool", oo[:], yc[:], gg[:], ALU.mult, r=[yck, bgk + "g"], w=[ook])
                    DMA(oT_s[1024 + g * 512:1024 + (g + 1) * 512, tsl].rearrange("(h d) t -> d h t", d=64), oo[:],
                        r=[ook], w=[f"orw_{c}_{g}"], key=ook)
                    yield

            for c in range(NCH):
                gens = [body(c, 0), body(c, 1)]
                while gens:
                    for gen in list(gens):
                        try:
                            next(gen)
                        except StopIteration:
                            gens.remove(gen)
            S.barrier()
        if upto >= 3:
          with contextlib.ExitStack() as st:
            def T3(name, shape, dt):
                return st.enter_context(nc.sbuf_tensor(name, list(shape), dt))
            KT = [T3(f"KT{i}", [128, L], BF16) for i in range(2)]
            VH = [T3(f"VH{i}", [128, NBLK, 130], BF16) for i in range(2)]
            QZ = [[T3(f"QZ{i}_{j}", [128, NMIX], BF16) for j in range(2)] for i in range(2)]
            bdall = T3("bdall", [128, 8, 128], F32)
            bpall = T3("bpall", [128, 8, 128], F32)
            maskd = T3("maskd_s", [128, 128], F32)
            PTr = Ring("PTr", [T3(f"PT{i}", [128, 512], BF16) for i in range(8)])
            tmpr = Ring("tmpr", [T3(f"tmp{i}", [128, 128], F32) for i in range(8)])
            o_r = Ring("o_r", [T3(f"of{i}", [128, 128], F32) for i in range(4)])
            ob_r = Ring("ob_r", [T3(f"ob{i}", [128, 128], BF16) for i in range(4)])
            ot_r = Ring("ot_r", [T3(f"ot{i}", [128, 128], BF16) for i in range(4)])
            st_r = Ring("st_r", [T3(f"ast{i}", [128, 8], F32) for i in range(6)])
            junk3 = T3("junk3", [128, 128], F32)
            pcf3 = Ring("pcf3", [T3(f"pcf3{i}", [128, D], F32) for i in range(2)])
            pcb3 = Ring("pcb3", [T3(f"pcb3{i}", [128, D], BF16) for i in range(2)])
            wupv3 = wup_d.rearrange("(kc p) n -> p kc n", p=128)
            woutv3 = wout_d.rearrange("(kc p) n -> p kc n", p=128)
            woutbv3 = woutb_s.rearrange("(kc p) n -> p kc n", p=128)
            precast = ([("up", c) for c in range(88)] + [("dn", f) for f in range(NFF)] + [("out", kc) for kc in range(16)])

            def do_precast(n):
                for _ in range(n):
                    if not precast:
                        return
                    kind, ix = precast.pop(0)
                    pf, pfk = pcf3.next()
                    pbt, pbk = pcb3.next()
                    if kind == "up":
                        DMA(pf[:].rearrange("p (k n) -> p k n", k=16), wupv3[:, :, ix * 128:(ix + 1) * 128], w=[pfk], key=pfk)
                    elif kind == "dn":
                        DMA(pf[:], wdn_d[ix * 128:(ix + 1) * 128, :], w=[pfk], key=pfk)
                    else:
                        DMA(pf[:], woutv3[:, ix, :], w=[pfk], key=pfk)
                    CP("pool", pbt[:], pf[:], r=[pfk], w=[pbk])
                    if kind == "up":
                        DMA(wupb_s[ix], pbt[:], r=[pbk], w=[f"wupb_{ix}"], key=pbk)
                    elif kind == "dn":
                        DMA(wdnb_s[ix * 128:(ix + 1) * 128, :], pbt[:], r=[pbk], w=[f"wdnb_{ix}"], key=pbk)
                    else:
                        DMA(woutbv3[:, ix, :], pbt[:], r=[pbk], w=[f"woutb_{ix}"], key=pbk)
            DMA(bdall[:], biasd_d, w=["bdall"], key="c0")
            DMA(bpall[:], biasp_d, w=["bpall"], key="c1")
            DMA(maskd[:], maskd_d, w=["maskd"], key="c2")
            TT("dve", bdall[:], bdall[:], maskd[:].unsqueeze(1).to_broadcast([128, 8, 128]), ALU.add, r=["bdall", "maskd"], w=["bdall"])
            for i in range(2):
                MEMSET("pool", VH[i][:, :, 128:130], 1.0, w=[f"VH{i}"])
                MEMSET("pool", QZ[i][0][64:128, :], 0.0, w=[f"QT{i}"])
                MEMSET("pool", QZ[i][1][0:64, :], 0.0, w=[f"QT{i}"])
            PB0 = P0 // 128
            qgroups = {}
            for qb in range(MIXB, NBLK):
                qgroups.setdefault(qb // 4, []).append(qb)
            SCR = Ring("pb", [None] * 8)

            def sc_bank():
                while True:
                    _, k = SCR.next()
                    if int(k[2:]) >= 4:
                        return bank(int(k[2:])), k

            def kvq_keys(h):
                return ([f"kT_{ti}_{h}" for ti in range(NTILE)], [f"v_{ti}_{h}" for ti in range(NTILE)],
                        [f"qT_{ti}_{h}" for ti in range(FULLT, NTILE)])

            for h in range(8):
                hb = h % 2
                kk_, vk_, qk_ = kvq_keys(h)
                DMA(KT[hb][:], kT_s[h], r=kk_, w=[f"KT{hb}"], key=f"KT{hb}")
                DMA(VH[hb][:, :, 0:128], v_s[:, h * 128:(h + 1) * 128].rearrange("(b p) v -> p b v", p=128), r=vk_, w=[f"VH{hb}"], key=f"VH{hb}")
                DMA(QZ[hb][0][0:64, :], qT_s[h, 0:64, TOK0:L], r=qk_, w=[f"QT{hb}"], key=f"QT{hb}")
                DMA(QZ[hb][1][64:128, :], qT_s[h, 64:128, TOK0:L], r=qk_, w=[f"QT{hb}"], key=f"QT{hb}")
                for gi, qbs in sorted(qgroups.items()):
                    qb0, qb1 = qbs[0], qbs[-1]
                    nqb = len(qbs)
                    do_precast(4)
                    for b_ in range(4):
                        S.op("dve", lambda e, b_=b_: e.memset(bank(b_), 0.0), writes=[f"pb{b_}"])

                    def score(i, j):
                        lo = max(j, qb0)
                        c0 = (lo - qb0) * 128
                        ncol = (qb1 - lo + 1) * 128
                        qc0 = lo * 128 - TOK0
                        sb, sbk = sc_bank()
                        PE(sb[:, c0:c0 + ncol], KT[hb][:, j * 128:(j + 1) * 128],
                           QZ[hb][i][:, qc0:qc0 + ncol], r=[f"KT{hb}", f"QT{hb}"], w=[sbk])
                        PT, PTk = PTr.next()
                        pref = j < PB0
                        for qb in range(lo, qb1 + 1):
                            cs = slice((qb - qb0) * 128, (qb - qb0 + 1) * 128)
                            if qb - j >= 2:
                                continue
                            tmp, tmpk = tmpr.next()
                            btile = bdall if qb == j else bpall
                            TT("dve", tmp[:], sb[:, cs], btile[:, h, :], ALU.add, r=[sbk, "bdall", "bpall"], w=[tmpk])
                            if pref:
                                ACT(PT[:, cs], tmp[:], AF.Exp, r=[tmpk, "flagv"], w=[PTk], bias=flagv[:, 1:2])
                            else:
                                ACT(PT[:, cs], tmp[:], AF.Exp, r=[tmpk], w=[PTk])
                        far0 = max(j + 2, qb0)
                        if far0 <= qb1:
                            fs = slice((far0 - qb0) * 128, (qb1 - qb0 + 1) * 128)
                            ACT(PT[:, fs], sb[:, fs], AF.Exp, r=[sbk, "farcp", "farc"], w=[PTk],
                                bias=(farcp[:, h:h + 1] if pref else farc[:, h:h + 1]))
                        return (i, j, lo, PT, PTk)

                    def pv(i, j, lo, PT, PTk):
                        for qb in range(lo, qb1 + 1):
                            ql = qb - qb0
                            ob_ = i * 2 + ql // 2
                            oc = (ql % 2) * 256
                            PE(bank(ob_)[:, oc:oc + 129], PT[:, ql * 128:(ql + 1) * 128], VH[hb][:, j, 0:129],
                               start=False, stop=True, r=[PTk, f"VH{hb}"], w=[f"pb{ob_}"], skip=True)

                    pend = []
                    for i in range(2):
                        for j in range(qb1 + 1):
                            pend.append(score(i, j))
                            if len(pend) > 2:
                                pv(*pend.pop(0))
                    while pend:
                        pv(*pend.pop(0))
                    for qb in qbs:
                        ql = qb - qb0
                        O0 = bank(0 + ql // 2)[:, (ql % 2) * 256:(ql % 2) * 256 + 129]
                        O1 = bank(2 + ql // 2)[:, (ql % 2) * 256:(ql % 2) * 256 + 129]
                        k0_, k1_ = f"pb{ql // 2}", f"pb{2 + ql // 2}"
                        sv, svk = st_r.next()
                        TS("dve", sv[:, 0:1], O0[:, 128:129], 1e-30, ALU.add, r=[k0_], w=[svk])
                        TS("dve", sv[:, 1:2], O1[:, 128:129], 1e-30, ALU.add, r=[k1_, svk], w=[svk])
                        S.op("dve", lambda e, sv=sv: e.reciprocal(out=sv[:, 2:4], in_=sv[:, 0:2]), reads=[svk], writes=[svk + "r"])
                        TT("dve", sv[:, 4:5], sv[:, 3:4], neglam[:], ALU.mult, r=[svk + "r", "neglam"], w=[svk + "n"])
                        of, ofk = o_r.next()
                        ACT(of[:], O0[:, 0:128], AF.Copy, r=[k0_, svk + "r"], w=[ofk], scale=sv[:, 2:3])
                        STT(of[:], O1[:, 0:128], sv[:, 4:5], of[:], ALU.mult, ALU.add, r=[k1_, svk + "n", ofk], w=[ofk])
                        ACT(junk3[:], of[:], AF.Square, r=[ofk], w=["junk3", svk + "s"], accum=sv[:, 5:6])
                        ACT(sv[:, 6:7], sv[:, 5:6], AF.Ln, r=[svk + "s"], w=[svk + "l"], scale=1.0 / 128.0, bias=1e-6)
                        ACT(sv[:, 7:8], sv[:, 6:7], AF.Exp, r=[svk + "l"], w=[svk + "e"], scale=-0.5)
                        ob, obk = ob_r.next()
                        TS("dve", ob[:], of[:], sv[:, 7:8], ALU.mult, r=[ofk, svk + "e"], w=[obk])
                        tb, tbk = sc_bank()
                        tbb = tb.bitcast(BF16)
                        PET(tbb[:, 0:128], ob[:], identB[:], r=[obk, "identB"], w=[tbk])
                        ot, otk = ot_r.next()
                        TS("dve", ot[:], tbb[:, 0:128], qkgs[:, 2:3], ALU.mult, r=[tbk, "qkgs"], w=[otk])
                        DMA(oT_s[h * 128:(h + 1) * 128, qb * 128:(qb + 1) * 128], ot[:], r=[otk], w=[f"oat_{h}_{qb}"], key=otk)
            do_precast(1000)
            S.barrier()
        if upto >= 4:
          with contextlib.ExitStack() as st:
            def T4(name, shape, dt):
                return st.enter_context(nc.sbuf_tensor(name, list(shape), dt))
            wo = T4("wo", [128, 16, D], BF16)
            oTr = Ring("oTb", [T4(f"oTb{i}", [128, 16, 128], BF16) for i in range(2)])
            x4r = Ring("x4", [T4(f"x4{i}", [128, D], F32) for i in range(2)])
            xmr = Ring("xm", [T4(f"xm{i}", [128, D], F32) for i in range(2)])
            xq = T4("xq", [128, D], F32)
            h2r = Ring("h2b", [T4(f"h2b{i}", [128, 16, 128], BF16) for i in range(2)])
            s4r = Ring("s4", [T4(f"s4{i}", [128, 4], F32) for i in range(4)])
            wov = woutb_s.rearrange("(kc p) n -> p kc n", p=128)
            for kc in range(16):
                DMA(wo[:, kc, :], wov[:, kc, :], r=[f"woutb_{kc}"], w=["wo%d" % (kc % 2)], key="wo%d" % (kc % 2))
            PS4 = Ring("pb", [None] * 8)
            oall = [f"orw_{c}_{g}" for c in range(MIXCH, NCH) for g in range(2)]
            for qb in range(MIXB, NBLK):
                tsl = slice(qb * 128, (qb + 1) * 128)
                oT, oTk = oTr.next()
                DMA(oT[:], oT_s.rearrange("(kc p) t -> p kc t", p=128)[:, :, tsl],
                    r=[f"oat_{h}_{qb}" for h in range(8)] + [f"orw_{c}_{g}" for c in (2 * qb, 2 * qb + 1) for g in range(2)],
                    w=[oTk], key=oTk)
                x4, x4k = x4r.next()
                DMA(x4[:], x_d[tsl, :], w=[x4k], key=x4k)
                xm, xmk = xmr.next()
                for dt in range(4):
                    _, pk = PS4.next()
                    pb = bank(int(pk[2:]))
                    dsl = slice(dt * 512, (dt + 1) * 512)
                    for kc in range(16):
                        PE(pb, oT[:, kc, :], wo[:, kc, dsl], start=(kc == 0), stop=(kc == 15), r=[oTk, "wo0", "wo1"], w=[pk])
                    TT("dve", xm[:, dsl], pb, gate_m[:, dsl], ALU.mult, r=[pk, "gate0"], w=[xmk + "a"])
                    TT("pool", xm[:, dsl], xm[:, dsl], x4[:, dsl], ALU.add, r=[xmk + "a", x4k], w=[xmk])
                DMA(xmid_s[tsl, :], xm[:], r=[xmk], w=[f"xmid_{qb}"], key=xmk)
                sv, svk = s4r.next()
                ACT(xq[:], xm[:], AF.Square, r=[xmk], w=["xq", svk], accum=sv[:, 0:1])
                ACT(sv[:, 1:2], sv[:, 0:1], AF.Ln, r=[svk], w=[svk + "b"], scale=1.0 / D, bias=1e-6)
                ACT(sv[:, 2:3], sv[:, 1:2], AF.Exp, r=[svk + "b"], w=[svk + "c"], scale=-0.5)
                TS("dve", xq[:], xm[:], sv[:, 2:3], ALU.mult, r=[xmk, svk + "c", "xq"], w=["xq"])
                h2, h2k = h2r.next()
                nb = 6 if qb * 128 < P0 else 4
                for kg in range(4):
                    _, pk = PS4.next()
                    pb = bank(int(pk[2:]))
                    for j in range(4):
                        kc = kg * 4 + j
                        PET(pb[:, j * 128:(j + 1) * 128], xq[:, kc * 128:(kc + 1) * 128], identF[:], r=["xq", "identF"], w=[pk])
                    for j in range(4):
                        kc = kg * 4 + j
                        TS("dve", h2[:, kc, :], pb[:, j * 128:(j + 1) * 128], nrm[:, kc, nb:nb + 1], ALU.mult,
                           nrm[:, kc, nb + 1:nb + 2], ALU.add, r=[pk, "nrm"], w=[h2k])
                DMA(h2T_s.rearrange("(kc p) t -> p kc t", p=128)[:, :, tsl], h2[:], r=[h2k], w=[f"h2T_{qb}"], key=h2k)
            S.barrier()

        if upto >= 5:
          with contextlib.ExitStack() as st:
            def T5(name, shape, dt):
                return st.enter_context(nc.sbuf_tensor(name, list(shape), dt))
            h2t = T5("h2t", [128, 16, FT], BF16)
            actT = T5("actT", [128, NFF, FT], BF16)
            wbr5 = Ring("wb5", [T5(f"wb5{i}", [128, 16, 128], BF16) for i in range(6)])
            usr = Ring("us", [T5(f"us{i}", [128, FT + 2], F32) for i in range(4)])
            tr5 = Ring("t5", [T5(f"t5{i}", [128, FT], F32) for i in range(4)])
            wdb = Ring("wdb", [T5(f"wdb{i}", [128, 1024], BF16) for i in range(6)])
            xm5 = Ring("xm5", [T5(f"xm5{i}", [128, 1024], F32) for i in range(2)])
            o5 = Ring("o5", [T5(f"o5{i}", [128, 512], F32) for i in range(3)])
            cw = T5("cw", [128, 88, 3], F32)
            cbv = T5("cbv", [128, 88], F32)
            car5 = T5("car5", [128, 88, 2], F32)
            DMA(cw[:], convw_d, w=["cw"], key="c0")
            DMA(cbv[:], convb_d, w=["cbv"], key="c1")
            MEMSET("pool", car5[:], 0.0, w=["car5"])
            PS5 = Ring("pb", [None] * 8)
            h2keys = [f"h2T_{qb}" for qb in range(MIXB, NBLK)]
            xmkeys = [f"xmid_{qb}" for qb in range(MIXB, NBLK)]
            subs = []
            o_ = 0
            while o_ < FT:
                subs.append((o_, min(128, FT - o_)))
                o_ += 128
            assert len(subs) * 2 <= 8
            for ft in range(NFT):
                t0 = FFN0 + ft * FT
                DMA(h2t[:], h2T_s.rearrange("(kc p) t -> p kc t", p=128)[:, :, t0:t0 + FT], r=h2keys, w=["h2t"], key="h2t")
                for f in range(NFF):
                    ys = {}
                    for which, col0, ci_ in (("g", f * 128, f), ("v", DFF + f * 128, NFF + f)):
                        wb, wbk = wbr5.next()
                        DMA(wb[:].rearrange("p k n -> p (k n)"), wupb_s[col0 // 128], r=[f"wupb_{col0 // 128}"], w=[wbk], key=wbk)
                        _, pk = PS5.next()
                        pb = bank(int(pk[2:]))
                        for kc in range(16):
                            PE(pb[:, 0:FT], wb[:, kc, :], h2t[:, kc, :], start=(kc == 0), stop=(kc == 15), r=[wbk, "h2t"], w=[pk])
                        us, usk = usr.next()
                        ACT(us[:, 2:FT + 2], pb[:, 0:FT], AF.Copy, r=[pk], w=[usk])
                        CP("pool", us[:, 0:2], car5[:, ci_, :], r=[f"car5_{ci_}", usk], w=[usk])
                        CP("pool", car5[:, ci_, :], us[:, FT:FT + 2], r=[usk], w=[f"car5_{ci_}"])
                        t, tk = tr5.next()
                        TS("dve", t[:], us[:, 2:FT + 2], cw[:, ci_, 2:3], ALU.mult, cbv[:, ci_:ci_ + 1], ALU.add, r=[usk, "cw", "cbv"], w=[tk])
                        STT(t[:], us[:, 1:FT + 1], cw[:, ci_, 1:2], t[:], ALU.mult, ALU.add, r=[usk, "cw", tk], w=[tk])
                        STT(t[:], us[:, 0:FT], cw[:, ci_, 0:1], t[:], ALU.mult, ALU.add, r=[usk, "cw", tk], w=[tk])
                        ys[which] = (t, tk)
                    (yg, ygk), (yv, yvk) = ys["g"], ys["v"]
                    ACT(yg[:], yg[:], AF.Silu, r=[ygk], w=[ygk])
                    TT("pool", actT[:, f, :], yg[:], yv[:], ALU.mult, r=[ygk, yvk], w=["actT"])
                for dh in range(2):
                    for f in range(NFF):
                        wb, wbk = wdb.next()
                        DMA(wb[:], wdnb_s[f * 128:(f + 1) * 128, dh * 1024:(dh + 1) * 1024], r=[f"wdnb_{f}"], w=[wbk], key=wbk)
                        for si, (so, sn) in enumerate(subs):
                            for dt in range(2):
                                bi_ = si * 2 + dt
                                PE(bank(bi_)[0:sn, :], actT[:, f, so:so + sn], wb[:, dt * 512:(dt + 1) * 512],
                                   start=(f == 0), stop=(f == NFF - 1), r=["actT", wbk], w=[f"pb{bi_}"])
                    for si, (so, sn) in enumerate(subs):
                        tk0 = t0 + so
                        xm, xmk = xm5.next()
                        DMA(xm[0:sn, :], xmid_s[tk0:tk0 + sn, dh * 1024:(dh + 1) * 1024], r=xmkeys, w=[xmk], key=xmk)
                        for dt in range(2):
                            bi_ = si * 2 + dt
                            dsl = slice(dh * 1024 + dt * 512, dh * 1024 + (dt + 1) * 512)
                            o, ok = o5.next()
                            TT("dve", o[0:sn, :], bank(bi_)[0:sn, :], gate_f[0:sn, dsl], ALU.mult, r=[f"pb{bi_}", "gate1"], w=[ok])
                            TT("pool", o[0:sn, :], o[0:sn, :], xm[0:sn, dt * 512:(dt + 1) * 512], ALU.add, r=[ok, xmk], w=[ok])
                            skip = max(0, P0 - tk0)
                            if skip < sn:
                                DMA(y_d[tk0 + skip - P0:tk0 + sn - P0, dsl], o[skip:sn, :], r=[ok], w=[f"y_{ft}_{si}_{dh}_{dt}"], key=ok)
            S.barrier()
        if upto <= 4:
            finish_partial()
            return nc
        S.emit()
    return nc


def _t5_bucket(n):
    n = np.maximum(n, 0)
    nf = np.maximum(n, 1).astype(np.float32)
    large = 16 + (np.log(nf / 16) / math.log(8) * 16).astype(np.int32)
    large = np.minimum(large, 31)
    return np.where(n < 16, n, large)


def host_maps(inp, SEQ):
    f32 = lambda a: np.ascontiguousarray(np.asarray(a, dtype=np.float32))
    x = f32(inp["x"])
    B = x.shape[0]
    P0 = SEQ // 2
    colk = lambda v: f32(np.asarray(v).reshape(-1, 128).T)
    rel_bias = f32(inp["rel_bias"])
    kk_, qq_ = np.meshgrid(np.arange(128), np.arange(128), indexing="ij")
    bd = rel_bias[_t5_bucket(qq_ - kk_)]
    bp = rel_bias[_t5_bucket(128 + qq_ - kk_)]
    maskd = np.where(qq_ >= kk_, 0.0, NEG).astype(np.float32)
    mS = np.tril(np.ones((64, 64), np.float32), -1)
    mI = np.tril(np.ones((64, 64), np.float32))
    rmasks = np.concatenate([-mS.T, mI.T, mS.T, mI.T, -mS], axis=1)
    blk = np.zeros((128, 128), np.float32)
    blk[:64, :64] = 1
    blk[64:, 64:] = 1
    common = {
        "w_ada": f32(inp["w_ada"][0]), "b_ada": f32(inp["b_ada"][0]).reshape(1, -1),
        "gcols": f32(np.stack([colk(inp["norm_mix_g"][0]), colk(inp["norm_ffn_g"][0])], -1)),
        "w_in": f32(inp["w_in"][0]),
        "w_l": f32(np.concatenate([inp["w1"][0], inp["a1"][0], inp["g1"][0]], 1)),
        "mu_wag": f32(np.stack([colk(inp["mu_wag"][0][j]) for j in range(3)], -1)),
        "mu_rkv": f32(np.stack([colk(inp["mu_rkv"][0][j]) for j in range(3)], -1)),
        "rwvec": f32(np.stack([colk(inp[k][0]) for k in ("w0", "a0", "k_k", "k_a", "r_k")], -1)),
        "ln64": f32(np.stack([np.asarray(inp["ln_x_g"][0]).reshape(16, 64).T, np.asarray(inp["ln_x_b"][0]).reshape(16, 64).T], -1)),
        "wa2": f32(np.concatenate([inp["w2"][0], inp["a2"][0]], 0)),
        "g2": f32(inp["g2"][0]),
        "qkg": f32(np.stack([np.tile(inp["q_norm_g"][0], 2), np.tile(inp["k_norm_g"][0], 2), inp["attn_subln_g"][0]], -1)),
        "lamrow": f32(np.concatenate([inp["lambda_q1"][0], inp["lambda_k1"][0], inp["lambda_q2"][0], inp["lambda_k2"][0]])).reshape(1, 256),
        "biasd": f32(np.transpose(bd, (0, 2, 1))), "biasp": f32(np.transpose(bp, (0, 2, 1))),
        "farc": f32(np.tile(rel_bias[31][None, :], (128, 1))),
        "maskd": maskd,
        "w_out": f32(inp["w_out"][0]), "w_up": f32(inp["w_up"][0]), "w_down": f32(inp["w_down"][0]),
        "convw": f32(np.transpose(np.asarray(inp["conv_w"][0]).reshape(3, 88, 128), (2, 1, 0))),
        "convb": colk(inp["conv_b"][0]),
        "identf": np.eye(128, dtype=np.float32), "identb": np.eye(128).astype(ml_dtypes.bfloat16),
        "blkones": blk, "rmasks": f32(rmasks),
    }
    maps = []
    for b in range(B):
        for half in range(2):
            m = dict(common)
            if half == 0:
                m["x"] = np.concatenate([np.zeros((P0, D), np.float32), x[b, :P0]], 0)
            else:
                m["x"] = x[b]
            m["c_col"] = f32(np.asarray(inp["c"][b]).reshape(128, 16))
            fl = np.zeros((128, 2), np.float32)
            fl[:, 0] = float(half)
            fl[:, 1] = 0.0 if half else NEG
            m["flagv"] = fl
            maps.append(m)
    return maps


_NC_CACHE = {}


def kernel(**inputs):
    x = np.asarray(inputs["x"])
    B, SEQ = x.shape[0], x.shape[1]
    if SEQ not in _NC_CACHE:
        _NC_CACHE[SEQ] = build_program(SEQ)
    nc = _NC_CACHE[SEQ]
    maps = host_maps(inputs, SEQ)
    res = run_bass_kernel_spmd(nc, maps, core_ids=list(range(len(maps))))
    P0 = SEQ // 2
    out = np.zeros((B, SEQ, D), np.float32)
    for b in range(B):
        for half in range(2):
            out[b, half * P0:(half + 1) * P0] = res.results[b * 2 + half]["y"]
    return out
```
